# Optimizing a Trainium2 kernel written in Bass

```python
import math
import jax, jax.numpy as jnp
from jax import lax
import numpy as np


D_MODEL = 4096
BATCH = 4
SEQ = 2048
DEPTH = 1

CHUNK = 64
Q_BLOCK = 2 * CHUNK
MIX_WIDTH = D_MODEL
RWKV_WIDTH = MIX_WIDTH // 2
RWKV_HEAD_DIM = 64
RWKV_HEADS = RWKV_WIDTH // RWKV_HEAD_DIM
DECAY_LORA = max(32, int(round(1.8 * math.sqrt(RWKV_WIDTH) / 32)) * 32)
AAA_LORA = DECAY_LORA
GATE_LORA = max(32, int(round(0.6 * RWKV_WIDTH ** 0.8 / 32)) * 32)
RWKV_PROJ = 3 * RWKV_WIDTH + DECAY_LORA + AAA_LORA + GATE_LORA
FOX_WIDTH = MIX_WIDTH - RWKV_WIDTH
FOX_HEAD_DIM = 128
FOX_HEADS = FOX_WIDTH // FOX_HEAD_DIM
FOX_PROJ = 3 * FOX_WIDTH + FOX_HEADS
IN_PROJ = RWKV_PROJ + FOX_PROJ
D_FF = ((8 * D_MODEL // 3 + 255) // 256) * 256
N_MOD = 9
NORM_EPS = 1e-6
LN_X_EPS = 64e-5

kernel_name = "hybrid_rwkv7_fox_macaron_adaln"


def rmsnorm(x, g, eps=NORM_EPS):
    xf = x.astype(jnp.float32)
    y = xf * lax.rsqrt(jnp.mean(xf * xf, axis=-1, keepdims=True) + eps)
    return (y * g.astype(jnp.float32)).astype(x.dtype)


def swiglu(h, w_gate, w_up, w_down):
    return (jax.nn.silu(h @ w_gate) * (h @ w_up)) @ w_down


def token_shift(p):
    return jnp.pad(p, ((0, 0), (1, 0), (0, 0)))[:, :-1]


def wkv7_scan(r, w, k, v, a, b):
    bsz, _, n_heads, n_dim = r.shape

    def step(state, inp):
        r_t, w_t, k_t, v_t, a_t, b_t = inp
        sa = jnp.einsum('bhvk,bhk->bhv', state, a_t)
        state = (state * w_t[:, :, None, :] + sa[..., None] * b_t[:, :, None, :]
                 + v_t[..., None] * k_t[:, :, None, :])
        return state, jnp.einsum('bhvk,bhk->bhv', state, r_t)

    xs = tuple(jnp.swapaxes(t, 0, 1) for t in (r, w, k, v, a, b))
    s0 = jnp.zeros((bsz, n_heads, n_dim, n_dim), jnp.float32)
    _, y = lax.scan(step, s0, xs)
    return jnp.swapaxes(y, 0, 1)


def rwkv7_group(p, mu, w0, w2, a0, a2, g2, k_k, k_a, r_k, ln_w, ln_b):
    bsz, seq, _ = p.shape
    H, N = RWKV_HEADS, RWKV_HEAD_DIM
    f32 = jnp.float32
    p = p + (token_shift(p) - p) * mu
    c1 = RWKV_WIDTH
    c2, c3 = 2 * c1, 3 * c1
    c4 = c3 + DECAY_LORA
    c5 = c4 + AAA_LORA
    r, k, v = p[..., :c1], p[..., c1:c2], p[..., c2:c3]
    wd, ad, gd = p[..., c3:c4], p[..., c4:c5], p[..., c5:]
    w = -jax.nn.softplus(-(w0 + jnp.tanh(wd) @ w2).astype(f32)) - 0.5
    decay = jnp.exp(-jnp.exp(w))
    a = jax.nn.sigmoid(a0 + ad @ a2)
    g = jax.nn.sigmoid(gd) @ g2
    kk = (k * k_k).astype(f32).reshape(bsz, seq, H, N)
    kk = kk / jnp.maximum(jnp.linalg.norm(kk, axis=-1, keepdims=True), 1e-12)
    k = k * (1 + (a - 1) * k_a)

    def heads(t):
        return t.astype(f32).reshape(bsz, seq, H, N)

    rh, kh, vh, ah = heads(r), heads(k), heads(v), heads(a)
    y = wkv7_scan(rh, heads(decay), kh, vh, -kk, kk * ah)
    mean = jnp.mean(y, axis=-1, keepdims=True)
    var = jnp.mean(jnp.square(y - mean), axis=-1, keepdims=True)
    y = ((y - mean) * lax.rsqrt(var + LN_X_EPS) * ln_w.astype(f32).reshape(H, N)
         + ln_b.astype(f32).reshape(H, N))
    y = y + jnp.sum(rh * kh * r_k.astype(f32), axis=-1, keepdims=True) * vh
    return y.reshape(bsz, seq, RWKV_WIDTH).astype(p.dtype) * g


def fox_group(p, b_f, out_norm):
    bsz, seq, _ = p.shape
    H, dh = FOX_HEADS, FOX_HEAD_DIM
    f32 = jnp.float32
    q = p[..., :FOX_WIDTH].reshape(bsz, seq, H, dh)
    k = p[..., FOX_WIDTH:2 * FOX_WIDTH].reshape(bsz, seq, H, dh)
    v = p[..., 2 * FOX_WIDTH:3 * FOX_WIDTH].reshape(bsz, seq, H, dh)
    logf = jax.nn.log_sigmoid((p[..., 3 * FOX_WIDTH:] + b_f).astype(f32))
    cum = jnp.swapaxes(jnp.cumsum(logf, axis=1), 1, 2)
    scale = FOX_HEAD_DIM ** -0.5
    outs = []
    for i in range(seq // Q_BLOCK):
        q0, q1 = i * Q_BLOCK, (i + 1) * Q_BLOCK
        logits = jnp.einsum('bqhd,bkhd->bhqk', q[:, q0:q1], k[:, :q1]).astype(f32) * scale
        logits = logits + cum[:, :, q0:q1, None] - cum[:, :, None, :q1]
        qpos = jnp.arange(q0, q1)
        kpos = jnp.arange(q1)
        logits = jnp.where(kpos[None, :] <= qpos[:, None], logits, -jnp.inf)
        probs = jax.nn.softmax(logits, axis=-1).astype(v.dtype)
        outs.append(jnp.einsum('bhqk,bkhd->bqhd', probs, v[:, :q1]))
    o = jnp.concatenate(outs, axis=1)
    o = rmsnorm(o, out_norm.reshape(H, dh))
    return o.reshape(bsz, seq, FOX_WIDTH)


def hybrid_mixer(h, w_in, rwkv_mu, rwkv_w0, rwkv_w2, rwkv_a0, rwkv_a2, rwkv_g2,
                 rwkv_k_k, rwkv_k_a, rwkv_r_k, rwkv_ln_w, rwkv_ln_b,
                 fox_b_f, fox_out_norm, w_out):
    p = h @ w_in
    y_r = rwkv7_group(p[..., :RWKV_PROJ], rwkv_mu, rwkv_w0, rwkv_w2, rwkv_a0, rwkv_a2,
                      rwkv_g2, rwkv_k_k, rwkv_k_a, rwkv_r_k, rwkv_ln_w, rwkv_ln_b)
    y_f = fox_group(p[..., RWKV_PROJ:], fox_b_f, fox_out_norm)
    return jnp.concatenate([y_r, y_f], axis=-1) @ w_out


def setup_inputs(seed: int = 0) -> dict:
    key = jax.random.key(seed)
    ks = jax.random.split(key, 32)
    f32 = jnp.float32
    L, D = DEPTH, D_MODEL

    def nrm(k, shape, scale):
        return jax.random.normal(k, shape, f32) * scale

    def gain(k, shape):
        return 1.0 + nrm(k, shape, 0.05)

    return {
        "x": nrm(ks[0], (BATCH, SEQ, D), 1.0),
        "c": nrm(ks[1], (BATCH, D), 1.0),
        "w_mod": nrm(ks[2], (L, D, N_MOD * D), D ** -0.5),
        "b_mod": nrm(ks[3], (L, N_MOD * D), 0.02),
        "norm_ffn1": gain(ks[4], (L, D)),
        "ffn1_gate": nrm(ks[5], (L, D, D_FF), D ** -0.5),
        "ffn1_up": nrm(ks[6], (L, D, D_FF), D ** -0.5),
        "ffn1_down": nrm(ks[7], (L, D_FF, D), D_FF ** -0.5),
        "norm_mix": gain(ks[8], (L, D)),
        "w_in": nrm(ks[9], (L, D, IN_PROJ), D ** -0.5),
        "rwkv_mu": jax.random.uniform(ks[10], (L, RWKV_PROJ), f32),
        "rwkv_w0": jax.random.uniform(ks[11], (L, RWKV_WIDTH), f32, -6.0, -1.0),
        "rwkv_w2": nrm(ks[12], (L, DECAY_LORA, RWKV_WIDTH), 0.1 * DECAY_LORA ** -0.5),
        "rwkv_a0": nrm(ks[13], (L, RWKV_WIDTH), 0.1),
        "rwkv_a2": nrm(ks[14], (L, AAA_LORA, RWKV_WIDTH), 0.5 * AAA_LORA ** -0.5),
        "rwkv_g2": nrm(ks[15], (L, GATE_LORA, RWKV_WIDTH), GATE_LORA ** -0.5),
        "rwkv_k_k": 0.85 + nrm(ks[16], (L, RWKV_WIDTH), 0.05),
        "rwkv_k_a": 1.0 + nrm(ks[17], (L, RWKV_WIDTH), 0.05),
        "rwkv_r_k": nrm(ks[18], (L, RWKV_HEADS, RWKV_HEAD_DIM), 0.1),
        "rwkv_ln_w": gain(ks[19], (L, RWKV_WIDTH)),
        "rwkv_ln_b": nrm(ks[20], (L, RWKV_WIDTH), 0.02),
        "fox_b_f": jax.random.uniform(ks[21], (L, FOX_HEADS), f32, 1.0, 4.0),
        "fox_out_norm": gain(ks[22], (L, FOX_WIDTH)),
        "w_out": nrm(ks[23], (L, MIX_WIDTH, D), MIX_WIDTH ** -0.5),
        "norm_ffn2": gain(ks[24], (L, D)),
        "ffn2_gate": nrm(ks[25], (L, D, D_FF), D ** -0.5),
        "ffn2_up": nrm(ks[26], (L, D, D_FF), D ** -0.5),
        "ffn2_down": nrm(ks[27], (L, D_FF, D), D_FF ** -0.5),
        "norm_final": gain(ks[28], (D,)),
    }


def reference(x, c, w_mod, b_mod, norm_ffn1, ffn1_gate, ffn1_up, ffn1_down,
              norm_mix, w_in, rwkv_mu, rwkv_w0, rwkv_w2, rwkv_a0, rwkv_a2, rwkv_g2,
              rwkv_k_k, rwkv_k_a, rwkv_r_k, rwkv_ln_w, rwkv_ln_b, fox_b_f,
              fox_out_norm, w_out, norm_ffn2, ffn2_gate, ffn2_up, ffn2_down,
              norm_final):
    for l in range(DEPTH):
        mod = (jax.nn.silu(c) @ w_mod[l] + b_mod[l])[:, None, :]
        sh1, sc1, gt1, sh2, sc2, gt2, sh3, sc3, gt3 = jnp.split(mod, N_MOD, axis=-1)
        h = rmsnorm(x, norm_ffn1[l]) * (1 + sc1) + sh1
        x = x + 0.5 * gt1 * swiglu(h, ffn1_gate[l], ffn1_up[l], ffn1_down[l])
        h = rmsnorm(x, norm_mix[l]) * (1 + sc2) + sh2
        x = x + gt2 * hybrid_mixer(h, w_in[l], rwkv_mu[l], rwkv_w0[l], rwkv_w2[l],
                                   rwkv_a0[l], rwkv_a2[l], rwkv_g2[l], rwkv_k_k[l],
                                   rwkv_k_a[l], rwkv_r_k[l], rwkv_ln_w[l], rwkv_ln_b[l],
                                   fox_b_f[l], fox_out_norm[l], w_out[l])
        h = rmsnorm(x, norm_ffn2[l]) * (1 + sc3) + sh3
        x = x + 0.5 * gt3 * swiglu(h, ffn2_gate[l], ffn2_up[l], ffn2_down[l])
    return rmsnorm(x, norm_final)
```

```python
import numpy as np
from contextlib import ExitStack
import concourse.bass as bass
import concourse.mybir as mybir
from concourse.bass_utils import run_bass_kernel_spmd

F32 = mybir.dt.float32
BF16 = mybir.dt.bfloat16
AF = mybir.ActivationFunctionType
ALU = mybir.AluOpType
AX = mybir.AxisListType

ENGS = ("tensor", "vector", "scalar", "gpsimd", "sync")
D = 4096
NFT = 32
TT = 512
NHP = 16
NFH = 16
LORA = 96
NTA = 88


class Prog:
    def __init__(self, nc):
        self.nc = nc
        self.ops = []
        self.cur_level = 0
        self.max_level = 99

    def op(self, eng, fn, reads=(), writes=(), dma=False, semkey=None):
        if self.cur_level > self.max_level:
            return
        self.ops.append(dict(eng=eng, fn=fn, reads=tuple(reads), writes=tuple(writes),
                             dma=dma, semkey=semkey, bar=False))

    def dma(self, eng, out, in_, reads=(), writes=(), semkey="dma"):
        self.op(eng, lambda e, o=out, i=in_: e.dma_start(out=o, in_=i),
                reads=reads, writes=writes, dma=True, semkey=semkey)

    def barrier(self):
        self.ops.append(dict(bar=True))

    def emit(self, final_keys=()):
        nc = self.nc
        ops = [o for o in self.ops]
        n = len(ops)
        last_w = {}
        readers = {}
        deps = [None] * n
        last_eng = {}
        last_dma = {}
        pend_bar = {}
        for i, o in enumerate(ops):
            if o["bar"]:
                s = set(last_eng.values()) | set(last_dma.values())
                for e in ENGS:
                    pend_bar[e] = set(s) | pend_bar.get(e, set())
                continue
            d = set()
            for k in o["reads"]:
                if k in last_w:
                    d.add(last_w[k])
            for k in o["writes"]:
                if k in last_w:
                    d.add(last_w[k])
                d.update(readers.get(k, ()))
            for k in o["reads"]:
                readers.setdefault(k, []).append(i)
            for k in o["writes"]:
                last_w[k] = i
                readers[k] = []
            if o["eng"] in pend_bar:
                d |= pend_bar.pop(o["eng"])
            d.discard(i)
            deps[i] = d
            if o["dma"]:
                last_dma[o["semkey"]] = i
            else:
                last_eng[o["eng"]] = i
        final_deps = set()
        for k in final_keys:
            if k in last_w:
                final_deps.add(last_w[k])
        needed = set()
        for i, o in enumerate(ops):
            if o["bar"]:
                continue
            best = {}
            for p in deps[i]:
                po = ops[p]
                if po["dma"]:
                    key = ("dma", po["semkey"])
                else:
                    if po["eng"] == o["eng"] and o["eng"] in ("tensor", "sync") and not o["dma"]:
                        continue
                    key = ("eng", po["eng"])
                if key not in best or p > best[key]:
                    best[key] = p
            deps[i] = set(best.values())
            needed |= deps[i]
        needed |= final_deps
        needed |= {i for i, o in enumerate(ops) if not o["bar"] and o["dma"]}
        sems = {}
        ctxs = []
        keys = [("eng", e) for e in ENGS]
        for o in ops:
            if not o["bar"] and o["dma"] and ("dma", o["semkey"]) not in keys:
                keys.append(("dma", o["semkey"]))
        for k in keys:
            cm = nc.semaphore("s_" + "_".join(str(x) for x in k))
            sems[k] = cm.__enter__()
            ctxs.append(cm)
        cnt = {k: 0 for k in sems}
        opsem = [None] * n
        for i, o in enumerate(ops):
            if i in needed:
                sk = ("dma", o["semkey"]) if o["dma"] else ("eng", o["eng"])
                cnt[sk] += 16 if o["dma"] else 1
                opsem[i] = (sk, cnt[sk])
        for i, o in enumerate(ops):
            if opsem[i] is not None and o["dma"] and str(o["semkey"]).startswith("const"):
                opsem[i] = (opsem[i][0], cnt[opsem[i][0]])
        self.nsem = len(sems)
        self.maxcnt = max(cnt.values())
        self.stats = {e: sum(1 for o in ops if not o["bar"] and o["eng"] == e) for e in ENGS}
        per_eng = {e: [] for e in ENGS}
        for i, o in enumerate(ops):
            if not o["bar"]:
                per_eng[o["eng"]].append(i)
        waited = {e: {} for e in ENGS}

        def emit_engine(ename, eng):
            for i in per_eng[ename]:
                o = ops[i]
                need = {}
                for p in deps[i]:
                    sk, v = opsem[p]
                    if v > need.get(sk, 0):
                        need[sk] = v
                for sk, v in need.items():
                    if waited[ename].get(sk, 0) >= v:
                        continue
                    eng.wait_ge(sems[sk], v)
                    waited[ename][sk] = v
                ins = o["fn"](eng)
                if opsem[i] is not None:
                    ins.then_inc(sems[opsem[i][0]], 16 if o["dma"] else 1)
            if ename == "sync":
                need = {}
                for p in final_deps:
                    sk, v = opsem[p]
                    if v > need.get(sk, 0):
                        need[sk] = v
                for sk, v in need.items():
                    eng.wait_ge(sems[sk], v)

        with nc.Block() as block:
            @block.tensor
            def _(e):
                emit_engine("tensor", e)

            @block.vector
            def _(e):
                emit_engine("vector", e)

            @block.scalar
            def _(e):
                emit_engine("scalar", e)

            @block.gpsimd
            def _(e):
                emit_engine("gpsimd", e)

            @block.sync
            def _(e):
                emit_engine("sync", e)
        for cm in reversed(ctxs):
            cm.__exit__(None, None, None)


SP_C = 0
SP_BMOD = 32
SP_GAIN = 320
SP_MU = 448
SP_W0 = 500
SP_A0 = 516
SP_KK = 532
SP_KA = 548
SP_LNW = 564
SP_LNB = 580
SP_RK = 596
SP_FON = 612
SP_BF = 628
NSP = 640


def build_nc(S, DFF, stop_after=None):
    NTG = S // TT
    NCH = S // 128
    NMF = DFF // 128
    NGU = DFF // 256
    assert S % TT == 0 and DFF % 256 == 0
    nc = bass.Bass("TRN2", target_bir_lowering=False)

    def din(name, shape, dt=F32):
        return nc.dram_tensor(name, list(shape), dt, kind="ExternalInput").ap()

    def dscr(name, shape, dt=F32):
        return nc.dram_tensor(name, list(shape), dt, kind="Internal").ap()

    x_in = din("x", [S, D])
    sp_in = din("sp", [128, NSP])
    wmod = din("wmod", [72, 128, 32, 512])
    wgu = [din(f"wgu{i}", [NGU, 128, 32, 512]) for i in (1, 2)]
    wdn = [din(f"wdn{i}", [8, 128, NMF, 512]) for i in (1, 2)]
    wina = din("wina", [NTA // 4, 128, 32, 512])
    winb = din("winb", [4, 128, 32, 512])
    wo_in = din("wo", [8, 128, 32, 512])
    w2_in = din("w2", [LORA, 2048])
    a2_in = din("a2", [LORA, 2048])
    g2_in = din("g2", [2, 128, 2048])
    out_d = nc.dram_tensor("out", [S, D], F32, kind="ExternalOutput").ap()

    xT = dscr("xT", [NFT, 128, S])
    pT = dscr("pT", [NTA, 128, S])
    fvS = dscr("fvS", [NCH, 128, 2048], BF16)
    yT = (nc.dram_tensor("yT", [NFT, 128, S], BF16, kind="ExternalOutput").ap() if DEBUG_OUT else dscr("yT", [NFT, 128, S], BF16))

    es = ExitStack()
    ARW = 53200
    arena = es.enter_context(nc.sbuf_tensor("arena", [128, ARW], F32))
    banks = [es.enter_context(nc.psum_tensor(f"bk{i}", [128, 512], F32)) for i in range(8)]
    P = Prog(nc)
    _off = [0]

    def alloc(n):
        o = _off[0]
        _off[0] += n
        assert _off[0] <= ARW, _off[0]
        return o

    def V(off, n, p0=0, p1=128):
        return arena[p0:p1, off:off + n]

    def VB(off, nb, p0=0, p1=128):
        return arena[p0:p1, off:off + nb // 2].bitcast(BF16)

    o_ident = alloc(128); ident = V(o_ident, 128)
    o_ones = alloc(128); ones = V(o_ones, 128)
    o_blk = alloc(128); blk = V(o_blk, 128)
    o_mcat = alloc(512); mcat = V(o_mcat, 512)
    o_msT = alloc(128); msT = V(o_msT, 128)
    o_sp = alloc(NSP); sp = V(o_sp, NSP)
    o_mod = alloc(288); modT = V(o_mod, 288)
    o_A = alloc(96); Acoef = V(o_A, 96)
    o_hg = alloc(96); hgate = V(o_hg, 96)
    o_onesb = alloc(64); onesb = VB(o_onesb, 128)
    o_mib = alloc(64); mib = VB(o_mib, 128)
    o_sel = alloc(16 * 64); selb = VB(o_sel, 16 * 128)
    o_idb = alloc(64); identb = VB(o_idb, 128)
    o_nmb = alloc(64); negmb = VB(o_nmb, 128)
    o_HT = alloc(8192)
    o_ACT = alloc(22016)
    o_WS = alloc(12288)
    o_XS = alloc(2048)
    o_XO = alloc(2048)
    o_MISC = alloc(2560)
    HT = VB(o_HT, 16384)
    ACTT = VB(o_ACT, 44032)
    WSs = [VB(o_WS + 4096 * s, 8192) for s in range(3)]
    XS = [V(o_XS + 512 * i, 512) for i in range(4)]
    XO = [V(o_XO + 512 * i, 512) for i in range(4)]
    RSTD = V(o_MISC, 512)
    TMPA = [V(o_MISC + 512 + 512 * i, 512) for i in range(2)]
    SQ = [V(o_MISC + 1536 + 512 * i, 512) for i in range(2)]

    cnts = dict(ws=0, xs=0, xo=0, sq=0, tmpa=0, grp=0, alt=0)

    def alt_eng():
        cnts["alt"] += 1
        return "vector" if cnts["alt"] % 2 else "scalar"

    def copy_op(eng, out, in_):
        if eng == "scalar":
            return lambda e: e.copy(out=out, in_=in_)
        return lambda e: e.tensor_copy(out=out, in_=in_)

    P.dma("sync", sp, sp_in, writes=["sp"], semkey="const0")
    P.op("gpsimd", lambda e: e.memset(ident, 0.0), writes=["ident"])
    P.op("gpsimd", lambda e: e.affine_select(out=ident, in_=ident, pattern=[[-1, 128]], compare_op=ALU.not_equal,
                                             fill=1.0, base=0, channel_multiplier=1), reads=["ident"], writes=["ident"])
    P.op("gpsimd", lambda e: e.memset(ones, 1.0), writes=["ones"])
    P.op("gpsimd", lambda e: e.memset(onesb, 1.0), writes=["onesb"])
    P.op("gpsimd", lambda e: e.memset(blk, 0.0), writes=["blk"])
    P.op("gpsimd", lambda e: e.memset(blk[0:64, 0:64], 1.0), reads=["blk"], writes=["blk"])
    P.op("gpsimd", lambda e: e.memset(blk[64:128, 64:128], 1.0), reads=["blk"], writes=["blk"])
    P.op("gpsimd", lambda e: e.memset(mcat, 1.0), writes=["mcat"])
    for q in range(4):
        cmp_ = ALU.is_gt if q % 2 == 0 else ALU.is_ge
        P.op("gpsimd", lambda e, q=q, cmp_=cmp_: e.affine_select(
            out=mcat[:, q * 128:(q + 1) * 128], in_=mcat[:, q * 128:(q + 1) * 128], pattern=[[1, 128]],
            compare_op=cmp_, fill=0.0, base=0, channel_multiplier=-1), reads=["mcat"], writes=["mcat"])
    P.op("gpsimd", lambda e: e.memset(msT, 1.0), writes=["msT"])
    P.op("gpsimd", lambda e: e.affine_select(out=msT, in_=msT, pattern=[[-1, 128]], compare_op=ALU.is_gt,
                                             fill=0.0, base=0, channel_multiplier=1), reads=["msT"], writes=["msT"])
    P.op("gpsimd", lambda e: e.tensor_copy(out=mib, in_=mcat[:, 128:256]), reads=["mcat"], writes=["mib"])
    P.op("gpsimd", lambda e: e.tensor_copy(out=identb, in_=ident), reads=["ident"], writes=["identb"])
    P.op("gpsimd", lambda e: e.tensor_scalar(out=negmb, in0=mcat[:, 128:256], scalar1=30000.0, scalar2=-30000.0, op0=ALU.mult, op1=ALU.add),
         reads=["mcat"], writes=["negmb"])
    for q in range(4):
        P.op("gpsimd", lambda e, q=q: e.tensor_tensor(out=mcat[:, q * 128:(q + 1) * 128], in0=mcat[:, q * 128:(q + 1) * 128], in1=blk, op=ALU.mult),
             reads=["mcat", "blk", "mib", "negmb"], writes=["mcat"])
    P.op("gpsimd", lambda e: e.tensor_tensor(out=msT, in0=msT, in1=blk, op=ALU.mult), reads=["msT", "blk"], writes=["msT"])
    P.op("gpsimd", lambda e: e.memset(selb, 0.0), writes=["selb"])
    inv_s = float(128.0 ** 0.5)
    P.op("vector", lambda e: e.tensor_scalar_mul(out=selb[0:16, :].rearrange("p (h k) -> p h k", k=128),
                                                 in0=ident[0:16, 0:16].unsqueeze(2).to_broadcast([16, 16, 128]), scalar1=inv_s),
         reads=["selb", "ident"], writes=["selb"])

    def linear(Wt, G, KT, ntiles, form, act, act_key, evac, nsb=4, ktu=16, ncols=TT):
        units = []
        for g in range(G):
            for k0 in range(0, KT, ktu):
                units.append((g, k0, min(KT, k0 + ktu)))

        def issue(u):
            g, k0, k1 = units[u]
            s = cnts["ws"] % 3
            cnts["ws"] += 1
            dst = WSs[s][:, 0:(k1 - k0) * 512].rearrange("p (k c) -> p k c", c=512)
            P.dma("gpsimd", dst, Wt[g, :, k0:k1, :], writes=[f"ws{s}"], semkey=f"ws{s}")
            return s
        slots = {}
        for u in range(min(2, len(units))):
            slots[u] = issue(u)
        for u, (g, k0, k1) in enumerate(units):
            if u + 2 < len(units):
                slots[u + 2] = issue(u + 2)
            s = slots.pop(u)
            if k0 == 0:
                bset = cnts["grp"] % 2
                cnts["grp"] += 1
            bk = [4 * bset + j for j in range(4)]
            for kt in range(k0, k1):
                kl = kt - k0
                first, last = (kt == 0), (kt == KT - 1)
                if form == "a":
                    for j in range(ntiles(g)):
                        P.op("tensor", lambda e, b=bk[j], s=s, kl=kl, j=j, kt=kt, first=first, last=last:
                             e.matmul(banks[b][:, 0:ncols], lhsT=WSs[s][:, kl * 512 + j * 128: kl * 512 + (j + 1) * 128],
                                      rhs=act(kt), start=first, stop=last),
                             reads=[f"ws{s}", act_key(kt)], writes=[f"bk{bk[j]}"])
                else:
                    for sb in range(nsb):
                        P.op("tensor", lambda e, b=bk[sb], s=s, kl=kl, sb=sb, kt=kt, first=first, last=last:
                             e.matmul(banks[b][:, 0:512], lhsT=act(kt, sb), rhs=WSs[s][:, kl * 512:(kl + 1) * 512],
                                      start=first, stop=last),
                             reads=[f"ws{s}", act_key(kt)], writes=[f"bk{bk[sb]}"])
            if k1 == KT:
                evac(g, bk)

    o_scf = o_ACT
    scf = V(o_scf, 32)
    screp = VB(o_scf + 64, 4096)
    P.op("scalar", lambda e: e.activation(out=scf, in_=sp[:, SP_C:SP_C + 32], func=AF.Silu), reads=["sp"], writes=["scf"])
    for kt in range(32):
        P.op("vector", lambda e, kt=kt: e.tensor_copy(out=screp[:, kt * 128:(kt + 1) * 128],
                                                       in_=scf[:, kt:kt + 1].to_broadcast([128, 128])),
             reads=["scf"], writes=["screp"])

    def mod_evac(g, bk):
        t = TMPA[cnts["tmpa"] % 2]
        tk = f"tmpa{cnts['tmpa'] % 2}"
        cnts["tmpa"] += 1
        b = bk[0]
        P.op("vector", lambda e: e.tensor_tensor(out=t.rearrange("p (a b) -> p a b", b=128),
                                                 in0=banks[b][:, :].rearrange("p (a b) -> p a b", b=128),
                                                 in1=ident.unsqueeze(1).to_broadcast([128, 4, 128]), op=ALU.mult),
             reads=[f"bk{b}", "ident"], writes=[tk])
        P.op("vector", lambda e: e.tensor_reduce(out=modT[:, 4 * g:4 * g + 4], in_=t.rearrange("p (a b) -> p a b", b=128),
                                                 axis=AX.X, op=ALU.add), reads=[tk], writes=["modT"])
    linear(wmod, 72, 32, None, "b", lambda kt, sb: screp[:, kt * 128:(kt + 1) * 128], lambda kt: "screp", mod_evac, nsb=1)
    P.op("vector", lambda e: e.tensor_add(out=modT, in0=modT, in1=sp[:, SP_BMOD:SP_BMOD + 288]), reads=["modT", "sp"], writes=["modT"])
    for n_ in range(3):
        gcol = SP_GAIN + 32 * n_
        P.op("vector", lambda e, n_=n_, gcol=gcol: e.scalar_tensor_tensor(
            out=Acoef[:, 32 * n_:32 * n_ + 32], in0=modT[:, 96 * n_ + 32:96 * n_ + 64], scalar=1.0,
            in1=sp[:, gcol:gcol + 32], op0=ALU.add, op1=ALU.mult), reads=["modT", "sp"], writes=["Acoef"])
        P.op("vector", lambda e, n_=n_: e.tensor_scalar_mul(out=hgate[:, 32 * n_:32 * n_ + 32],
                                                           in0=modT[:, 96 * n_ + 64:96 * n_ + 96],
                                                           scalar1=(1.0 if n_ == 1 else 0.5)),
             reads=["modT"], writes=["hgate"])
    P.barrier()

    XIN = [V(o_ACT + 4096 * s_, 4096) for s_ in range(4)]
    for tg in range(NTG):
        for s_ in range(4):
            r0 = tg * TT + s_ * 128
            P.dma("sync", XIN[s_], x_in[r0:r0 + 128, :], writes=[f"xin{s_}"], semkey=f"xin{s_}")
        for ft in range(NFT):
            b = ft % 8
            for s_ in range(4):
                P.op("tensor", lambda e, b=b, s_=s_, ft=ft: e.transpose(banks[b][:, s_ * 128:(s_ + 1) * 128],
                                                                      XIN[s_][:, ft * 128:(ft + 1) * 128], ident),
                     reads=[f"xin{s_}", "ident"], writes=[f"bk{b}"])
            i = cnts["xo"] % 4
            cnts["xo"] += 1
            P.op(alt_eng(), copy_op("vector" if cnts["alt"] % 2 else "scalar", XO[i], banks[b][:, :]),
                 reads=[f"bk{b}"], writes=[f"xo{i}"])
            P.dma("sync", xT[ft, :, tg * TT:(tg + 1) * TT], XO[i], reads=[f"xo{i}"], writes=[f"xT{ft}_{tg}"], semkey=f"xo{i}")
    P.barrier()

    def load_x(ft, tg):
        i = cnts["xs"] % 4
        cnts["xs"] += 1
        P.dma("sync", XS[i], xT[ft, :, tg * TT:(tg + 1) * TT], reads=[f"xT{ft}_{tg}"], writes=[f"xs{i}"], semkey=f"xs{i}")
        return i

    def sumsq_rstd(tg, bank, src=None):
        for ft in range(NFT):
            i = load_x(ft, tg)
            q = cnts["sq"] % 2
            cnts["sq"] += 1
            P.op("scalar", lambda e, i=i, q=q: e.activation(out=SQ[q], in_=XS[i], func=AF.Square),
                 reads=[f"xs{i}"], writes=[f"sq{q}"])
            P.op("tensor", lambda e, q=q, ft=ft: e.matmul(banks[bank][:, :], lhsT=ones, rhs=SQ[q], start=(ft == 0), stop=(ft == NFT - 1)),
                 reads=[f"sq{q}", "ones"], writes=[f"bk{bank}"])
        P.op("scalar", lambda e: e.activation(out=RSTD, in_=banks[bank][:, :], func=AF.Sqrt, bias=1e-6, scale=1.0 / D),
             reads=[f"bk{bank}"], writes=["rstd"])
        P.op("vector", lambda e: e.reciprocal(out=RSTD, in_=RSTD), reads=["rstd"], writes=["rstd"])

    def norm_to_HT(tg, n_):
        sumsq_rstd(tg, 0)
        for ft in range(NFT):
            i = load_x(ft, tg)
            t = cnts["tmpa"] % 2
            cnts["tmpa"] += 1
            P.op("vector", lambda e, i=i, t=t: e.tensor_tensor(out=TMPA[t], in0=XS[i], in1=RSTD, op=ALU.mult),
                 reads=[f"xs{i}", "rstd"], writes=[f"tmpa{t}"])
            P.op("scalar", lambda e, t=t, ft=ft: e.activation(out=HT[:, ft * TT:(ft + 1) * TT], in_=TMPA[t], func=AF.Identity,
                                                             bias=modT[:, 96 * n_ + ft:96 * n_ + ft + 1],
                                                             scale=Acoef[:, 32 * n_ + ft:32 * n_ + ft + 1]),
                 reads=[f"tmpa{t}", "modT", "Acoef"], writes=[f"ht{ft}"])

    def resid_evac(tg, gcol_base):
        def ev(g, bk):
            for j in range(4):
                ft = 4 * g + j
                i = load_x(ft, tg)
                o = cnts["xo"] % 4
                cnts["xo"] += 1
                P.op("vector", lambda e, b=bk[j], i=i, o=o, ft=ft: e.scalar_tensor_tensor(
                    out=XO[o], in0=banks[b][:, :], scalar=hgate[:, gcol_base + ft:gcol_base + ft + 1], in1=XS[i],
                    op0=ALU.mult, op1=ALU.add), reads=[f"bk{bk[j]}", f"xs{i}", "hgate"], writes=[f"xo{o}"])
                P.dma("sync", xT[ft, :, tg * TT:(tg + 1) * TT], XO[o], reads=[f"xo{o}"], writes=[f"xT{ft}_{tg}"], semkey=f"xo{o}")
        return ev

    def ffn(tg, which, n_):
        norm_to_HT(tg, n_)

        def gu_evac(g, bk):
            for m in range(2):
                t = cnts["tmpa"] % 2
                cnts["tmpa"] += 1
                mt = 2 * g + m
                P.op("scalar", lambda e, b=bk[m], t=t: e.activation(out=TMPA[t], in_=banks[b][:, :], func=AF.Silu),
                     reads=[f"bk{bk[m]}"], writes=[f"tmpa{t}"])
                P.op("vector", lambda e, b=bk[2 + m], t=t, mt=mt: e.tensor_tensor(
                    out=ACTT[:, mt * TT:(mt + 1) * TT], in0=TMPA[t], in1=banks[b][:, :], op=ALU.mult),
                    reads=[f"tmpa{t}", f"bk{bk[2 + m]}"], writes=[f"act{mt}"])
        linear(wgu[which], NGU, 32, lambda g: 4, "a", lambda kt: HT[:, kt * TT:(kt + 1) * TT], lambda kt: f"ht{kt}", gu_evac)
        linear(wdn[which], 8, NMF, lambda g: 4, "a", lambda kt: ACTT[:, kt * TT:(kt + 1) * TT], lambda kt: f"act{kt}",
               resid_evac(tg, 32 * n_))


    def final_phase():
        OUTROW = V(o_ACT, 16384).rearrange("p (s f) -> p s f", f=4096)
        for tg in range(NTG):
            sumsq_rstd(tg, 0)
            for ft in range(NFT):
                i = load_x(ft, tg)
                t = cnts["tmpa"] % 2
                cnts["tmpa"] += 1
                P.op("vector", lambda e, i=i, t=t, ft=ft: e.scalar_tensor_tensor(
                    out=TMPA[t], in0=XS[i], scalar=sp[:, SP_GAIN + 96 + ft:SP_GAIN + 97 + ft], in1=RSTD,
                    op0=ALU.mult, op1=ALU.mult), reads=[f"xs{i}", "rstd", "sp"], writes=[f"tmpa{t}"])
                b = 4 + ft % 4
                for s_ in range(4):
                    P.op("tensor", lambda e, b=b, s_=s_, t=t: e.transpose(banks[b][:, s_ * 128:(s_ + 1) * 128],
                                                                        TMPA[t][:, s_ * 128:(s_ + 1) * 128], ident),
                         reads=[f"tmpa{t}", "ident"], writes=[f"bk{b}"])
                eng = alt_eng()
                P.op(eng, copy_op(eng, OUTROW[:, :, ft * 128:(ft + 1) * 128],
                                  banks[b][:, :].rearrange("p (s f) -> p s f", f=128)),
                     reads=[f"bk{b}"], writes=["outrow"])
            for s_ in range(4):
                r0 = tg * TT + s_ * 128
                P.dma("sync", out_d[r0:r0 + 128, :], OUTROW[:, s_, :], reads=["outrow"], writes=[f"out{s_}"], semkey=f"outd{s_}")
        P.emit(final_keys=[f"out{s_}" for s_ in range(4)])
        es.close()
        if VERBOSE:
            print("PROG stats", P.stats, "nsem", P.nsem, "maxcnt", P.maxcnt, flush=True)
        return nc

    def inproj(tg):
        norm_to_HT(tg, 1)

        def a_evac(g, bk):
            for j in range(4):
                tile = 4 * g + j
                if tile >= 85:
                    continue
                o = cnts["xo"] % 4
                cnts["xo"] += 1
                eng = alt_eng()
                P.op(eng, copy_op(eng, XO[o], banks[bk[j]][:, :]), reads=[f"bk{bk[j]}"], writes=[f"xo{o}"])
                P.dma("sync", pT[tile, :, tg * TT:(tg + 1) * TT], XO[o], reads=[f"xo{o}"], writes=[f"pT{tile}"], semkey=f"xo{o}")
        linear(wina, NTA // 4, 32, lambda g: (4 if g < 21 else 1), "a", lambda kt: HT[:, kt * TT:(kt + 1) * TT],
               lambda kt: f"ht{kt}", a_evac)

        def b_evac(g, bk):
            for sb in range(4):
                t = cnts["tmpa"] % 2
                cnts["tmpa"] += 1
                stg = VB(o_MISC + 512 + 512 * t, 512)
                eng = alt_eng()
                P.op(eng, copy_op(eng, stg, banks[bk[sb]][:, :]), reads=[f"bk{bk[sb]}"], writes=[f"tmpa{t}"])
                P.dma("sync", fvS[tg * 4 + sb, :, g * 512:(g + 1) * 512], stg, reads=[f"tmpa{t}"], writes=["fvS"], semkey=f"tmpa{t}")
        linear(winb, 4, 32, None, "b", lambda kt, sb: HT[:, kt * TT + sb * 128: kt * TT + (sb + 1) * 128],
               lambda kt: f"ht{kt}", b_evac)

    def rwkv_phase():
        nbuf = 30208 // S
        assert nbuf >= 14
        def B(i, p0=0, p1=128):
            return V(o_HT + S * i, S, p0, p1)
        oc = o_WS
        W2B = VB(oc, 2048)
        A2B = VB(oc + 1024, 2048)
        G2B = VB(oc + 2048, 4096)
        TH = VB(oc + 4096, S)
        ADM = VB(oc + 4096 + S // 2, S)
        SG = VB(oc + 4096 + S, 2 * S)
        GTm = V(oc + 4096 + 2 * S, 2 * NCH)
        OMMU = V(oc + 4096 + 2 * S + 32, 52)
        OMKA = V(oc + 4096 + 2 * S + 84, 16)
        ob = oc + 4096 + 2 * S + 112
        M4s = [V(ob + 512 * e, 512) for e in range(2)]
        XA = [V(ob + 1024 + 512 * q, 512) for q in range(2)]
        PTs = [V(ob + 2048 + 256 * q, 256) for q in range(2)]
        NS = V(ob + 2560, 256)
        WSB = V(ob + 2816, 128)
        LU = V(ob + 2944, 192)
        Zp = [V(ob + 3136 + 128 * q, 128) for q in range(2)]
        MST2 = V(ob + 3392, 256)
        assert ob + 3648 <= o_WS + 12288, (ob + 3648, o_WS + 12288)
        c60 = 0.6065306597126334

        P.cur_level = 1
        P.op("vector", lambda e: e.memset(VB(oc, 4096), 0.0), writes=["w2b", "a2b"])
        P.op("vector", lambda e: e.memset(VB(oc + 4096, 2 * S), 0.0), writes=["th", "adm"])
        P.dma("gpsimd", W2B[0:LORA, :], w2_in, reads=["w2b"], writes=["w2b"], semkey="constr")
        P.dma("gpsimd", A2B[0:LORA, :], a2_in, reads=["a2b"], writes=["a2b"], semkey="constr")
        P.dma("gpsimd", G2B.rearrange("p (k c) -> p k c", c=2048), g2_in.rearrange("k p c -> p k c"), writes=["g2b"], semkey="constr")
        P.op("vector", lambda e: e.tensor_scalar(out=OMMU, in0=sp[:, SP_MU:SP_MU + 52], scalar1=-1.0, scalar2=1.0,
                                                 op0=ALU.mult, op1=ALU.add), reads=["sp"], writes=["ommu"])
        P.op("vector", lambda e: e.tensor_scalar(out=OMKA, in0=sp[:, SP_KA:SP_KA + 16], scalar1=-1.0, scalar2=1.0,
                                                 op0=ALU.mult, op1=ALU.add), reads=["sp"], writes=["omka"])
        P.op("gpsimd", lambda e: e.memset(LU, 0.0), writes=["lu"])
        P.op("vector", lambda e: e.tensor_copy(out=MST2[:, 0:128], in_=msT), reads=["msT"], writes=["mst2"])
        P.op("vector", lambda e: e.tensor_copy(out=MST2[:, 128:256], in_=msT), reads=["msT", "mst2"], writes=["mst2"])

        P.cur_level = 2
        def mix(dst, dk, src, sk, tmp, tk, mcol, rows=128):
            mu = sp[0:rows, SP_MU + mcol:SP_MU + mcol + 1]
            om = OMMU[0:rows, mcol:mcol + 1]
            P.op("gpsimd", lambda e: e.memset(tmp[0:rows, 0:1], 0.0), writes=[tk])
            P.op("scalar", lambda e: e.mul(out=tmp[0:rows, 1:S], in_=src[0:rows, 0:S - 1], mul=mu), reads=[sk, "sp", tk], writes=[tk])
            P.op("vector", lambda e: e.scalar_tensor_tensor(out=dst[0:rows, :], in0=src[0:rows, :], scalar=om, in1=tmp[0:rows, :],
                                                            op0=ALU.mult, op1=ALU.add), reads=[sk, tk, "ommu"], writes=[dk])

        P.dma("sync", B(0, 0, LORA), pT[48, 0:LORA, :], reads=["pT48"], writes=["b0"], semkey="rb0")
        mix(B(1), "b1", B(0), "b0", B(2), "b2", 48, LORA)
        P.op("scalar", lambda e: e.activation(out=TH[0:LORA, :], in_=B(1, 0, LORA), func=AF.Tanh), reads=["b1", "th"], writes=["th"])
        P.dma("sync", B(3, 0, LORA), pT[49, 0:LORA, :], reads=["pT49"], writes=["b3"], semkey="rb3")
        mix(B(4), "b4", B(3), "b3", B(5), "b5", 49, LORA)
        P.op("vector", lambda e: e.tensor_copy(out=ADM[0:LORA, :], in_=B(4, 0, LORA)), reads=["b4", "adm"], writes=["adm"])
        for kt in range(2):
            P.dma("sync", B(6 + 3 * kt), pT[50 + kt, :, :], reads=[f"pT{50 + kt}"], writes=[f"b{6 + 3 * kt}"], semkey=f"rb{6 + 3 * kt}")
            mix(B(7 + 3 * kt), f"b{7 + 3 * kt}", B(6 + 3 * kt), f"b{6 + 3 * kt}", B(8 + 3 * kt), f"b{8 + 3 * kt}", 50 + kt)
            P.op("scalar", lambda e, kt=kt: e.activation(out=SG[:, kt * S:(kt + 1) * S], in_=B(7 + 3 * kt), func=AF.Sigmoid),
                 reads=[f"b{7 + 3 * kt}"], writes=["sg"])

        def spcol(base, hp, rows=128):
            return sp[0:rows, base + hp:base + hp + 1]

        for hp in range(NHP):
            R_, K_, V_, SIG, A_, KKN, E1, E2 = B(0), B(1), B(2), B(3), B(4), B(5), B(6), B(7)
            AT, BT, KTf, RT, G_, BON = B(8), B(9), B(10), B(11), B(12), B(13)
            hs = slice(hp * 128, (hp + 1) * 128)
            P.cur_level = 3
            for n_, (raw, dst, tmpb) in enumerate([(8, 0, 11), (9, 1, 12), (10, 2, 13)]):
                P.dma("sync", B(raw), pT[16 * n_ + hp, :, :], reads=[f"pT{16 * n_ + hp}"], writes=[f"b{raw}"], semkey=f"rb{raw}")
                mix(B(dst), f"b{dst}", B(raw), f"b{raw}", B(tmpb), f"b{tmpb}", 16 * n_ + hp)
            P.cur_level = 4
            for tg in range(NTG):
                ts_ = slice(tg * TT, (tg + 1) * TT)
                b = tg % 4
                P.op("tensor", lambda e, b=b, ts_=ts_, hs=hs: e.matmul(banks[b][:, :], lhsT=W2B[:, hs], rhs=TH[:, ts_], start=True, stop=True),
                     reads=["w2b", "th"], writes=[f"bk{b}"])
                P.op("scalar", lambda e, b=b, ts_=ts_, hp=hp: e.activation(out=SIG[:, ts_], in_=banks[b][:, :], func=AF.Sigmoid,
                                                                         bias=spcol(SP_W0, hp), scale=1.0),
                     reads=[f"bk{b}", "sp"], writes=["b3"])
                b = 4 + tg % 4
                P.op("tensor", lambda e, b=b, ts_=ts_, hs=hs: e.matmul(banks[b][:, :], lhsT=A2B[:, hs], rhs=ADM[:, ts_], start=True, stop=True),
                     reads=["a2b", "adm"], writes=[f"bk{b}"])
                P.op("scalar", lambda e, b=b, ts_=ts_, hp=hp: e.activation(out=A_[:, ts_], in_=banks[b][:, :], func=AF.Sigmoid,
                                                                         bias=spcol(SP_A0, hp), scale=1.0),
                     reads=[f"bk{b}", "sp"], writes=["b4"])
            CUMS = B(8)
            P.cur_level = 5
            for c in range(2 * NCH):
                cs = slice(c * 64, (c + 1) * 64)
                P.op("vector", lambda e, cs=cs: e.tensor_tensor_scan(out=CUMS[:, cs], data0=ones[:, 0:64], data1=SIG[:, cs], initial=0.0,
                                                                    op0=ALU.mult, op1=ALU.add), reads=["b3", "ones"], writes=["b8"])
            P.op("scalar", lambda e: e.activation(out=E1, in_=CUMS, func=AF.Exp, scale=-c60), reads=["b8"], writes=["b6"])
            P.op("scalar", lambda e: e.activation(out=E2, in_=CUMS, func=AF.Exp, scale=c60), reads=["b8"], writes=["b7"])
            E3 = B(9)
            P.op("vector", lambda e: e.tensor_sub(out=E3, in0=CUMS, in1=SIG), reads=["b8", "b3"], writes=["b9"])
            P.op("scalar", lambda e: e.activation(out=E3, in_=E3, func=AF.Exp, scale=-c60), reads=["b9"], writes=["b9"])
            P.op("vector", lambda e: e.tensor_copy(out=GTm, in_=E1.rearrange("p (c t) -> p c t", t=64)[:, :, 63]), reads=["b6"], writes=["gt"])
            KKr = B(10)
            P.cur_level = 6
            P.op("scalar", lambda e, hp=hp: e.mul(out=KKr, in_=K_, mul=spcol(SP_KK, hp)), reads=["b1", "sp"], writes=["b10"])
            P.op("scalar", lambda e: e.activation(out=BON, in_=KKr, func=AF.Square), reads=["b10"], writes=["b13"])
            for tg in range(NTG):
                ts_ = slice(tg * TT, (tg + 1) * TT)
                b = tg % 4
                P.op("tensor", lambda e, b=b, ts_=ts_: e.matmul(banks[b][:, :], lhsT=blk, rhs=BON[:, ts_], start=True, stop=True),
                     reads=["blk", "b13"], writes=[f"bk{b}"])
                P.op("scalar", lambda e, b=b, ts_=ts_: e.activation(out=KKN[:, ts_], in_=banks[b][:, :], func=AF.Sqrt, bias=1e-24, scale=1.0),
                     reads=[f"bk{b}"], writes=["b5"])
            P.op("vector", lambda e: e.reciprocal(out=KKN, in_=KKN), reads=["b5"], writes=["b5"])
            P.op("vector", lambda e: e.tensor_tensor(out=KKN, in0=KKr, in1=KKN, op=ALU.mult), reads=["b10", "b5"], writes=["b5"])
            P.cur_level = 7
            P.op("vector", lambda e: e.scalar_tensor_tensor(out=AT, in0=KKN, scalar=-1.0, in1=E3, op0=ALU.mult, op1=ALU.mult),
                 reads=["b5", "b9"], writes=["b8"])
            P.op("vector", lambda e: e.tensor_tensor(out=KKr, in0=KKN, in1=A_, op=ALU.mult), reads=["b5", "b4"], writes=["b10"])
            P.op("vector", lambda e: e.tensor_tensor(out=BT, in0=KKr, in1=E2, op=ALU.mult), reads=["b10", "b7", "b8"], writes=["b9"])
            P.op("vector", lambda e, hp=hp: e.tensor_scalar(out=BON, in0=A_, scalar1=spcol(SP_KA, hp), scalar2=OMKA[:, hp:hp + 1],
                                                           op0=ALU.mult, op1=ALU.add), reads=["b4", "sp", "omka"], writes=["b13"])
            P.op("vector", lambda e: e.tensor_tensor(out=K_, in0=K_, in1=BON, op=ALU.mult), reads=["b1", "b13", "b10"], writes=["b1"])
            P.op("vector", lambda e: e.tensor_tensor(out=KTf, in0=K_, in1=E2, op=ALU.mult), reads=["b1", "b7", "b9"], writes=["b10"])
            P.op("vector", lambda e: e.tensor_tensor(out=RT, in0=R_, in1=E1, op=ALU.mult), reads=["b0", "b6"], writes=["b11"])
            P.cur_level = 8
            for tg in range(NTG):
                ts_ = slice(tg * TT, (tg + 1) * TT)
                b = 4 + tg % 4
                for kt in range(2):
                    P.op("tensor", lambda e, b=b, ts_=ts_, kt=kt, hs=hs: e.matmul(
                        banks[b][:, :], lhsT=G2B[:, kt * 2048 + hs.start: kt * 2048 + hs.stop],
                        rhs=SG[:, kt * S + ts_.start: kt * S + ts_.stop], start=(kt == 0), stop=(kt == 1)),
                        reads=["g2b", "sg"], writes=[f"bk{b}"])
                P.op("scalar", lambda e, b=b, ts_=ts_: e.copy(out=G_[:, ts_], in_=banks[b][:, :]), reads=[f"bk{b}"], writes=["b12"])
            P.cur_level = 9
            P.op("vector", lambda e, hp=hp: e.scalar_tensor_tensor(out=BON, in0=R_, scalar=spcol(SP_RK, hp), in1=K_,
                                                                  op0=ALU.mult, op1=ALU.mult), reads=["b0", "b1", "sp"], writes=["b13"])
            for tg in range(NTG):
                ts_ = slice(tg * TT, (tg + 1) * TT)
                b = tg % 4
                P.op("tensor", lambda e, b=b, ts_=ts_: e.matmul(banks[b][:, :], lhsT=blk, rhs=BON[:, ts_], start=True, stop=True),
                     reads=["blk", "b13"], writes=[f"bk{b}"])
                P.op("vector", lambda e, b=b, ts_=ts_: e.tensor_tensor(out=BON[:, ts_], in0=banks[b][:, :], in1=V_[:, ts_], op=ALU.mult),
                     reads=[f"bk{b}", "b2", "b13"], writes=["b13"])
            P.cur_level = 10
            BTOK = B(0).rearrange("p (c k) -> p c k", k=128)
            KTOK = B(1).rearrange("p (c k) -> p c k", k=128)
            LV = V(o_HT + 3 * S, NCH * 192).rearrange("p (c k) -> p c k", k=192)
            P.op("gpsimd", lambda e: e.memset(V(o_HT + 3 * S, NCH * 192), 0.0), reads=["b3", "b4"], writes=["b3", "b4"])
            for c in range(NCH):
                cs = slice(c * 128, (c + 1) * 128)
                b0_ = 3 * (c % 2)
                b1_, b2_ = b0_ + 1, b0_ + 2
                P.op("tensor", lambda e, b=b0_, cs=cs: e.transpose(banks[b][:, 0:128], BT[:, cs], ident), reads=["b9", "ident"], writes=[f"bk{b0_}"])
                P.op("tensor", lambda e, b=b1_, cs=cs: e.transpose(banks[b][:, 0:128], KTf[:, cs], ident), reads=["b10", "ident"], writes=[f"bk{b1_}"])
                P.op("tensor", lambda e, b=b2_, cs=cs: e.transpose(banks[b][:, 0:128], V_[:, cs], ident), reads=["b2", "ident"], writes=[f"bk{b2_}"])
                P.op("vector", lambda e, b=b0_, c=c: e.tensor_copy(out=BTOK[:, c, :], in_=banks[b][:, 0:128]), reads=[f"bk{b0_}", "b11", "b13"], writes=["b0"])
                P.op("scalar", lambda e, b=b1_, c=c: e.copy(out=KTOK[:, c, :], in_=banks[b][:, 0:128]), reads=[f"bk{b1_}", "b10", "b11", "b13"], writes=["b1"])
                P.op("vector", lambda e, b=b2_, c=c: e.tensor_copy(out=LV[:, c, 0:64], in_=banks[b][:, 0:64]), reads=[f"bk{b2_}"], writes=["b3", "b4"])
                P.op("vector", lambda e, b=b2_, c=c: e.tensor_copy(out=LV[:, c, 128:192], in_=banks[b][:, 64:128]), reads=[f"bk{b2_}", "b3", "b4"], writes=["b3", "b4"])
            YOUT = B(5)
            P.cur_level = 11
            P.op("gpsimd", lambda e: e.memset(Zp[0], 0.0), writes=["zp0"])
            zcnt = [0]
            for c in (range(NCH) if "c" in parts else []):
                cs = slice(c * 128, (c + 1) * 128)
                for e_ in range(2):
                    ps_ = slice(64 * e_, 64 * e_ + 64)
                    for q, (lk, lb, rk_, rb) in enumerate([(BT, "b9", AT, "b8"), (BT, "b9", RT, "b11"), (KTf, "b10", AT, "b8"), (KTf, "b10", RT, "b11")]):
                        P.op("tensor", lambda e, e_=e_, q=q, lk=lk, rk_=rk_, ps_=ps_, cs=cs: e.matmul(
                            banks[e_][:, q * 128:(q + 1) * 128], lhsT=lk[ps_, cs], rhs=rk_[ps_, cs], start=True, stop=True),
                            reads=[lb, rb], writes=[f"bk{e_}"])
                    P.op("vector", lambda e, e_=e_: e.tensor_tensor(out=M4s[e_], in0=banks[e_][:, :], in1=mcat, op=ALU.mult),
                         reads=[f"bk{e_}", "mcat"], writes=[f"m4{e_}"])
                    P.op("tensor", lambda e, e_=e_, ps_=ps_, cs=cs: e.matmul(banks[4][:, 256 + e_ * 128:256 + (e_ + 1) * 128],
                                                                             lhsT=AT[ps_, cs], rhs=BT[ps_, cs], start=True, stop=True),
                         reads=["b8", "b9"], writes=["bk4"])
                P.op("vector", lambda e: e.tensor_tensor(out=NS, in0=banks[4][:, 256:512], in1=MST2, op=ALU.mult),
                     reads=["bk4", "mst2"], writes=["ns"])
                for e_ in range(2):
                    P.op("gpsimd", lambda e, e_=e_: e.tensor_tensor(out=PTs[0][:, e_ * 128:(e_ + 1) * 128], in0=M4s[e_][:, 0:128], in1=ident, op=ALU.add),
                         reads=[f"m4{e_}", "ident"], writes=["pt0"])
                for lv in range(5):
                    qc, qn = lv % 2, (lv + 1) % 2
                    def Xop(e_, lv=lv, qc=qc):
                        if lv == 0:
                            return NS[:, e_ * 128:(e_ + 1) * 128], M4s[e_][:, 0:128]
                        return XA[qc][:, e_ * 256: e_ * 256 + 128], XA[qc][:, e_ * 256 + 128: e_ * 256 + 256]
                    xkeys = ["ns", "m40", "m41"] if lv == 0 else [f"xa{qc}"]
                    for e_ in range(2):
                        X_, XT_ = Xop(e_)
                        P.op("tensor", lambda e, e_=e_, X_=X_, XT_=XT_: e.matmul(banks[3][:, e_ * 256: e_ * 256 + 128], lhsT=XT_, rhs=X_, start=True, stop=True),
                             reads=xkeys, writes=["bk3"])
                        if lv < 4:
                            P.op("tensor", lambda e, e_=e_, X_=X_, XT_=XT_: e.matmul(banks[3][:, e_ * 256 + 128: e_ * 256 + 256], lhsT=X_, rhs=XT_, start=True, stop=True),
                                 reads=xkeys, writes=["bk3"])
                    eng = "scalar" if lv % 2 == 0 else "vector"
                    P.op(eng, copy_op(eng, XA[qn], banks[3][:, :]), reads=["bk3"], writes=[f"xa{qn}"])
                    for e_ in range(2):
                        es_ = slice(e_ * 128, (e_ + 1) * 128)
                        P.op("tensor", lambda e, es_=es_, qc=qc: e.matmul(banks[4][:, es_], lhsT=ident, rhs=PTs[qc][:, es_], start=True, stop=False),
                             reads=["ident", f"pt{qc}"], writes=["bk4"])
                        P.op("tensor", lambda e, es_=es_, qc=qc, qn=qn, e_=e_: e.matmul(banks[4][:, es_], lhsT=XA[qn][:, e_ * 256: e_ * 256 + 128],
                                                                                      rhs=PTs[qc][:, es_], start=False, stop=True),
                             reads=[f"xa{qn}", f"pt{qc}"], writes=["bk4"])
                    eng = "vector" if lv % 2 == 0 else "scalar"
                    P.op(eng, copy_op(eng, PTs[qn], banks[4][:, 0:256]), reads=["bk4"], writes=[f"pt{qn}"])
                TmT = PTs[1]
                for s_ in range(2):
                    zc, zn = zcnt[0] % 2, (zcnt[0] + 1) % 2
                    zcnt[0] += 1
                    rows = slice(64 * s_, 64 * s_ + 64)
                    tsub = slice(c * 128 + 64 * s_, c * 128 + 64 * s_ + 64)
                    P.op("tensor", lambda e, cs=cs, zc=zc: e.matmul(banks[5][:, 0:128], lhsT=AT[:, cs], rhs=Zp[zc], start=True, stop=False),
                         reads=["b8", f"zp{zc}"], writes=["bk5"])
                    for e_ in range(2):
                        P.op("tensor", lambda e, e_=e_, c=c: e.matmul(banks[5][:, e_ * 64:(e_ + 1) * 64], lhsT=M4s[e_][:, 256:384],
                                                                     rhs=LV[:, c, e_ * 128: e_ * 128 + 64], start=False, stop=(e_ == 1)),
                             reads=[f"m4{e_}", "b3", "b4"], writes=["bk5"])
                    P.op("scalar", lambda e: e.copy(out=WSB, in_=banks[5][:, 0:128]), reads=["bk5"], writes=["wsb"])
                    for e_ in range(2):
                        P.op("tensor", lambda e, e_=e_: e.matmul(banks[6][:, e_ * 64:(e_ + 1) * 64], lhsT=TmT[:, e_ * 128:(e_ + 1) * 128],
                                                                rhs=WSB[:, e_ * 64:(e_ + 1) * 64], start=True, stop=True),
                             reads=["pt1", "wsb"], writes=["bk6"])
                    P.op("vector", lambda e: e.tensor_copy(out=LU[:, 0:64], in_=banks[6][:, 0:64]), reads=["bk6"], writes=["lu"])
                    P.op("vector", lambda e: e.tensor_copy(out=LU[:, 128:192], in_=banks[6][:, 64:128]), reads=["bk6", "lu"], writes=["lu"])
                    P.op("tensor", lambda e, tsub=tsub, zc=zc: e.matmul(banks[2][:, 0:64], lhsT=Zp[zc], rhs=RT[:, tsub], start=True, stop=False),
                         reads=[f"zp{zc}", "b11"], writes=["bk2"])
                    for e_ in range(2):
                        P.op("tensor", lambda e, e_=e_, s_=s_: e.matmul(banks[2][:, 0:64], lhsT=LU[:, e_ * 64: e_ * 64 + 128],
                                                                       rhs=M4s[e_][:, 128 + 64 * s_:128 + 64 * s_ + 64],
                                                                       start=False, stop=False), reads=["lu", f"m4{e_}"], writes=["bk2"])
                    for e_ in range(2):
                        P.op("tensor", lambda e, e_=e_, c=c, s_=s_: e.matmul(banks[2][:, 0:64], lhsT=LV[:, c, e_ * 64: e_ * 64 + 128],
                                                                            rhs=M4s[e_][:, 384 + 64 * s_:384 + 64 * s_ + 64],
                                                                            start=False, stop=(e_ == 1)), reads=["b3", "b4", f"m4{e_}"], writes=["bk2"])
                    P.op("scalar", lambda e, tsub=tsub: e.copy(out=YOUT[:, tsub], in_=banks[2][:, 0:64]), reads=["bk2"], writes=["b5"])
                    P.op("tensor", lambda e, zc=zc: e.matmul(banks[7][:, 0:128], lhsT=ident, rhs=Zp[zc], start=True, stop=False),
                         reads=["ident", f"zp{zc}"], writes=["bk7"])
                    for e_ in range(2):
                        P.op("tensor", lambda e, e_=e_, c=c, rows=rows: e.matmul(banks[7][:, e_ * 64:(e_ + 1) * 64], lhsT=BTOK[rows, c, :],
                                                                                rhs=LU[rows, e_ * 128: e_ * 128 + 64], start=False, stop=False),
                             reads=["b0", "lu"], writes=["bk7"])
                    for e_ in range(2):
                        P.op("tensor", lambda e, e_=e_, c=c, rows=rows: e.matmul(banks[7][:, e_ * 64:(e_ + 1) * 64], lhsT=KTOK[rows, c, :],
                                                                                rhs=LV[rows, c, e_ * 128: e_ * 128 + 64], start=False, stop=(e_ == 1)),
                             reads=["b1", "b3", "b4"], writes=["bk7"])
                    P.op("vector", lambda e, c=c, s_=s_, zn=zn: e.scalar_tensor_tensor(out=Zp[zn], in0=banks[7][:, 0:128],
                                                                                      scalar=GTm[:, 2 * c + s_:2 * c + s_ + 1], in1=blk,
                                                                                      op0=ALU.mult, op1=ALU.mult),
                         reads=["bk7", "gt", "blk"], writes=[f"zp{zn}"])
            P.cur_level = 12
            CEN, SQb = B(6), B(7)
            for tg in range(NTG):
                ts_ = slice(tg * TT, (tg + 1) * TT)
                b = tg % 2
                P.op("tensor", lambda e, b=b, ts_=ts_: e.matmul(banks[b][:, :], lhsT=blk, rhs=YOUT[:, ts_], start=True, stop=True),
                     reads=["blk", "b5"], writes=[f"bk{b}"])
                P.op("vector", lambda e, b=b, ts_=ts_: e.scalar_tensor_tensor(out=CEN[:, ts_], in0=banks[b][:, :], scalar=-1.0 / 64, in1=YOUT[:, ts_],
                                                                             op0=ALU.mult, op1=ALU.add), reads=[f"bk{b}", "b5"], writes=["b6"])
                P.op("scalar", lambda e, ts_=ts_: e.activation(out=SQb[:, ts_], in_=CEN[:, ts_], func=AF.Square), reads=["b6"], writes=["b7"])
                b2 = 2 + tg % 2
                P.op("tensor", lambda e, b2=b2, ts_=ts_: e.matmul(banks[b2][:, :], lhsT=blk, rhs=SQb[:, ts_], start=True, stop=True),
                     reads=["blk", "b7"], writes=[f"bk{b2}"])
                P.op("scalar", lambda e, b2=b2, ts_=ts_: e.activation(out=SQb[:, ts_], in_=banks[b2][:, :], func=AF.Sqrt, bias=64e-5, scale=1.0 / 64),
                     reads=[f"bk{b2}", "b7"], writes=["b7"])
                P.op("vector", lambda e, ts_=ts_: e.reciprocal(out=SQb[:, ts_], in_=SQb[:, ts_]), reads=["b7"], writes=["b7"])
                P.op("vector", lambda e, ts_=ts_: e.tensor_tensor(out=CEN[:, ts_], in0=CEN[:, ts_], in1=SQb[:, ts_], op=ALU.mult), reads=["b6", "b7"], writes=["b6"])
                P.op("scalar", lambda e, ts_=ts_, hp=hp: e.activation(out=CEN[:, ts_], in_=CEN[:, ts_], func=AF.Identity,
                                                                    bias=spcol(SP_LNB, hp), scale=spcol(SP_LNW, hp)), reads=["b6", "sp"], writes=["b6"])
                P.op("vector", lambda e, ts_=ts_: e.tensor_tensor(out=CEN[:, ts_], in0=CEN[:, ts_], in1=BON[:, ts_], op=ALU.add), reads=["b6", "b13"], writes=["b6"])
                t = cnts["tmpa"] % 2
                cnts["tmpa"] += 1
                stg = VB(o_MISC + 512 + 512 * t, 512)
                P.op("vector", lambda e, ts_=ts_, stg=stg: e.tensor_tensor(out=stg, in0=CEN[:, ts_], in1=G_[:, ts_], op=ALU.mult),
                     reads=["b6", "b12"], writes=[f"tmpa{t}"])
                P.dma("sync", yT[hp, :, ts_], stg, reads=[f"tmpa{t}"], writes=[f"yT{hp}"], semkey=f"tmpa{t}")

    def fox_phase():
        oa = o_HT
        QT = VB(oa, S); KTb = VB(oa + S // 2, S); VTm = VB(oa + S, S).rearrange("p (c d) -> p c d", d=128)
        CUM = V(oa + 3 * S // 2, S, 0, 16)
        CUMB = VB(oa + 5 * S // 2, S, 0, 16)
        FRAW = V(oa + 3 * S, S, 0, 16)
        NEGC = V(oa + 4 * S, NCH * 16).rearrange("p (c h) -> p c h", h=16)
        ob = oa + 4 * S + NCH * 16
        PTr = [VB(ob + 256 * i, 512) for i in range(3)]
        OS = V(ob + 768, 512); RS = V(ob + 1280, 512); SQf = V(ob + 1792, 512); RN = V(ob + 2304, 512)
        YB = [VB(ob + 2816 + 256 * i, 512) for i in range(2)]
        scale = float(128.0 ** -0.5)
        P.dma("sync", FRAW, pT[84, 0:16, :], reads=["pT84"], writes=["fraw"], semkey="fraw")
        P.op("scalar", lambda e: e.activation(out=FRAW, in_=FRAW, func=AF.Sigmoid, bias=sp[0:16, SP_BF:SP_BF + 1], scale=1.0),
             reads=["fraw", "sp"], writes=["fraw"])
        P.op("scalar", lambda e: e.activation(out=FRAW, in_=FRAW, func=AF.Ln), reads=["fraw"], writes=["fraw"])
        for c in range(NCH):
            cs = slice(c * 128, (c + 1) * 128)
            init = 0.0 if c == 0 else CUM[:, c * 128 - 1:c * 128]
            P.op("vector", lambda e, cs=cs, init=init: e.tensor_tensor_scan(out=CUM[:, cs], data0=ones[0:16, :], data1=FRAW[:, cs], initial=init,
                                                                           op0=ALU.mult, op1=ALU.add), reads=["fraw", "ones", "cum"], writes=["cum"])
        P.op("vector", lambda e: e.tensor_copy(out=CUMB, in_=CUM), reads=["cum"], writes=["cumb"])
        for c in range(NCH):
            b = c % 4
            P.op("tensor", lambda e, b=b, c=c: e.transpose(banks[b][:, 0:16], CUM[:, c * 128:(c + 1) * 128], ident[0:16, 0:16]),
                 reads=["cum", "ident"], writes=[f"bk{b}"])
            P.op("scalar", lambda e, b=b, c=c: e.mul(out=NEGC[:, c, :], in_=banks[b][:, 0:16], mul=-1.0), reads=[f"bk{b}"], writes=["negc"])
        NQG = S // 512
        for h in range(NFH):
            P.dma("gpsimd", QT, pT[52 + h, :, :], reads=[f"pT{52 + h}"], writes=["qt"], semkey="qt")
            P.dma("gpsimd", KTb, pT[68 + h, :, :], reads=[f"pT{68 + h}"], writes=["ktb"], semkey="ktb")
            P.dma("sync", VTm, fvS[:, :, h * 128:(h + 1) * 128].rearrange("c p d -> p c d"), reads=["fvS"], writes=["vtm"], semkey="vtm")
            for qg in range(NQG):
                qs = slice(qg * 512, (qg + 1) * 512)
                nkb = 4 * (qg + 1)
                for j in range(nkb):
                    c0 = (j - 4 * qg) * 128 if j >= 4 * qg else 0
                    sb_ = j % 2
                    pr = cnts["sq"] % 3
                    cnts["sq"] += 1
                    P.op("tensor", lambda e, sb_=sb_, j=j, qs=qs, c0=c0: e.matmul(banks[sb_][:, c0:512], lhsT=KTb[:, j * 128:(j + 1) * 128],
                                                                                 rhs=QT[:, qs.start + c0:qs.stop], start=True, stop=False),
                         reads=["ktb", "qt"], writes=[f"bk{sb_}"])
                    diag = j >= 4 * qg
                    P.op("tensor", lambda e, sb_=sb_, h=h, qs=qs, c0=c0, diag=diag: e.matmul(banks[sb_][:, c0:512], lhsT=selb[0:16, h * 128:(h + 1) * 128],
                                                                                 rhs=CUMB[:, qs.start + c0:qs.stop], start=False, stop=(not diag)),
                         reads=["selb", "cumb"], writes=[f"bk{sb_}"])
                    if diag:
                        P.op("tensor", lambda e, sb_=sb_, c0=c0: e.matmul(banks[sb_][:, c0:c0 + 128], lhsT=identb, rhs=negmb, start=False, stop=True),
                             reads=["identb", "negmb"], writes=[f"bk{sb_}"])
                    P.op("scalar", lambda e, sb_=sb_, pr=pr, j=j, h=h, c0=c0: e.activation(out=PTr[pr][:, c0:512], in_=banks[sb_][:, c0:512], func=AF.Exp,
                                                                                          bias=NEGC[:, j, h:h + 1], scale=scale),
                         reads=[f"bk{sb_}", "negc"], writes=[f"ptr{pr}"])
                    P.op("tensor", lambda e, pr=pr, j=j, c0=c0, nkb=nkb: e.matmul(banks[2][:, c0:512], lhsT=VTm[:, j, :], rhs=PTr[pr][:, c0:512],
                                                                                 start=(j == 0), stop=(j == nkb - 1)),
                         reads=["vtm", f"ptr{pr}"], writes=["bk2"])
                    P.op("tensor", lambda e, pr=pr, j=j, c0=c0, nkb=nkb: e.matmul(banks[3][:, c0:512], lhsT=onesb, rhs=PTr[pr][:, c0:512],
                                                                                 start=(j == 0), stop=(j == nkb - 1)),
                         reads=["onesb", f"ptr{pr}"], writes=["bk3"])
                P.op("vector", lambda e: e.reciprocal(out=RS, in_=banks[3][:, :]), reads=["bk3"], writes=["rs"])
                P.op("vector", lambda e: e.tensor_tensor(out=OS, in0=banks[2][:, :], in1=RS, op=ALU.mult), reads=["bk2", "rs"], writes=["os"])
                P.op("scalar", lambda e: e.activation(out=SQf, in_=OS, func=AF.Square), reads=["os"], writes=["sqf"])
                P.op("tensor", lambda e: e.matmul(banks[4][:, :], lhsT=ones, rhs=SQf, start=True, stop=True), reads=["ones", "sqf"], writes=["bk4"])
                P.op("scalar", lambda e: e.activation(out=RN, in_=banks[4][:, :], func=AF.Sqrt, bias=1e-6, scale=1.0 / 128), reads=["bk4"], writes=["rn"])
                P.op("vector", lambda e: e.reciprocal(out=RN, in_=RN), reads=["rn"], writes=["rn"])
                yb = cnts["tmpa"] % 2
                cnts["tmpa"] += 1
                P.op("vector", lambda e, yb=yb, h=h: e.scalar_tensor_tensor(out=YB[yb], in0=OS, scalar=sp[:, SP_FON + h:SP_FON + h + 1], in1=RN,
                                                                           op0=ALU.mult, op1=ALU.mult), reads=["os", "rn", "sp"], writes=[f"yb{yb}"])
                P.dma("sync", yT[16 + h, :, qs], YB[yb], reads=[f"yb{yb}"], writes=[f"yT{16 + h}"], semkey=f"yb{yb}")

    def outproj(tg):
        for kt in range(NFT):
            P.dma("sync", HT[:, kt * TT:(kt + 1) * TT], yT[kt, :, tg * TT:(tg + 1) * TT], reads=[f"yT{kt}"], writes=[f"ht{kt}"], semkey=f"htl{kt}")
        linear(wo_in, 8, 32, lambda g: 4, "a", lambda kt: HT[:, kt * TT:(kt + 1) * TT], lambda kt: f"ht{kt}", resid_evac(tg, 32))

    for tg in range(NTG):
        ffn(tg, 0, 0)
    P.barrier()
    if stop_after == "ffn1":
        return final_phase()
    for tg in range(NTG):
        inproj(tg)
    P.barrier()
    parts = "rcfo"
    if stop_after == "nomix":
        parts = ""
    elif stop_after is not None and stop_after.startswith("mix:"):
        parts = stop_after[4:]
    if "r" in parts:
        import re as _re
        m_ = _re.search(r"L(\d+)", parts)
        if m_:
            P.max_level = int(m_.group(1))
        rwkv_phase()
        P.cur_level = 0
        P.barrier()
    if "f" in parts:
        fox_phase()
        P.barrier()
    if "o" in parts:
        for tg in range(NTG):
            outproj(tg)
        P.barrier()
    for tg in range(NTG):
        ffn(tg, 1, 2)
    P.barrier()
    return final_phase()


def _tile_w(W, C=512):
    K, M = W.shape
    return np.ascontiguousarray(W.reshape(K // 128, 128, M // C, C).transpose(2, 1, 0, 3))


def _col(v):
    v = np.asarray(v, np.float32).reshape(-1, 128)
    return np.ascontiguousarray(v.T)


def prepare(inputs, S, DFF):
    f32 = np.float32
    g = lambda k: np.asarray(inputs[k], f32)
    B = g("x").shape[0]
    w_in = g("w_in")[0]
    shared = {}
    shared["wmod"] = _tile_w(g("w_mod")[0])
    for i, nm in ((1, "ffn1"), (2, "ffn2")):
        ga, up, dn = g(nm + "_gate")[0], g(nm + "_up")[0], g(nm + "_down")[0]
        K = ga.shape[0]
        gu = np.concatenate([ga.reshape(K, DFF // 256, 256), up.reshape(K, DFF // 256, 256)], axis=2).reshape(K, 2 * DFF)
        shared[f"wgu{i}"] = _tile_w(gu)
        shared[f"wdn{i}"] = _tile_w(dn)
    wa = np.zeros((D, NTA * 128), f32)
    wa[:, 0:6144] = w_in[:, 0:6144]
    wa[:, 48 * 128:48 * 128 + 96] = w_in[:, 6144:6240]
    wa[:, 49 * 128:49 * 128 + 96] = w_in[:, 6240:6336]
    wa[:, 50 * 128:52 * 128] = w_in[:, 6336:6592]
    wa[:, 52 * 128:84 * 128] = w_in[:, 6592:10688]
    wa[:, 84 * 128:84 * 128 + 16] = w_in[:, 12736:12752]
    shared["wina"] = _tile_w(wa)
    shared["winb"] = _tile_w(w_in[:, 10688:12736])
    shared["wo"] = _tile_w(g("w_out")[0])
    shared["w2"] = np.ascontiguousarray(g("rwkv_w2")[0])
    shared["a2"] = np.ascontiguousarray(g("rwkv_a2")[0])
    shared["g2"] = np.ascontiguousarray(g("rwkv_g2")[0].reshape(2, 128, 2048))
    mu = g("rwkv_mu")[0]
    mut = np.zeros((128, 52), f32)
    mut[:, 0:48] = _col(mu[0:6144])
    mut[0:96, 48] = mu[6144:6240]
    mut[0:96, 49] = mu[6240:6336]
    mut[:, 50:52] = _col(mu[6336:6592])
    per = []
    for b in range(B):
        sp = np.zeros((128, NSP), f32)
        sp[:, SP_C:SP_C + 32] = _col(g("c")[b])
        sp[:, SP_BMOD:SP_BMOD + 288] = _col(g("b_mod")[0])
        sp[:, SP_GAIN:SP_GAIN + 32] = _col(g("norm_ffn1")[0])
        sp[:, SP_GAIN + 32:SP_GAIN + 64] = _col(g("norm_mix")[0])
        sp[:, SP_GAIN + 64:SP_GAIN + 96] = _col(g("norm_ffn2")[0])
        sp[:, SP_GAIN + 96:SP_GAIN + 128] = _col(g("norm_final"))
        sp[:, SP_MU:SP_MU + 52] = mut
        sp[:, SP_W0:SP_W0 + 16] = _col(g("rwkv_w0")[0])
        sp[:, SP_A0:SP_A0 + 16] = _col(g("rwkv_a0")[0])
        sp[:, SP_KK:SP_KK + 16] = _col(g("rwkv_k_k")[0])
        sp[:, SP_KA:SP_KA + 16] = _col(g("rwkv_k_a")[0])
        sp[:, SP_LNW:SP_LNW + 16] = _col(g("rwkv_ln_w")[0])
        sp[:, SP_LNB:SP_LNB + 16] = _col(g("rwkv_ln_b")[0])
        sp[:, SP_RK:SP_RK + 16] = _col(g("rwkv_r_k")[0].reshape(-1))
        sp[:, SP_FON:SP_FON + 16] = _col(g("fox_out_norm")[0])
        sp[0:16, SP_BF] = g("fox_b_f")[0]
        d = dict(shared)
        d["x"] = np.ascontiguousarray(g("x")[b])
        d["sp"] = sp
        per.append(d)
    return per


_NC_CACHE = {}
VERBOSE = False
DEBUG_OUT = False
LAST_RES = [None]


def run(inputs, S, DFF, stop_after=None, n_cores=8):
    import time
    t0 = time.time()
    per = prepare(inputs, S, DFF)
    t1 = time.time()
    key = (S, DFF, stop_after)
    if key not in _NC_CACHE:
        _NC_CACHE[key] = build_nc(S, DFF, stop_after)
    nc = _NC_CACHE[key]
    B = len(per)
    in_maps = [per[(i * B) // n_cores] for i in range(n_cores)]
    t2 = time.time()
    res = run_bass_kernel_spmd(nc, in_maps, core_ids=list(range(n_cores)))
    if VERBOSE:
        print("TIMES prepare %.1f build %.1f run %.1f" % (t1 - t0, t2 - t1, time.time() - t2), flush=True)
    step = n_cores // B
    LAST_RES[0] = res
    return np.stack([np.asarray(res.results[b * step]["out"], np.float32) for b in range(B)], axis=0)


def kernel(**inputs):
    S = int(np.asarray(inputs["x"]).shape[1])
    DFF = int(np.asarray(inputs["ffn1_gate"]).shape[2])
    return run(inputs, S, DFF)
```

```python
import numpy as np
from contextlib import ExitStack
import concourse.bass as bass
import concourse.mybir as mybir
from concourse.bass_utils import run_bass_kernel_spmd

F32 = mybir.dt.float32
BF16 = mybir.dt.bfloat16
AF = mybir.ActivationFunctionType
ALU = mybir.AluOpType
AX = mybir.AxisListType

ENGS = ("tensor", "vector", "scalar", "gpsimd", "sync")
D = 4096
NFT = 32
TT = 512
NHP = 16
NFH = 16
LORA = 96
NTA = 88


class Prog:
    def __init__(self, nc):
        self.nc = nc
        self.ops = []
        self.cur_level = 0
        self.max_level = 99

    def op(self, eng, fn, reads=(), writes=(), dma=False, semkey=None):
        if self.cur_level > self.max_level:
            return
        self.ops.append(dict(eng=eng, fn=fn, reads=tuple(reads), writes=tuple(writes),
                             dma=dma, semkey=semkey, bar=False))

    def dma(self, eng, out, in_, reads=(), writes=(), semkey="dma"):
        self.op(eng, lambda e, o=out, i=in_: e.dma_start(out=o, in_=i),
                reads=reads, writes=writes, dma=True, semkey=semkey)

    def barrier(self):
        self.ops.append(dict(bar=True))

    def emit(self, final_keys=()):
        nc = self.nc
        ops = [o for o in self.ops]
        n = len(ops)
        last_w = {}
        readers = {}
        deps = [None] * n
        last_eng = {}
        last_dma = {}
        pend_bar = {}
        for i, o in enumerate(ops):
            if o["bar"]:
                s = set(last_eng.values()) | set(last_dma.values())
                for e in ENGS:
                    pend_bar[e] = set(s) | pend_bar.get(e, set())
                continue
            d = set()
            for k in o["reads"]:
                if k in last_w:
                    d.add(last_w[k])
            for k in o["writes"]:
                if k in last_w:
                    d.add(last_w[k])
                d.update(readers.get(k, ()))
            for k in o["reads"]:
                readers.setdefault(k, []).append(i)
            for k in o["writes"]:
                last_w[k] = i
                readers[k] = []
            if o["eng"] in pend_bar:
                d |= pend_bar.pop(o["eng"])
            d.discard(i)
            deps[i] = d
            if o["dma"]:
                last_dma[o["semkey"]] = i
            else:
                last_eng[o["eng"]] = i
        final_deps = set()
        for k in final_keys:
            if k in last_w:
                final_deps.add(last_w[k])
        needed = set()
        for i, o in enumerate(ops):
            if o["bar"]:
                continue
            best = {}
            for p in deps[i]:
                po = ops[p]
                if po["dma"]:
                    key = ("dma", po["semkey"])
                else:
                    if po["eng"] == o["eng"] and o["eng"] in ("tensor", "sync") and not o["dma"]:
                        continue
                    key = ("eng", po["eng"])
                if key not in best or p > best[key]:
                    best[key] = p
            deps[i] = set(best.values())
            needed |= deps[i]
        needed |= final_deps
        needed |= {i for i, o in enumerate(ops) if not o["bar"] and o["dma"]}
        sems = {}
        ctxs = []
        keys = [("eng", e) for e in ENGS]
        for o in ops:
            if not o["bar"] and o["dma"] and ("dma", o["semkey"]) not in keys:
                keys.append(("dma", o["semkey"]))
        for k in keys:
            cm = nc.semaphore("s_" + "_".join(str(x) for x in k))
            sems[k] = cm.__enter__()
            ctxs.append(cm)
        cnt = {k: 0 for k in sems}
        opsem = [None] * n
        for i, o in enumerate(ops):
            if i in needed:
                sk = ("dma", o["semkey"]) if o["dma"] else ("eng", o["eng"])
                cnt[sk] += 16 if o["dma"] else 1
                opsem[i] = (sk, cnt[sk])
        for i, o in enumerate(ops):
            if opsem[i] is not None and o["dma"] and str(o["semkey"]).startswith("const"):
                opsem[i] = (opsem[i][0], cnt[opsem[i][0]])
        self.nsem = len(sems)
        self.maxcnt = max(cnt.values())
        self.stats = {e: sum(1 for o in ops if not o["bar"] and o["eng"] == e) for e in ENGS}
        per_eng = {e: [] for e in ENGS}
        for i, o in enumerate(ops):
            if not o["bar"]:
                per_eng[o["eng"]].append(i)
        waited = {e: {} for e in ENGS}

        def emit_engine(ename, eng):
            for i in per_eng[ename]:
                o = ops[i]
                need = {}
                for p in deps[i]:
                    sk, v = opsem[p]
                    if v > need.get(sk, 0):
                        need[sk] = v
                for sk, v in need.items():
                    if waited[ename].get(sk, 0) >= v:
                        continue
                    eng.wait_ge(sems[sk], v)
                    waited[ename][sk] = v
                ins = o["fn"](eng)
                if opsem[i] is not None:
                    ins.then_inc(sems[opsem[i][0]], 16 if o["dma"] else 1)
            if ename == "sync":
                need = {}
                for p in final_deps:
                    sk, v = opsem[p]
                    if v > need.get(sk, 0):
                        need[sk] = v
                for sk, v in need.items():
                    eng.wait_ge(sems[sk], v)

        with nc.Block() as block:
            @block.tensor
            def _(e):
                emit_engine("tensor", e)

            @block.vector
            def _(e):
                emit_engine("vector", e)

            @block.scalar
            def _(e):
                emit_engine("scalar", e)

            @block.gpsimd
            def _(e):
                emit_engine("gpsimd", e)

            @block.sync
            def _(e):
                emit_engine("sync", e)
        for cm in reversed(ctxs):
            cm.__exit__(None, None, None)


SP_C = 0
SP_BMOD = 32
SP_GAIN = 320
SP_MU = 448
SP_W0 = 500
SP_A0 = 516
SP_KK = 532
SP_KA = 548
SP_LNW = 564
SP_LNB = 580
SP_RK = 596
SP_FON = 612
SP_BF = 628
NSP = 640


def build_nc(S, DFF, stop_after=None):
    NTG = S // TT
    NCH = S // 128
    NMF = DFF // 128
    NGU = DFF // 256
    assert S % TT == 0 and DFF % 256 == 0
    nc = bass.Bass("TRN2", target_bir_lowering=False)

    def din(name, shape, dt=F32):
        return nc.dram_tensor(name, list(shape), dt, kind="ExternalInput").ap()

    def dscr(name, shape, dt=F32):
        return nc.dram_tensor(name, list(shape), dt, kind="Internal").ap()

    x_in = din("x", [S, D])
    sp_in = din("sp", [128, NSP])
    wmod = din("wmod", [72, 128, 32, 512])
    wgu = [din(f"wgu{i}", [NGU, 128, 32, 512]) for i in (1, 2)]
    wdn = [din(f"wdn{i}", [8, 128, NMF, 512]) for i in (1, 2)]
    wina = din("wina", [NTA // 4, 128, 32, 512])
    winb = din("winb", [4, 128, 32, 512])
    wo_in = din("wo", [8, 128, 32, 512])
    w2_in = din("w2", [LORA, 2048])
    a2_in = din("a2", [LORA, 2048])
    g2_in = din("g2", [2, 128, 2048])
    out_d = nc.dram_tensor("out", [S, D], F32, kind="ExternalOutput").ap()

    xT = dscr("xT", [NFT, 128, S])
    pT = dscr("pT", [NTA, 128, S])
    fvS = dscr("fvS", [NCH, 128, 2048], BF16)
    yT = (nc.dram_tensor("yT", [NFT, 128, S], BF16, kind="ExternalOutput").ap() if DEBUG_OUT else dscr("yT", [NFT, 128, S], BF16))

    es = ExitStack()
    ARW = 53200
    arena = es.enter_context(nc.sbuf_tensor("arena", [128, ARW], F32))
    banks = [es.enter_context(nc.psum_tensor(f"bk{i}", [128, 512], F32)) for i in range(8)]
    P = Prog(nc)
    _off = [0]

    def alloc(n):
        o = _off[0]
        _off[0] += n
        assert _off[0] <= ARW, _off[0]
        return o

    def V(off, n, p0=0, p1=128):
        return arena[p0:p1, off:off + n]

    def VB(off, nb, p0=0, p1=128):
        return arena[p0:p1, off:off + nb // 2].bitcast(BF16)

    o_ident = alloc(128); ident = V(o_ident, 128)
    o_ones = alloc(128); ones = V(o_ones, 128)
    o_blk = alloc(128); blk = V(o_blk, 128)
    o_mcat = alloc(512); mcat = V(o_mcat, 512)
    o_msT = alloc(128); msT = V(o_msT, 128)
    o_sp = alloc(NSP); sp = V(o_sp, NSP)
    o_mod = alloc(288); modT = V(o_mod, 288)
    o_A = alloc(96); Acoef = V(o_A, 96)
    o_hg = alloc(96); hgate = V(o_hg, 96)
    o_onesb = alloc(64); onesb = VB(o_onesb, 128)
    o_mib = alloc(64); mib = VB(o_mib, 128)
    o_sel = alloc(16 * 64); selb = VB(o_sel, 16 * 128)
    o_idb = alloc(64); identb = VB(o_idb, 128)
    o_nmb = alloc(64); negmb = VB(o_nmb, 128)
    o_HT = alloc(8192)
    o_ACT = alloc(22016)
    o_WS = alloc(12288)
    o_XS = alloc(2048)
    o_XO = alloc(2048)
    o_MISC = alloc(2560)
    HT = VB(o_HT, 16384)
    ACTT = VB(o_ACT, 44032)
    WSs = [VB(o_WS + 4096 * s, 8192) for s in range(3)]
    XS = [V(o_XS + 512 * i, 512) for i in range(4)]
    XO = [V(o_XO + 512 * i, 512) for i in range(4)]
    RSTD = V(o_MISC, 512)
    TMPA = [V(o_MISC + 512 + 512 * i, 512) for i in range(2)]
    SQ = [V(o_MISC + 1536 + 512 * i, 512) for i in range(2)]

    cnts = dict(ws=0, xs=0, xo=0, sq=0, tmpa=0, grp=0, alt=0)

    def alt_eng():
        cnts["alt"] += 1
        return "vector" if cnts["alt"] % 2 else "scalar"

    def copy_op(eng, out, in_):
        if eng == "scalar":
            return lambda e: e.copy(out=out, in_=in_)
        return lambda e: e.tensor_copy(out=out, in_=in_)

    P.dma("sync", sp, sp_in, writes=["sp"], semkey="const0")
    P.op("gpsimd", lambda e: e.memset(ident, 0.0), writes=["ident"])
    P.op("gpsimd", lambda e: e.affine_select(out=ident, in_=ident, pattern=[[-1, 128]], compare_op=ALU.not_equal,
                                             fill=1.0, base=0, channel_multiplier=1), reads=["ident"], writes=["ident"])
    P.op("gpsimd", lambda e: e.memset(ones, 1.0), writes=["ones"])
    P.op("gpsimd", lambda e: e.memset(onesb, 1.0), writes=["onesb"])
    P.op("gpsimd", lambda e: e.memset(blk, 0.0), writes=["blk"])
    P.op("gpsimd", lambda e: e.memset(blk[0:64, 0:64], 1.0), reads=["blk"], writes=["blk"])
    P.op("gpsimd", lambda e: e.memset(blk[64:128, 64:128], 1.0), reads=["blk"], writes=["blk"])
    P.op("gpsimd", lambda e: e.memset(mcat, 1.0), writes=["mcat"])
    for q in range(4):
        cmp_ = ALU.is_gt if q % 2 == 0 else ALU.is_ge
        P.op("gpsimd", lambda e, q=q, cmp_=cmp_: e.affine_select(
            out=mcat[:, q * 128:(q + 1) * 128], in_=mcat[:, q * 128:(q + 1) * 128], pattern=[[1, 128]],
            compare_op=cmp_, fill=0.0, base=0, channel_multiplier=-1), reads=["mcat"], writes=["mcat"])
    P.op("gpsimd", lambda e: e.memset(msT, 1.0), writes=["msT"])
    P.op("gpsimd", lambda e: e.affine_select(out=msT, in_=msT, pattern=[[-1, 128]], compare_op=ALU.is_gt,
                                             fill=0.0, base=0, channel_multiplier=1), reads=["msT"], writes=["msT"])
    P.op("gpsimd", lambda e: e.tensor_copy(out=mib, in_=mcat[:, 128:256]), reads=["mcat"], writes=["mib"])
    P.op("gpsimd", lambda e: e.tensor_copy(out=identb, in_=ident), reads=["ident"], writes=["identb"])
    P.op("gpsimd", lambda e: e.tensor_scalar(out=negmb, in0=mcat[:, 128:256], scalar1=30000.0, scalar2=-30000.0, op0=ALU.mult, op1=ALU.add),
         reads=["mcat"], writes=["negmb"])
    for q in range(4):
        P.op("gpsimd", lambda e, q=q: e.tensor_tensor(out=mcat[:, q * 128:(q + 1) * 128], in0=mcat[:, q * 128:(q + 1) * 128], in1=blk, op=ALU.mult),
             reads=["mcat", "blk", "mib", "negmb"], writes=["mcat"])
    P.op("gpsimd", lambda e: e.tensor_tensor(out=msT, in0=msT, in1=blk, op=ALU.mult), reads=["msT", "blk"], writes=["msT"])
    P.op("gpsimd", lambda e: e.memset(selb, 0.0), writes=["selb"])
    inv_s = float(128.0 ** 0.5)
    P.op("vector", lambda e: e.tensor_scalar_mul(out=selb[0:16, :].rearrange("p (h k) -> p h k", k=128),
                                                 in0=ident[0:16, 0:16].unsqueeze(2).to_broadcast([16, 16, 128]), scalar1=inv_s),
         reads=["selb", "ident"], writes=["selb"])

    def linear(Wt, G, KT, ntiles, form, act, act_key, evac, nsb=4, ktu=16, ncols=TT):
        units = []
        for g in range(G):
            for k0 in range(0, KT, ktu):
                units.append((g, k0, min(KT, k0 + ktu)))

        def issue(u):
            g, k0, k1 = units[u]
            s = cnts["ws"] % 3
            cnts["ws"] += 1
            dst = WSs[s][:, 0:(k1 - k0) * 512].rearrange("p (k c) -> p k c", c=512)
            P.dma("gpsimd", dst, Wt[g, :, k0:k1, :], writes=[f"ws{s}"], semkey=f"ws{s}")
            return s
        slots = {}
        for u in range(min(2, len(units))):
            slots[u] = issue(u)
        for u, (g, k0, k1) in enumerate(units):
            if u + 2 < len(units):
                slots[u + 2] = issue(u + 2)
            s = slots.pop(u)
            if k0 == 0:
                bset = cnts["grp"] % 2
                cnts["grp"] += 1
            bk = [4 * bset + j for j in range(4)]
            for kt in range(k0, k1):
                kl = kt - k0
                first, last = (kt == 0), (kt == KT - 1)
                if form == "a":
                    for j in range(ntiles(g)):
                        P.op("tensor", lambda e, b=bk[j], s=s, kl=kl, j=j, kt=kt, first=first, last=last:
                             e.matmul(banks[b][:, 0:ncols], lhsT=WSs[s][:, kl * 512 + j * 128: kl * 512 + (j + 1) * 128],
                                      rhs=act(kt), start=first, stop=last),
                             reads=[f"ws{s}", act_key(kt)], writes=[f"bk{bk[j]}"])
                else:
                    for sb in range(nsb):
                        P.op("tensor", lambda e, b=bk[sb], s=s, kl=kl, sb=sb, kt=kt, first=first, last=last:
                             e.matmul(banks[b][:, 0:512], lhsT=act(kt, sb), rhs=WSs[s][:, kl * 512:(kl + 1) * 512],
                                      start=first, stop=last),
                             reads=[f"ws{s}", act_key(kt)], writes=[f"bk{bk[sb]}"])
            if k1 == KT:
                evac(g, bk)

    o_scf = o_ACT
    scf = V(o_scf, 32)
    screp = VB(o_scf + 64, 4096)
    P.op("scalar", lambda e: e.activation(out=scf, in_=sp[:, SP_C:SP_C + 32], func=AF.Silu), reads=["sp"], writes=["scf"])
    for kt in range(32):
        P.op("vector", lambda e, kt=kt: e.tensor_copy(out=screp[:, kt * 128:(kt + 1) * 128],
                                                       in_=scf[:, kt:kt + 1].to_broadcast([128, 128])),
             reads=["scf"], writes=["screp"])

    def mod_evac(g, bk):
        t = TMPA[cnts["tmpa"] % 2]
        tk = f"tmpa{cnts['tmpa'] % 2}"
        cnts["tmpa"] += 1
        b = bk[0]
        P.op("vector", lambda e: e.tensor_tensor(out=t.rearrange("p (a b) -> p a b", b=128),
                                                 in0=banks[b][:, :].rearrange("p (a b) -> p a b", b=128),
                                                 in1=ident.unsqueeze(1).to_broadcast([128, 4, 128]), op=ALU.mult),
             reads=[f"bk{b}", "ident"], writes=[tk])
        P.op("vector", lambda e: e.tensor_reduce(out=modT[:, 4 * g:4 * g + 4], in_=t.rearrange("p (a b) -> p a b", b=128),
                                                 axis=AX.X, op=ALU.add), reads=[tk], writes=["modT"])
    linear(wmod, 72, 32, None, "b", lambda kt, sb: screp[:, kt * 128:(kt + 1) * 128], lambda kt: "screp", mod_evac, nsb=1)
    P.op("vector", lambda e: e.tensor_add(out=modT, in0=modT, in1=sp[:, SP_BMOD:SP_BMOD + 288]), reads=["modT", "sp"], writes=["modT"])
    for n_ in range(3):
        gcol = SP_GAIN + 32 * n_
        P.op("vector", lambda e, n_=n_, gcol=gcol: e.scalar_tensor_tensor(
            out=Acoef[:, 32 * n_:32 * n_ + 32], in0=modT[:, 96 * n_ + 32:96 * n_ + 64], scalar=1.0,
            in1=sp[:, gcol:gcol + 32], op0=ALU.add, op1=ALU.mult), reads=["modT", "sp"], writes=["Acoef"])
        P.op("vector", lambda e, n_=n_: e.tensor_scalar_mul(out=hgate[:, 32 * n_:32 * n_ + 32],
                                                           in0=modT[:, 96 * n_ + 64:96 * n_ + 96],
                                                           scalar1=(1.0 if n_ == 1 else 0.5)),
             reads=["modT"], writes=["hgate"])
    P.barrier()

    XIN = [V(o_ACT + 4096 * s_, 4096) for s_ in range(4)]
    for tg in range(NTG):
        for s_ in range(4):
            r0 = tg * TT + s_ * 128
            P.dma("sync", XIN[s_], x_in[r0:r0 + 128, :], writes=[f"xin{s_}"], semkey=f"xin{s_}")
        for ft in range(NFT):
            b = ft % 8
            for s_ in range(4):
                P.op("tensor", lambda e, b=b, s_=s_, ft=ft: e.transpose(banks[b][:, s_ * 128:(s_ + 1) * 128],
                                                                      XIN[s_][:, ft * 128:(ft + 1) * 128], ident),
                     reads=[f"xin{s_}", "ident"], writes=[f"bk{b}"])
            i = cnts["xo"] % 4
            cnts["xo"] += 1
            P.op(alt_eng(), copy_op("vector" if cnts["alt"] % 2 else "scalar", XO[i], banks[b][:, :]),
                 reads=[f"bk{b}"], writes=[f"xo{i}"])
            P.dma("sync", xT[ft, :, tg * TT:(tg + 1) * TT], XO[i], reads=[f"xo{i}"], writes=[f"xT{ft}_{tg}"], semkey=f"xo{i}")
    P.barrier()

    def load_x(ft, tg):
        i = cnts["xs"] % 4
        cnts["xs"] += 1
        P.dma("sync", XS[i], xT[ft, :, tg * TT:(tg + 1) * TT], reads=[f"xT{ft}_{tg}"], writes=[f"xs{i}"], semkey=f"xs{i}")
        return i

    def sumsq_rstd(tg, bank, src=None):
        for ft in range(NFT):
            i = load_x(ft, tg)
            q = cnts["sq"] % 2
            cnts["sq"] += 1
            P.op("scalar", lambda e, i=i, q=q: e.activation(out=SQ[q], in_=XS[i], func=AF.Square),
                 reads=[f"xs{i}"], writes=[f"sq{q}"])
            P.op("tensor", lambda e, q=q, ft=ft: e.matmul(banks[bank][:, :], lhsT=ones, rhs=SQ[q], start=(ft == 0), stop=(ft == NFT - 1)),
                 reads=[f"sq{q}", "ones"], writes=[f"bk{bank}"])
        P.op("scalar", lambda e: e.activation(out=RSTD, in_=banks[bank][:, :], func=AF.Sqrt, bias=1e-6, scale=1.0 / D),
             reads=[f"bk{bank}"], writes=["rstd"])
        P.op("vector", lambda e: e.reciprocal(out=RSTD, in_=RSTD), reads=["rstd"], writes=["rstd"])

    def norm_to_HT(tg, n_):
        sumsq_rstd(tg, 0)
        for ft in range(NFT):
            i = load_x(ft, tg)
            t = cnts["tmpa"] % 2
            cnts["tmpa"] += 1
            P.op("vector", lambda e, i=i, t=t: e.tensor_tensor(out=TMPA[t], in0=XS[i], in1=RSTD, op=ALU.mult),
                 reads=[f"xs{i}", "rstd"], writes=[f"tmpa{t}"])
            P.op("scalar", lambda e, t=t, ft=ft: e.activation(out=HT[:, ft * TT:(ft + 1) * TT], in_=TMPA[t], func=AF.Identity,
                                                             bias=modT[:, 96 * n_ + ft:96 * n_ + ft + 1],
                                                             scale=Acoef[:, 32 * n_ + ft:32 * n_ + ft + 1]),
                 reads=[f"tmpa{t}", "modT", "Acoef"], writes=[f"ht{ft}"])

    def resid_evac(tg, gcol_base):
        def ev(g, bk):
            for j in range(4):
                ft = 4 * g + j
                i = load_x(ft, tg)
                o = cnts["xo"] % 4
                cnts["xo"] += 1
                P.op("vector", lambda e, b=bk[j], i=i, o=o, ft=ft: e.scalar_tensor_tensor(
                    out=XO[o], in0=banks[b][:, :], scalar=hgate[:, gcol_base + ft:gcol_base + ft + 1], in1=XS[i],
                    op0=ALU.mult, op1=ALU.add), reads=[f"bk{bk[j]}", f"xs{i}", "hgate"], writes=[f"xo{o}"])
                P.dma("sync", xT[ft, :, tg * TT:(tg + 1) * TT], XO[o], reads=[f"xo{o}"], writes=[f"xT{ft}_{tg}"], semkey=f"xo{o}")
        return ev

    def ffn(tg, which, n_):
        norm_to_HT(tg, n_)

        def gu_evac(g, bk):
            for m in range(2):
                t = cnts["tmpa"] % 2
                cnts["tmpa"] += 1
                mt = 2 * g + m
                P.op("scalar", lambda e, b=bk[m], t=t: e.activation(out=TMPA[t], in_=banks[b][:, :], func=AF.Silu),
                     reads=[f"bk{bk[m]}"], writes=[f"tmpa{t}"])
                P.op("vector", lambda e, b=bk[2 + m], t=t, mt=mt: e.tensor_tensor(
                    out=ACTT[:, mt * TT:(mt + 1) * TT], in0=TMPA[t], in1=banks[b][:, :], op=ALU.mult),
                    reads=[f"tmpa{t}", f"bk{bk[2 + m]}"], writes=[f"act{mt}"])
        linear(wgu[which], NGU, 32, lambda g: 4, "a", lambda kt: HT[:, kt * TT:(kt + 1) * TT], lambda kt: f"ht{kt}", gu_evac)
        linear(wdn[which], 8, NMF, lambda g: 4, "a", lambda kt: ACTT[:, kt * TT:(kt + 1) * TT], lambda kt: f"act{kt}",
               resid_evac(tg, 32 * n_))


    def final_phase():
        OUTROW = V(o_ACT, 16384).rearrange("p (s f) -> p s f", f=4096)
        for tg in range(NTG):
            sumsq_rstd(tg, 0)
            for ft in range(NFT):
                i = load_x(ft, tg)
                t = cnts["tmpa"] % 2
                cnts["tmpa"] += 1
                P.op("vector", lambda e, i=i, t=t, ft=ft: e.scalar_tensor_tensor(
                    out=TMPA[t], in0=XS[i], scalar=sp[:, SP_GAIN + 96 + ft:SP_GAIN + 97 + ft], in1=RSTD,
                    op0=ALU.mult, op1=ALU.mult), reads=[f"xs{i}", "rstd", "sp"], writes=[f"tmpa{t}"])
                b = 4 + ft % 4
                for s_ in range(4):
                    P.op("tensor", lambda e, b=b, s_=s_, t=t: e.transpose(banks[b][:, s_ * 128:(s_ + 1) * 128],
                                                                        TMPA[t][:, s_ * 128:(s_ + 1) * 128], ident),
                         reads=[f"tmpa{t}", "ident"], writes=[f"bk{b}"])
                eng = alt_eng()
                P.op(eng, copy_op(eng, OUTROW[:, :, ft * 128:(ft + 1) * 128],
                                  banks[b][:, :].rearrange("p (s f) -> p s f", f=128)),
                     reads=[f"bk{b}"], writes=["outrow"])
            for s_ in range(4):
                r0 = tg * TT + s_ * 128
                P.dma("sync", out_d[r0:r0 + 128, :], OUTROW[:, s_, :], reads=["outrow"], writes=[f"out{s_}"], semkey=f"outd{s_}")
        P.emit(final_keys=[f"out{s_}" for s_ in range(4)])
        es.close()
        if VERBOSE:
            print("PROG stats", P.stats, "nsem", P.nsem, "maxcnt", P.maxcnt, flush=True)
        return nc

    def inproj(tg):
        norm_to_HT(tg, 1)

        def a_evac(g, bk):
            for j in range(4):
                tile = 4 * g + j
                if tile >= 85:
                    continue
                o = cnts["xo"] % 4
                cnts["xo"] += 1
                eng = alt_eng()
                P.op(eng, copy_op(eng, XO[o], banks[bk[j]][:, :]), reads=[f"bk{bk[j]}"], writes=[f"xo{o}"])
                P.dma("sync", pT[tile, :, tg * TT:(tg + 1) * TT], XO[o], reads=[f"xo{o}"], writes=[f"pT{tile}"], semkey=f"xo{o}")
        linear(wina, NTA // 4, 32, lambda g: (4 if g < 21 else 1), "a", lambda kt: HT[:, kt * TT:(kt + 1) * TT],
               lambda kt: f"ht{kt}", a_evac)

        def b_evac(g, bk):
            for sb in range(4):
                t = cnts["tmpa"] % 2
                cnts["tmpa"] += 1
                stg = VB(o_MISC + 512 + 512 * t, 512)
                eng = alt_eng()
                P.op(eng, copy_op(eng, stg, banks[bk[sb]][:, :]), reads=[f"bk{bk[sb]}"], writes=[f"tmpa{t}"])
                P.dma("sync", fvS[tg * 4 + sb, :, g * 512:(g + 1) * 512], stg, reads=[f"tmpa{t}"], writes=["fvS"], semkey=f"tmpa{t}")
        linear(winb, 4, 32, None, "b", lambda kt, sb: HT[:, kt * TT + sb * 128: kt * TT + (sb + 1) * 128],
               lambda kt: f"ht{kt}", b_evac)

    def rwkv_phase():
        nbuf = 30208 // S
        assert nbuf >= 14
        def B(i, p0=0, p1=128):
            return V(o_HT + S * i, S, p0, p1)
        oc = o_WS
        W2B = VB(oc, 2048)
        A2B = VB(oc + 1024, 2048)
        G2B = VB(oc + 2048, 4096)
        TH = VB(oc + 4096, S)
        ADM = VB(oc + 4096 + S // 2, S)
        SG = VB(oc + 4096 + S, 2 * S)
        GTm = V(oc + 4096 + 2 * S, 2 * NCH)
        OMMU = V(oc + 4096 + 2 * S + 32, 52)
        OMKA = V(oc + 4096 + 2 * S + 84, 16)
        ob = oc + 4096 + 2 * S + 112
        M4s = [V(ob + 512 * e, 512) for e in range(2)]
        XA = [V(ob + 1024 + 512 * q, 512) for q in range(2)]
        PTs = [V(ob + 2048 + 256 * q, 256) for q in range(2)]
        NS = V(ob + 2560, 256)
        WSB = V(ob + 2816, 128)
        LU = V(ob + 2944, 192)
        Zp = [V(ob + 3136 + 128 * q, 128) for q in range(2)]
        MST2 = V(ob + 3392, 256)
        assert ob + 3648 <= o_WS + 12288, (ob + 3648, o_WS + 12288)
        c60 = 0.6065306597126334

        P.cur_level = 1
        P.op("vector", lambda e: e.memset(VB(oc, 4096), 0.0), writes=["w2b", "a2b"])
        P.op("vector", lambda e: e.memset(VB(oc + 4096, 2 * S), 0.0), writes=["th", "adm"])
        P.dma("gpsimd", W2B[0:LORA, :], w2_in, reads=["w2b"], writes=["w2b"], semkey="constr")
        P.dma("gpsimd", A2B[0:LORA, :], a2_in, reads=["a2b"], writes=["a2b"], semkey="constr")
        P.dma("gpsimd", G2B.rearrange("p (k c) -> p k c", c=2048), g2_in.rearrange("k p c -> p k c"), writes=["g2b"], semkey="constr")
        P.op("vector", lambda e: e.tensor_scalar(out=OMMU, in0=sp[:, SP_MU:SP_MU + 52], scalar1=-1.0, scalar2=1.0,
                                                 op0=ALU.mult, op1=ALU.add), reads=["sp"], writes=["ommu"])
        P.op("vector", lambda e: e.tensor_scalar(out=OMKA, in0=sp[:, SP_KA:SP_KA + 16], scalar1=-1.0, scalar2=1.0,
                                                 op0=ALU.mult, op1=ALU.add), reads=["sp"], writes=["omka"])
        P.op("gpsimd", lambda e: e.memset(LU, 0.0), writes=["lu"])
        P.op("vector", lambda e: e.tensor_copy(out=MST2[:, 0:128], in_=msT), reads=["msT"], writes=["mst2"])
        P.op("vector", lambda e: e.tensor_copy(out=MST2[:, 128:256], in_=msT), reads=["msT", "mst2"], writes=["mst2"])

        P.cur_level = 2
        def mix(dst, dk, src, sk, tmp, tk, mcol, rows=128):
            mu = sp[0:rows, SP_MU + mcol:SP_MU + mcol + 1]
            om = OMMU[0:rows, mcol:mcol + 1]
            P.op("gpsimd", lambda e: e.memset(tmp[0:rows, 0:1], 0.0), writes=[tk])
            P.op("scalar", lambda e: e.mul(out=tmp[0:rows, 1:S], in_=src[0:rows, 0:S - 1], mul=mu), reads=[sk, "sp", tk], writes=[tk])
            P.op("vector", lambda e: e.scalar_tensor_tensor(out=dst[0:rows, :], in0=src[0:rows, :], scalar=om, in1=tmp[0:rows, :],
                                                            op0=ALU.mult, op1=ALU.add), reads=[sk, tk, "ommu"], writes=[dk])

        P.dma("sync", B(0, 0, LORA), pT[48, 0:LORA, :], reads=["pT48"], writes=["b0"], semkey="rb0")
        mix(B(1), "b1", B(0), "b0", B(2), "b2", 48, LORA)
        P.op("scalar", lambda e: e.activation(out=TH[0:LORA, :], in_=B(1, 0, LORA), func=AF.Tanh), reads=["b1", "th"], writes=["th"])
        P.dma("sync", B(3, 0, LORA), pT[49, 0:LORA, :], reads=["pT49"], writes=["b3"], semkey="rb3")
        mix(B(4), "b4", B(3), "b3", B(5), "b5", 49, LORA)
        P.op("vector", lambda e: e.tensor_copy(out=ADM[0:LORA, :], in_=B(4, 0, LORA)), reads=["b4", "adm"], writes=["adm"])
        for kt in range(2):
            P.dma("sync", B(6 + 3 * kt), pT[50 + kt, :, :], reads=[f"pT{50 + kt}"], writes=[f"b{6 + 3 * kt}"], semkey=f"rb{6 + 3 * kt}")
            mix(B(7 + 3 * kt), f"b{7 + 3 * kt}", B(6 + 3 * kt), f"b{6 + 3 * kt}", B(8 + 3 * kt), f"b{8 + 3 * kt}", 50 + kt)
            P.op("scalar", lambda e, kt=kt: e.activation(out=SG[:, kt * S:(kt + 1) * S], in_=B(7 + 3 * kt), func=AF.Sigmoid),
                 reads=[f"b{7 + 3 * kt}"], writes=["sg"])

        def spcol(base, hp, rows=128):
            return sp[0:rows, base + hp:base + hp + 1]

        for hp in range(NHP):
            R_, K_, V_, SIG, A_, KKN, E1, E2 = B(0), B(1), B(2), B(3), B(4), B(5), B(6), B(7)
            AT, BT, KTf, RT, G_, BON = B(8), B(9), B(10), B(11), B(12), B(13)
            hs = slice(hp * 128, (hp + 1) * 128)
            P.cur_level = 3
            for n_, (raw, dst, tmpb) in enumerate([(8, 0, 11), (9, 1, 12), (10, 2, 13)]):
                P.dma("sync", B(raw), pT[16 * n_ + hp, :, :], reads=[f"pT{16 * n_ + hp}"], writes=[f"b{raw}"], semkey=f"rb{raw}")
                mix(B(dst), f"b{dst}", B(raw), f"b{raw}", B(tmpb), f"b{tmpb}", 16 * n_ + hp)
            P.cur_level = 4
            for tg in range(NTG):
                ts_ = slice(tg * TT, (tg + 1) * TT)
                b = tg % 4
                P.op("tensor", lambda e, b=b, ts_=ts_, hs=hs: e.matmul(banks[b][:, :], lhsT=W2B[:, hs], rhs=TH[:, ts_], start=True, stop=True),
                     reads=["w2b", "th"], writes=[f"bk{b}"])
                P.op("scalar", lambda e, b=b, ts_=ts_, hp=hp: e.activation(out=SIG[:, ts_], in_=banks[b][:, :], func=AF.Sigmoid,
                                                                         bias=spcol(SP_W0, hp), scale=1.0),
                     reads=[f"bk{b}", "sp"], writes=["b3"])
                b = 4 + tg % 4
                P.op("tensor", lambda e, b=b, ts_=ts_, hs=hs: e.matmul(banks[b][:, :], lhsT=A2B[:, hs], rhs=ADM[:, ts_], start=True, stop=True),
                     reads=["a2b", "adm"], writes=[f"bk{b}"])
                P.op("scalar", lambda e, b=b, ts_=ts_, hp=hp: e.activation(out=A_[:, ts_], in_=banks[b][:, :], func=AF.Sigmoid,
                                                                         bias=spcol(SP_A0, hp), scale=1.0),
                     reads=[f"bk{b}", "sp"], writes=["b4"])
            CUMS = B(8)
            P.cur_level = 5
            for c in range(2 * NCH):
                cs = slice(c * 64, (c + 1) * 64)
                P.op("vector", lambda e, cs=cs: e.tensor_tensor_scan(out=CUMS[:, cs], data0=ones[:, 0:64], data1=SIG[:, cs], initial=0.0,
                                                                    op0=ALU.mult, op1=ALU.add), reads=["b3", "ones"], writes=["b8"])
            P.op("scalar", lambda e: e.activation(out=E1, in_=CUMS, func=AF.Exp, scale=-c60), reads=["b8"], writes=["b6"])
            P.op("scalar", lambda e: e.activation(out=E2, in_=CUMS, func=AF.Exp, scale=c60), reads=["b8"], writes=["b7"])
            E3 = B(9)
            P.op("vector", lambda e: e.tensor_sub(out=E3, in0=CUMS, in1=SIG), reads=["b8", "b3"], writes=["b9"])
            P.op("scalar", lambda e: e.activation(out=E3, in_=E3, func=AF.Exp, scale=-c60), reads=["b9"], writes=["b9"])
            P.op("vector", lambda e: e.tensor_copy(out=GTm, in_=E1.rearrange("p (c t) -> p c t", t=64)[:, :, 63]), reads=["b6"], writes=["gt"])
            KKr = B(10)
            P.cur_level = 6
            P.op("scalar", lambda e, hp=hp: e.mul(out=KKr, in_=K_, mul=spcol(SP_KK, hp)), reads=["b1", "sp"], writes=["b10"])
            P.op("scalar", lambda e: e.activation(out=BON, in_=KKr, func=AF.Square), reads=["b10"], writes=["b13"])
            for tg in range(NTG):
                ts_ = slice(tg * TT, (tg + 1) * TT)
                b = tg % 4
                P.op("tensor", lambda e, b=b, ts_=ts_: e.matmul(banks[b][:, :], lhsT=blk, rhs=BON[:, ts_], start=True, stop=True),
                     reads=["blk", "b13"], writes=[f"bk{b}"])
                P.op("scalar", lambda e, b=b, ts_=ts_: e.activation(out=KKN[:, ts_], in_=banks[b][:, :], func=AF.Sqrt, bias=1e-24, scale=1.0),
                     reads=[f"bk{b}"], writes=["b5"])
            P.op("vector", lambda e: e.reciprocal(out=KKN, in_=KKN), reads=["b5"], writes=["b5"])
            P.op("vector", lambda e: e.tensor_tensor(out=KKN, in0=KKr, in1=KKN, op=ALU.mult), reads=["b10", "b5"], writes=["b5"])
            P.cur_level = 7
            P.op("vector", lambda e: e.scalar_tensor_tensor(out=AT, in0=KKN, scalar=-1.0, in1=E3, op0=ALU.mult, op1=ALU.mult),
                 reads=["b5", "b9"], writes=["b8"])
            P.op("vector", lambda e: e.tensor_tensor(out=KKr, in0=KKN, in1=A_, op=ALU.mult), reads=["b5", "b4"], writes=["b10"])
            P.op("vector", lambda e: e.tensor_tensor(out=BT, in0=KKr, in1=E2, op=ALU.mult), reads=["b10", "b7", "b8"], writes=["b9"])
            P.op("vector", lambda e, hp=hp: e.tensor_scalar(out=BON, in0=A_, scalar1=spcol(SP_KA, hp), scalar2=OMKA[:, hp:hp + 1],
                                                           op0=ALU.mult, op1=ALU.add), reads=["b4", "sp", "omka"], writes=["b13"])
            P.op("vector", lambda e: e.tensor_tensor(out=K_, in0=K_, in1=BON, op=ALU.mult), reads=["b1", "b13", "b10"], writes=["b1"])
            P.op("vector", lambda e: e.tensor_tensor(out=KTf, in0=K_, in1=E2, op=ALU.mult), reads=["b1", "b7", "b9"], writes=["b10"])
            P.op("vector", lambda e: e.tensor_tensor(out=RT, in0=R_, in1=E1, op=ALU.mult), reads=["b0", "b6"], writes=["b11"])
            P.cur_level = 8
            for tg in range(NTG):
                ts_ = slice(tg * TT, (tg + 1) * TT)
                b = 4 + tg % 4
                for kt in range(2):
                    P.op("tensor", lambda e, b=b, ts_=ts_, kt=kt, hs=hs: e.matmul(
                        banks[b][:, :], lhsT=G2B[:, kt * 2048 + hs.start: kt * 2048 + hs.stop],
                        rhs=SG[:, kt * S + ts_.start: kt * S + ts_.stop], start=(kt == 0), stop=(kt == 1)),
                        reads=["g2b", "sg"], writes=[f"bk{b}"])
                P.op("scalar", lambda e, b=b, ts_=ts_: e.copy(out=G_[:, ts_], in_=banks[b][:, :]), reads=[f"bk{b}"], writes=["b12"])
            P.cur_level = 9
            P.op("vector", lambda e, hp=hp: e.scalar_tensor_tensor(out=BON, in0=R_, scalar=spcol(SP_RK, hp), in1=K_,
                                                                  op0=ALU.mult, op1=ALU.mult), reads=["b0", "b1", "sp"], writes=["b13"])
            for tg in range(NTG):
                ts_ = slice(tg * TT, (tg + 1) * TT)
                b = tg % 4
                P.op("tensor", lambda e, b=b, ts_=ts_: e.matmul(banks[b][:, :], lhsT=blk, rhs=BON[:, ts_], start=True, stop=True),
                     reads=["blk", "b13"], writes=[f"bk{b}"])
                P.op("vector", lambda e, b=b, ts_=ts_: e.tensor_tensor(out=BON[:, ts_], in0=banks[b][:, :], in1=V_[:, ts_], op=ALU.mult),
                     reads=[f"bk{b}", "b2", "b13"], writes=["b13"])
            P.cur_level = 10
            BTOK = B(0).rearrange("p (c k) -> p c k", k=128)
            KTOK = B(1).rearrange("p (c k) -> p c k", k=128)
            LV = V(o_HT + 3 * S, NCH * 192).rearrange("p (c k) -> p c k", k=192)
            P.op("gpsimd", lambda e: e.memset(V(o_HT + 3 * S, NCH * 192), 0.0), reads=["b3", "b4"], writes=["b3", "b4"])
            for c in range(NCH):
                cs = slice(c * 128, (c + 1) * 128)
                b0_ = 3 * (c % 2)
                b1_, b2_ = b0_ + 1, b0_ + 2
                P.op("tensor", lambda e, b=b0_, cs=cs: e.transpose(banks[b][:, 0:128], BT[:, cs], ident), reads=["b9", "ident"], writes=[f"bk{b0_}"])
                P.op("tensor", lambda e, b=b1_, cs=cs: e.transpose(banks[b][:, 0:128], KTf[:, cs], ident), reads=["b10", "ident"], writes=[f"bk{b1_}"])
                P.op("tensor", lambda e, b=b2_, cs=cs: e.transpose(banks[b][:, 0:128], V_[:, cs], ident), reads=["b2", "ident"], writes=[f"bk{b2_}"])
                P.op("vector", lambda e, b=b0_, c=c: e.tensor_copy(out=BTOK[:, c, :], in_=banks[b][:, 0:128]), reads=[f"bk{b0_}", "b11", "b13"], writes=["b0"])
                P.op("scalar", lambda e, b=b1_, c=c: e.copy(out=KTOK[:, c, :], in_=banks[b][:, 0:128]), reads=[f"bk{b1_}", "b10", "b11", "b13"], writes=["b1"])
                P.op("vector", lambda e, b=b2_, c=c: e.tensor_copy(out=LV[:, c, 0:64], in_=banks[b][:, 0:64]), reads=[f"bk{b2_}"], writes=["b3", "b4"])
                P.op("vector", lambda e, b=b2_, c=c: e.tensor_copy(out=LV[:, c, 128:192], in_=banks[b][:, 64:128]), reads=[f"bk{b2_}", "b3", "b4"], writes=["b3", "b4"])
            YOUT = B(5)
            P.cur_level = 11
            P.op("gpsimd", lambda e: e.memset(Zp[0], 0.0), writes=["zp0"])
            zcnt = [0]
            SC = [dict(M4s=M4s, XA=XA, PTs=PTs, NS=NS),
                  dict(M4s=[V(o_XS + 512 * e, 512) for e in range(2)], XA=[V(o_XS + 1024 + 512 * q, 512) for q in range(2)],
                       PTs=[V(o_XS + 2048 + 256 * q, 256) for q in range(2)], NS=V(o_XS + 2560, 256))]

            def tmat_gen(c, st):
                M4c, XAc, PTc, NSc = SC[st]["M4s"], SC[st]["XA"], SC[st]["PTs"], SC[st]["NS"]
                cs = slice(c * 128, (c + 1) * 128)
                for e_ in range(2):
                    ps_ = slice(64 * e_, 64 * e_ + 64)
                    for q, (lk, lb, rk_, rb) in enumerate([(BT, "b9", AT, "b8"), (BT, "b9", RT, "b11"), (KTf, "b10", AT, "b8"), (KTf, "b10", RT, "b11")]):
                        P.op("tensor", lambda e, e_=e_, q=q, lk=lk, rk_=rk_, ps_=ps_: e.matmul(
                            banks[e_][:, q * 128:(q + 1) * 128], lhsT=lk[ps_, cs], rhs=rk_[ps_, cs], start=True, stop=True),
                            reads=[lb, rb], writes=[f"bk{e_}"])
                    P.op("vector", lambda e, e_=e_: e.tensor_tensor(out=M4c[e_], in0=banks[e_][:, :], in1=mcat, op=ALU.mult),
                         reads=[f"bk{e_}", "mcat"], writes=[f"m4{e_}_{st}"])
                    P.op("tensor", lambda e, e_=e_, ps_=ps_: e.matmul(banks[4][:, 256 + e_ * 128:256 + (e_ + 1) * 128],
                                                                     lhsT=AT[ps_, cs], rhs=BT[ps_, cs], start=True, stop=True),
                         reads=["b8", "b9"], writes=["bk4"])
                    yield
                P.op("vector", lambda e: e.tensor_tensor(out=NSc, in0=banks[4][:, 256:512], in1=MST2, op=ALU.mult),
                     reads=["bk4", "mst2"], writes=[f"ns{st}"])
                for e_ in range(2):
                    P.op("gpsimd", lambda e, e_=e_: e.tensor_tensor(out=PTc[0][:, e_ * 128:(e_ + 1) * 128], in0=M4c[e_][:, 0:128], in1=ident, op=ALU.add),
                         reads=[f"m4{e_}_{st}", "ident"], writes=[f"pt0_{st}"])
                yield
                for lv in range(5):
                    qc, qn = lv % 2, (lv + 1) % 2
                    xkeys = [f"ns{st}", f"m40_{st}", f"m41_{st}"] if lv == 0 else [f"xa{qc}_{st}"]
                    for e_ in range(2):
                        if lv == 0:
                            X_, XT_ = NSc[:, e_ * 128:(e_ + 1) * 128], M4c[e_][:, 0:128]
                        else:
                            X_, XT_ = XAc[qc][:, e_ * 256: e_ * 256 + 128], XAc[qc][:, e_ * 256 + 128: e_ * 256 + 256]
                        P.op("tensor", lambda e, e_=e_, X_=X_, XT_=XT_: e.matmul(banks[3][:, e_ * 256: e_ * 256 + 128], lhsT=XT_, rhs=X_, start=True, stop=True),
                             reads=xkeys, writes=["bk3"])
                        if lv < 4:
                            P.op("tensor", lambda e, e_=e_, X_=X_, XT_=XT_: e.matmul(banks[3][:, e_ * 256 + 128: e_ * 256 + 256], lhsT=X_, rhs=XT_, start=True, stop=True),
                                 reads=xkeys, writes=["bk3"])
                    eng = "scalar" if lv % 2 == 0 else "vector"
                    P.op(eng, copy_op(eng, XAc[qn], banks[3][:, :]), reads=["bk3"], writes=[f"xa{qn}_{st}"])
                    yield
                    for e_ in range(2):
                        es_ = slice(e_ * 128, (e_ + 1) * 128)
                        P.op("tensor", lambda e, es_=es_, qc=qc: e.matmul(banks[4][:, es_], lhsT=ident, rhs=PTc[qc][:, es_], start=True, stop=False),
                             reads=["ident", f"pt{qc}_{st}"], writes=["bk4"])
                        P.op("tensor", lambda e, es_=es_, qc=qc, qn=qn, e_=e_: e.matmul(banks[4][:, es_], lhsT=XAc[qn][:, e_ * 256: e_ * 256 + 128],
                                                                                      rhs=PTc[qc][:, es_], start=False, stop=True),
                             reads=[f"xa{qn}_{st}", f"pt{qc}_{st}"], writes=["bk4"])
                    eng = "vector" if lv % 2 == 0 else "scalar"
                    P.op(eng, copy_op(eng, PTc[qn], banks[4][:, 0:256]), reads=["bk4"], writes=[f"pt{qn}_{st}"])
                    yield

            def chain_gen(c, st):
                M4c = SC[st]["M4s"]
                TmT = SC[st]["PTs"][1]
                cs = slice(c * 128, (c + 1) * 128)
                for s_ in range(2):
                    zc, zn = zcnt[0] % 2, (zcnt[0] + 1) % 2
                    zcnt[0] += 1
                    rows = slice(64 * s_, 64 * s_ + 64)
                    tsub = slice(c * 128 + 64 * s_, c * 128 + 64 * s_ + 64)
                    P.op("tensor", lambda e, zc=zc: e.matmul(banks[5][:, 0:128], lhsT=AT[:, cs], rhs=Zp[zc], start=True, stop=False),
                         reads=["b8", f"zp{zc}"], writes=["bk5"])
                    for e_ in range(2):
                        P.op("tensor", lambda e, e_=e_: e.matmul(banks[5][:, e_ * 64:(e_ + 1) * 64], lhsT=M4c[e_][:, 256:384],
                                                                rhs=LV[:, c, e_ * 128: e_ * 128 + 64], start=False, stop=(e_ == 1)),
                             reads=[f"m4{e_}_{st}", "b3", "b4"], writes=["bk5"])
                    P.op("scalar", lambda e: e.copy(out=WSB, in_=banks[5][:, 0:128]), reads=["bk5"], writes=["wsb"])
                    yield
                    for e_ in range(2):
                        P.op("tensor", lambda e, e_=e_: e.matmul(banks[6][:, e_ * 64:(e_ + 1) * 64], lhsT=TmT[:, e_ * 128:(e_ + 1) * 128],
                                                                rhs=WSB[:, e_ * 64:(e_ + 1) * 64], start=True, stop=True),
                             reads=[f"pt1_{st}", "wsb"], writes=["bk6"])
                    P.op("vector", lambda e: e.tensor_copy(out=LU[:, 0:64], in_=banks[6][:, 0:64]), reads=["bk6"], writes=["lu"])
                    P.op("vector", lambda e: e.tensor_copy(out=LU[:, 128:192], in_=banks[6][:, 64:128]), reads=["bk6", "lu"], writes=["lu"])
                    yield
                    P.op("tensor", lambda e, zc=zc: e.matmul(banks[7][:, 0:128], lhsT=ident, rhs=Zp[zc], start=True, stop=False),
                         reads=["ident", f"zp{zc}"], writes=["bk7"])
                    for e_ in range(2):
                        P.op("tensor", lambda e, e_=e_, rows=rows: e.matmul(banks[7][:, e_ * 64:(e_ + 1) * 64], lhsT=BTOK[rows, c, :],
                                                                           rhs=LU[rows, e_ * 128: e_ * 128 + 64], start=False, stop=False),
                             reads=["b0", "lu"], writes=["bk7"])
                    for e_ in range(2):
                        P.op("tensor", lambda e, e_=e_, rows=rows: e.matmul(banks[7][:, e_ * 64:(e_ + 1) * 64], lhsT=KTOK[rows, c, :],
                                                                           rhs=LV[rows, c, e_ * 128: e_ * 128 + 64], start=False, stop=(e_ == 1)),
                             reads=["b1", "b3", "b4"], writes=["bk7"])
                    P.op("vector", lambda e, s_=s_, zn=zn: e.scalar_tensor_tensor(out=Zp[zn], in0=banks[7][:, 0:128],
                                                                                 scalar=GTm[:, 2 * c + s_:2 * c + s_ + 1], in1=blk,
                                                                                 op0=ALU.mult, op1=ALU.mult),
                         reads=["bk7", "gt", "blk"], writes=[f"zp{zn}"])
                    P.op("tensor", lambda e, tsub=tsub, zc=zc: e.matmul(banks[2][:, 0:64], lhsT=Zp[zc], rhs=RT[:, tsub], start=True, stop=False),
                         reads=[f"zp{zc}", "b11"], writes=["bk2"])
                    for e_ in range(2):
                        P.op("tensor", lambda e, e_=e_, s_=s_: e.matmul(banks[2][:, 0:64], lhsT=LU[:, e_ * 64: e_ * 64 + 128],
                                                                       rhs=M4c[e_][:, 128 + 64 * s_:128 + 64 * s_ + 64],
                                                                       start=False, stop=False), reads=["lu", f"m4{e_}_{st}"], writes=["bk2"])
                    for e_ in range(2):
                        P.op("tensor", lambda e, e_=e_, s_=s_: e.matmul(banks[2][:, 0:64], lhsT=LV[:, c, e_ * 64: e_ * 64 + 128],
                                                                       rhs=M4c[e_][:, 384 + 64 * s_:384 + 64 * s_ + 64],
                                                                       start=False, stop=(e_ == 1)), reads=["b3", "b4", f"m4{e_}_{st}"], writes=["bk2"])
                    P.op("scalar", lambda e, tsub=tsub: e.copy(out=YOUT[:, tsub], in_=banks[2][:, 0:64]), reads=["bk2"], writes=["b5"])
                    yield

            def drive(gA, gB, ratio=2):
                doneA = doneB = False
                while not (doneA and doneB):
                    if not doneA:
                        try:
                            next(gA)
                        except StopIteration:
                            doneA = True
                    for _ in range(ratio):
                        if not doneB:
                            try:
                                next(gB)
                            except StopIteration:
                                doneB = True

            chunks = list(range(NCH)) if "c" in parts else []
            if chunks:
                for _ in tmat_gen(0, 0):
                    pass
            for c in chunks:
                gA = chain_gen(c, c % 2)
                gB = tmat_gen(c + 1, (c + 1) % 2) if c + 1 < NCH else iter(())
                drive(gA, gB)
            P.cur_level = 12
            CEN, SQb = B(6), B(7)
            for tg in range(NTG):
                ts_ = slice(tg * TT, (tg + 1) * TT)
                b = tg % 2
                P.op("tensor", lambda e, b=b, ts_=ts_: e.matmul(banks[b][:, :], lhsT=blk, rhs=YOUT[:, ts_], start=True, stop=True),
                     reads=["blk", "b5"], writes=[f"bk{b}"])
                P.op("vector", lambda e, b=b, ts_=ts_: e.scalar_tensor_tensor(out=CEN[:, ts_], in0=banks[b][:, :], scalar=-1.0 / 64, in1=YOUT[:, ts_],
                                                                             op0=ALU.mult, op1=ALU.add), reads=[f"bk{b}", "b5"], writes=["b6"])
                P.op("scalar", lambda e, ts_=ts_: e.activation(out=SQb[:, ts_], in_=CEN[:, ts_], func=AF.Square), reads=["b6"], writes=["b7"])
                b2 = 2 + tg % 2
                P.op("tensor", lambda e, b2=b2, ts_=ts_: e.matmul(banks[b2][:, :], lhsT=blk, rhs=SQb[:, ts_], start=True, stop=True),
                     reads=["blk", "b7"], writes=[f"bk{b2}"])
                P.op("scalar", lambda e, b2=b2, ts_=ts_: e.activation(out=SQb[:, ts_], in_=banks[b2][:, :], func=AF.Sqrt, bias=64e-5, scale=1.0 / 64),
                     reads=[f"bk{b2}", "b7"], writes=["b7"])
                P.op("vector", lambda e, ts_=ts_: e.reciprocal(out=SQb[:, ts_], in_=SQb[:, ts_]), reads=["b7"], writes=["b7"])
                P.op("vector", lambda e, ts_=ts_: e.tensor_tensor(out=CEN[:, ts_], in0=CEN[:, ts_], in1=SQb[:, ts_], op=ALU.mult), reads=["b6", "b7"], writes=["b6"])
                P.op("scalar", lambda e, ts_=ts_, hp=hp: e.activation(out=CEN[:, ts_], in_=CEN[:, ts_], func=AF.Identity,
                                                                    bias=spcol(SP_LNB, hp), scale=spcol(SP_LNW, hp)), reads=["b6", "sp"], writes=["b6"])
                P.op("vector", lambda e, ts_=ts_: e.tensor_tensor(out=CEN[:, ts_], in0=CEN[:, ts_], in1=BON[:, ts_], op=ALU.add), reads=["b6", "b13"], writes=["b6"])
                t = cnts["tmpa"] % 2
                cnts["tmpa"] += 1
                stg = VB(o_MISC + 512 + 512 * t, 512)
                P.op("vector", lambda e, ts_=ts_, stg=stg: e.tensor_tensor(out=stg, in0=CEN[:, ts_], in1=G_[:, ts_], op=ALU.mult),
                     reads=["b6", "b12"], writes=[f"tmpa{t}"])
                P.dma("sync", yT[hp, :, ts_], stg, reads=[f"tmpa{t}"], writes=[f"yT{hp}"], semkey=f"tmpa{t}")

    def fox_phase():
        oa = o_HT
        QT = VB(oa, S); KTb = VB(oa + S // 2, S); VTm = VB(oa + S, S).rearrange("p (c d) -> p c d", d=128)
        CUM = V(oa + 3 * S // 2, S, 0, 16)
        CUMB = VB(oa + 5 * S // 2, S, 0, 16)
        FRAW = V(oa + 3 * S, S, 0, 16)
        NEGC = V(oa + 4 * S, NCH * 16).rearrange("p (c h) -> p c h", h=16)
        ob = oa + 4 * S + NCH * 16
        PTr = [VB(ob + 256 * i, 512) for i in range(3)]
        OS = V(ob + 768, 512); RS = V(ob + 1280, 512); SQf = V(ob + 1792, 512); RN = V(ob + 2304, 512)
        YB = [VB(ob + 2816 + 256 * i, 512) for i in range(2)]
        scale = float(128.0 ** -0.5)
        P.dma("sync", FRAW, pT[84, 0:16, :], reads=["pT84"], writes=["fraw"], semkey="fraw")
        P.op("scalar", lambda e: e.activation(out=FRAW, in_=FRAW, func=AF.Sigmoid, bias=sp[0:16, SP_BF:SP_BF + 1], scale=1.0),
             reads=["fraw", "sp"], writes=["fraw"])
        P.op("scalar", lambda e: e.activation(out=FRAW, in_=FRAW, func=AF.Ln), reads=["fraw"], writes=["fraw"])
        for c in range(NCH):
            cs = slice(c * 128, (c + 1) * 128)
            init = 0.0 if c == 0 else CUM[:, c * 128 - 1:c * 128]
            P.op("vector", lambda e, cs=cs, init=init: e.tensor_tensor_scan(out=CUM[:, cs], data0=ones[0:16, :], data1=FRAW[:, cs], initial=init,
                                                                           op0=ALU.mult, op1=ALU.add), reads=["fraw", "ones", "cum"], writes=["cum"])
        P.op("vector", lambda e: e.tensor_copy(out=CUMB, in_=CUM), reads=["cum"], writes=["cumb"])
        for c in range(NCH):
            b = c % 4
            P.op("tensor", lambda e, b=b, c=c: e.transpose(banks[b][:, 0:16], CUM[:, c * 128:(c + 1) * 128], ident[0:16, 0:16]),
                 reads=["cum", "ident"], writes=[f"bk{b}"])
            P.op("scalar", lambda e, b=b, c=c: e.mul(out=NEGC[:, c, :], in_=banks[b][:, 0:16], mul=-1.0), reads=[f"bk{b}"], writes=["negc"])
        NQG = S // 512
        for h in range(NFH):
            P.dma("gpsimd", QT, pT[52 + h, :, :], reads=[f"pT{52 + h}"], writes=["qt"], semkey="qt")
            P.dma("gpsimd", KTb, pT[68 + h, :, :], reads=[f"pT{68 + h}"], writes=["ktb"], semkey="ktb")
            P.dma("sync", VTm, fvS[:, :, h * 128:(h + 1) * 128].rearrange("c p d -> p c d"), reads=["fvS"], writes=["vtm"], semkey="vtm")
            for qg in range(NQG):
                qs = slice(qg * 512, (qg + 1) * 512)
                nkb = 4 * (qg + 1)
                for j in range(nkb):
                    c0 = (j - 4 * qg) * 128 if j >= 4 * qg else 0
                    sb_ = j % 2
                    pr = cnts["sq"] % 3
                    cnts["sq"] += 1
                    P.op("tensor", lambda e, sb_=sb_, j=j, qs=qs, c0=c0: e.matmul(banks[sb_][:, c0:512], lhsT=KTb[:, j * 128:(j + 1) * 128],
                                                                                 rhs=QT[:, qs.start + c0:qs.stop], start=True, stop=False),
                         reads=["ktb", "qt"], writes=[f"bk{sb_}"])
                    diag = j >= 4 * qg
                    P.op("tensor", lambda e, sb_=sb_, h=h, qs=qs, c0=c0, diag=diag: e.matmul(banks[sb_][:, c0:512], lhsT=selb[0:16, h * 128:(h + 1) * 128],
                                                                                 rhs=CUMB[:, qs.start + c0:qs.stop], start=False, stop=(not diag)),
                         reads=["selb", "cumb"], writes=[f"bk{sb_}"])
                    if diag:
                        P.op("tensor", lambda e, sb_=sb_, c0=c0: e.matmul(banks[sb_][:, c0:c0 + 128], lhsT=identb, rhs=negmb, start=False, stop=True),
                             reads=["identb", "negmb"], writes=[f"bk{sb_}"])
                    P.op("scalar", lambda e, sb_=sb_, pr=pr, j=j, h=h, c0=c0: e.activation(out=PTr[pr][:, c0:512], in_=banks[sb_][:, c0:512], func=AF.Exp,
                                                                                          bias=NEGC[:, j, h:h + 1], scale=scale),
                         reads=[f"bk{sb_}", "negc"], writes=[f"ptr{pr}"])
                    P.op("tensor", lambda e, pr=pr, j=j, c0=c0, nkb=nkb: e.matmul(banks[2][:, c0:512], lhsT=VTm[:, j, :], rhs=PTr[pr][:, c0:512],
                                                                                 start=(j == 0), stop=(j == nkb - 1)),
                         reads=["vtm", f"ptr{pr}"], writes=["bk2"])
                    P.op("tensor", lambda e, pr=pr, j=j, c0=c0, nkb=nkb: e.matmul(banks[3][:, c0:512], lhsT=onesb, rhs=PTr[pr][:, c0:512],
                                                                                 start=(j == 0), stop=(j == nkb - 1)),
                         reads=["onesb", f"ptr{pr}"], writes=["bk3"])
                P.op("vector", lambda e: e.reciprocal(out=RS, in_=banks[3][:, :]), reads=["bk3"], writes=["rs"])
                P.op("vector", lambda e: e.tensor_tensor(out=OS, in0=banks[2][:, :], in1=RS, op=ALU.mult), reads=["bk2", "rs"], writes=["os"])
                P.op("scalar", lambda e: e.activation(out=SQf, in_=OS, func=AF.Square), reads=["os"], writes=["sqf"])
                P.op("tensor", lambda e: e.matmul(banks[4][:, :], lhsT=ones, rhs=SQf, start=True, stop=True), reads=["ones", "sqf"], writes=["bk4"])
                P.op("scalar", lambda e: e.activation(out=RN, in_=banks[4][:, :], func=AF.Sqrt, bias=1e-6, scale=1.0 / 128), reads=["bk4"], writes=["rn"])
                P.op("vector", lambda e: e.reciprocal(out=RN, in_=RN), reads=["rn"], writes=["rn"])
                yb = cnts["tmpa"] % 2
                cnts["tmpa"] += 1
                P.op("vector", lambda e, yb=yb, h=h: e.scalar_tensor_tensor(out=YB[yb], in0=OS, scalar=sp[:, SP_FON + h:SP_FON + h + 1], in1=RN,
                                                                           op0=ALU.mult, op1=ALU.mult), reads=["os", "rn", "sp"], writes=[f"yb{yb}"])
                P.dma("sync", yT[16 + h, :, qs], YB[yb], reads=[f"yb{yb}"], writes=[f"yT{16 + h}"], semkey=f"yb{yb}")

    def outproj(tg):
        for kt in range(NFT):
            P.dma("sync", HT[:, kt * TT:(kt + 1) * TT], yT[kt, :, tg * TT:(tg + 1) * TT], reads=[f"yT{kt}"], writes=[f"ht{kt}"], semkey=f"htl{kt}")
        linear(wo_in, 8, 32, lambda g: 4, "a", lambda kt: HT[:, kt * TT:(kt + 1) * TT], lambda kt: f"ht{kt}", resid_evac(tg, 32))

    for tg in range(NTG):
        ffn(tg, 0, 0)
    P.barrier()
    if stop_after == "ffn1":
        return final_phase()
    for tg in range(NTG):
        inproj(tg)
    P.barrier()
    parts = "rcfo"
    if stop_after == "nomix":
        parts = ""
    elif stop_after is not None and stop_after.startswith("mix:"):
        parts = stop_after[4:]
    if "r" in parts:
        import re as _re
        m_ = _re.search(r"L(\d+)", parts)
        if m_:
            P.max_level = int(m_.group(1))
        rwkv_phase()
        P.cur_level = 0
        P.barrier()
    if "f" in parts:
        fox_phase()
        P.barrier()
    if "o" in parts:
        for tg in range(NTG):
            outproj(tg)
        P.barrier()
    for tg in range(NTG):
        ffn(tg, 1, 2)
    P.barrier()
    return final_phase()


def _tile_w(W, C=512):
    K, M = W.shape
    return np.ascontiguousarray(W.reshape(K // 128, 128, M // C, C).transpose(2, 1, 0, 3))


def _col(v):
    v = np.asarray(v, np.float32).reshape(-1, 128)
    return np.ascontiguousarray(v.T)


def prepare(inputs, S, DFF):
    f32 = np.float32
    g = lambda k: np.asarray(inputs[k], f32)
    B = g("x").shape[0]
    w_in = g("w_in")[0]
    shared = {}
    shared["wmod"] = _tile_w(g("w_mod")[0])
    for i, nm in ((1, "ffn1"), (2, "ffn2")):
        ga, up, dn = g(nm + "_gate")[0], g(nm + "_up")[0], g(nm + "_down")[0]
        K = ga.shape[0]
        gu = np.concatenate([ga.reshape(K, DFF // 256, 256), up.reshape(K, DFF // 256, 256)], axis=2).reshape(K, 2 * DFF)
        shared[f"wgu{i}"] = _tile_w(gu)
        shared[f"wdn{i}"] = _tile_w(dn)
    wa = np.zeros((D, NTA * 128), f32)
    wa[:, 0:6144] = w_in[:, 0:6144]
    wa[:, 48 * 128:48 * 128 + 96] = w_in[:, 6144:6240]
    wa[:, 49 * 128:49 * 128 + 96] = w_in[:, 6240:6336]
    wa[:, 50 * 128:52 * 128] = w_in[:, 6336:6592]
    wa[:, 52 * 128:84 * 128] = w_in[:, 6592:10688]
    wa[:, 84 * 128:84 * 128 + 16] = w_in[:, 12736:12752]
    shared["wina"] = _tile_w(wa)
    shared["winb"] = _tile_w(w_in[:, 10688:12736])
    shared["wo"] = _tile_w(g("w_out")[0])
    shared["w2"] = np.ascontiguousarray(g("rwkv_w2")[0])
    shared["a2"] = np.ascontiguousarray(g("rwkv_a2")[0])
    shared["g2"] = np.ascontiguousarray(g("rwkv_g2")[0].reshape(2, 128, 2048))
    mu = g("rwkv_mu")[0]
    mut = np.zeros((128, 52), f32)
    mut[:, 0:48] = _col(mu[0:6144])
    mut[0:96, 48] = mu[6144:6240]
    mut[0:96, 49] = mu[6240:6336]
    mut[:, 50:52] = _col(mu[6336:6592])
    per = []
    for b in range(B):
        sp = np.zeros((128, NSP), f32)
        sp[:, SP_C:SP_C + 32] = _col(g("c")[b])
        sp[:, SP_BMOD:SP_BMOD + 288] = _col(g("b_mod")[0])
        sp[:, SP_GAIN:SP_GAIN + 32] = _col(g("norm_ffn1")[0])
        sp[:, SP_GAIN + 32:SP_GAIN + 64] = _col(g("norm_mix")[0])
        sp[:, SP_GAIN + 64:SP_GAIN + 96] = _col(g("norm_ffn2")[0])
        sp[:, SP_GAIN + 96:SP_GAIN + 128] = _col(g("norm_final"))
        sp[:, SP_MU:SP_MU + 52] = mut
        sp[:, SP_W0:SP_W0 + 16] = _col(g("rwkv_w0")[0])
        sp[:, SP_A0:SP_A0 + 16] = _col(g("rwkv_a0")[0])
        sp[:, SP_KK:SP_KK + 16] = _col(g("rwkv_k_k")[0])
        sp[:, SP_KA:SP_KA + 16] = _col(g("rwkv_k_a")[0])
        sp[:, SP_LNW:SP_LNW + 16] = _col(g("rwkv_ln_w")[0])
        sp[:, SP_LNB:SP_LNB + 16] = _col(g("rwkv_ln_b")[0])
        sp[:, SP_RK:SP_RK + 16] = _col(g("rwkv_r_k")[0].reshape(-1))
        sp[:, SP_FON:SP_FON + 16] = _col(g("fox_out_norm")[0])
        sp[0:16, SP_BF] = g("fox_b_f")[0]
        d = dict(shared)
        d["x"] = np.ascontiguousarray(g("x")[b])
        d["sp"] = sp
        per.append(d)
    return per


_NC_CACHE = {}
VERBOSE = False
DEBUG_OUT = False
LAST_RES = [None]


def run(inputs, S, DFF, stop_after=None, n_cores=8):
    import time
    t0 = time.time()
    per = prepare(inputs, S, DFF)
    t1 = time.time()
    key = (S, DFF, stop_after)
    if key not in _NC_CACHE:
        _NC_CACHE[key] = build_nc(S, DFF, stop_after)
    nc = _NC_CACHE[key]
    B = len(per)
    in_maps = [per[(i * B) // n_cores] for i in range(n_cores)]
    t2 = time.time()
    res = run_bass_kernel_spmd(nc, in_maps, core_ids=list(range(n_cores)))
    if VERBOSE:
        print("TIMES prepare %.1f build %.1f run %.1f" % (t1 - t0, t2 - t1, time.time() - t2), flush=True)
    step = n_cores // B
    LAST_RES[0] = res
    return np.stack([np.asarray(res.results[b * step]["out"], np.float32) for b in range(B)], axis=0)


def kernel(**inputs):
    S = int(np.asarray(inputs["x"]).shape[1])
    DFF = int(np.asarray(inputs["ffn1_gate"]).shape[2])
    return run(inputs, S, DFF)
```

```python
import numpy as np
from contextlib import ExitStack
import concourse.bass as bass
import concourse.mybir as mybir
from concourse.bass_utils import run_bass_kernel_spmd

F32 = mybir.dt.float32
BF16 = mybir.dt.bfloat16
AF = mybir.ActivationFunctionType
ALU = mybir.AluOpType
AX = mybir.AxisListType

ENGS = ("tensor", "vector", "scalar", "gpsimd", "sync")
D = 4096
NFT = 32
TT = 512
NHP = 16
NFH = 16
LORA = 96
NTA = 88


class Prog:
    def __init__(self, nc):
        self.nc = nc
        self.ops = []
        self.cur_level = 0
        self.max_level = 99

    def op(self, eng, fn, reads=(), writes=(), dma=False, semkey=None):
        if self.cur_level > self.max_level:
            return
        self.ops.append(dict(eng=eng, fn=fn, reads=tuple(reads), writes=tuple(writes),
                             dma=dma, semkey=semkey, bar=False))

    def dma(self, eng, out, in_, reads=(), writes=(), semkey="dma"):
        self.op(eng, lambda e, o=out, i=in_: e.dma_start(out=o, in_=i),
                reads=reads, writes=writes, dma=True, semkey=semkey)

    def barrier(self):
        self.ops.append(dict(bar=True))

    def emit(self, final_keys=()):
        nc = self.nc
        ops = [o for o in self.ops]
        n = len(ops)
        last_w = {}
        readers = {}
        deps = [None] * n
        last_eng = {}
        last_dma = {}
        pend_bar = {}
        for i, o in enumerate(ops):
            if o["bar"]:
                s = set(last_eng.values()) | set(last_dma.values())
                for e in ENGS:
                    pend_bar[e] = set(s) | pend_bar.get(e, set())
                continue
            d = set()
            for k in o["reads"]:
                if k in last_w:
                    d.add(last_w[k])
            for k in o["writes"]:
                if k in last_w:
                    d.add(last_w[k])
                d.update(readers.get(k, ()))
            for k in o["reads"]:
                readers.setdefault(k, []).append(i)
            for k in o["writes"]:
                last_w[k] = i
                readers[k] = []
            if o["eng"] in pend_bar:
                d |= pend_bar.pop(o["eng"])
            d.discard(i)
            deps[i] = d
            if o["dma"]:
                last_dma[o["semkey"]] = i
            else:
                last_eng[o["eng"]] = i
        final_deps = set()
        for k in final_keys:
            if k in last_w:
                final_deps.add(last_w[k])
        needed = set()
        for i, o in enumerate(ops):
            if o["bar"]:
                continue
            best = {}
            for p in deps[i]:
                po = ops[p]
                if po["dma"]:
                    key = ("dma", po["semkey"])
                else:
                    if po["eng"] == o["eng"] and o["eng"] in ("tensor", "sync") and not o["dma"]:
                        continue
                    key = ("eng", po["eng"])
                if key not in best or p > best[key]:
                    best[key] = p
            deps[i] = set(best.values())
            needed |= deps[i]
        needed |= final_deps
        needed |= {i for i, o in enumerate(ops) if not o["bar"] and o["dma"]}
        sems = {}
        ctxs = []
        keys = [("eng", e) for e in ENGS]
        for o in ops:
            if not o["bar"] and o["dma"] and ("dma", o["semkey"]) not in keys:
                keys.append(("dma", o["semkey"]))
        for k in keys:
            cm = nc.semaphore("s_" + "_".join(str(x) for x in k))
            sems[k] = cm.__enter__()
            ctxs.append(cm)
        cnt = {k: 0 for k in sems}
        opsem = [None] * n
        for i, o in enumerate(ops):
            if i in needed:
                sk = ("dma", o["semkey"]) if o["dma"] else ("eng", o["eng"])
                cnt[sk] += 16 if o["dma"] else 1
                opsem[i] = (sk, cnt[sk])
        for i, o in enumerate(ops):
            if opsem[i] is not None and o["dma"] and str(o["semkey"]).startswith("const"):
                opsem[i] = (opsem[i][0], cnt[opsem[i][0]])
        self.nsem = len(sems)
        self.maxcnt = max(cnt.values())
        self.stats = {e: sum(1 for o in ops if not o["bar"] and o["eng"] == e) for e in ENGS}
        per_eng = {e: [] for e in ENGS}
        for i, o in enumerate(ops):
            if not o["bar"]:
                per_eng[o["eng"]].append(i)
        waited = {e: {} for e in ENGS}

        def emit_engine(ename, eng):
            for i in per_eng[ename]:
                o = ops[i]
                need = {}
                for p in deps[i]:
                    sk, v = opsem[p]
                    if v > need.get(sk, 0):
                        need[sk] = v
                for sk, v in need.items():
                    if waited[ename].get(sk, 0) >= v:
                        continue
                    eng.wait_ge(sems[sk], v)
                    waited[ename][sk] = v
                ins = o["fn"](eng)
                if opsem[i] is not None:
                    ins.then_inc(sems[opsem[i][0]], 16 if o["dma"] else 1)
            if ename == "sync":
                need = {}
                for p in final_deps:
                    sk, v = opsem[p]
                    if v > need.get(sk, 0):
                        need[sk] = v
                for sk, v in need.items():
                    eng.wait_ge(sems[sk], v)

        with nc.Block() as block:
            @block.tensor
            def _(e):
                emit_engine("tensor", e)

            @block.vector
            def _(e):
                emit_engine("vector", e)

            @block.scalar
            def _(e):
                emit_engine("scalar", e)

            @block.gpsimd
            def _(e):
                emit_engine("gpsimd", e)

            @block.sync
            def _(e):
                emit_engine("sync", e)
        for cm in reversed(ctxs):
            cm.__exit__(None, None, None)


SP_C = 0
SP_BMOD = 32
SP_GAIN = 320
SP_MU = 448
SP_W0 = 500
SP_A0 = 516
SP_KK = 532
SP_KA = 548
SP_LNW = 564
SP_LNB = 580
SP_RK = 596
SP_FON = 612
SP_BF = 628
SP_PAR = 629
SP_OMP = 630
NSP = 640


def build_nc(S, DFF, stop_after=None, split=False):
    NTG = S // TT
    NCH = S // 128
    NMF = DFF // 128
    NGU = DFF // 256
    assert S % TT == 0 and DFF % 256 == 0
    split = bool(split and NTG % 2 == 0 and stop_after is None)
    NTD = NTG // 2 if split else NTG
    nc = bass.Bass("TRN2", target_bir_lowering=False)

    def din(name, shape, dt=F32):
        return nc.dram_tensor(name, list(shape), dt, kind="ExternalInput").ap()

    def dscr(name, shape, dt=F32):
        return nc.dram_tensor(name, list(shape), dt, kind="Internal").ap()

    x_in = din("x", [S, D])
    sp_in = din("sp", [128, NSP])
    wmod = din("wmod", [72, 128, 32, 512])
    wgu = [din(f"wgu{i}", [NGU, 128, 32, 512]) for i in (1, 2)]
    wdn = [din(f"wdn{i}", [8, 128, NMF, 512]) for i in (1, 2)]
    wina = din("wina", [NTA // 4, 128, 32, 512])
    winb = din("winb", [4, 128, 32, 512])
    wo_in = din("wo", [8, 128, 32, 512])
    w2_in = din("w2", [LORA, 2048])
    a2_in = din("a2", [LORA, 2048])
    g2_in = din("g2", [2, 128, 2048])
    out_d = nc.dram_tensor("out", [(S // 2 if (split and (S // TT) % 2 == 0 and stop_after is None) else S), D], F32, kind="ExternalOutput").ap()

    xT = dscr("xT", [NFT, 128, S])
    pT = dscr("pT", [NTA, 128, S])
    fvS = dscr("fvS", [NCH, 128, 2048], BF16)
    yT = (nc.dram_tensor("yT", [NFT, 128, S], BF16, kind="ExternalOutput").ap() if DEBUG_OUT else dscr("yT", [NFT, 128, S], BF16))

    es = ExitStack()
    ARW = 53200
    arena = es.enter_context(nc.sbuf_tensor("arena", [128, ARW], F32))
    banks = [es.enter_context(nc.psum_tensor(f"bk{i}", [128, 512], F32)) for i in range(8)]
    P = Prog(nc)
    _off = [0]

    def alloc(n):
        o = _off[0]
        _off[0] += n
        assert _off[0] <= ARW, _off[0]
        return o

    def V(off, n, p0=0, p1=128):
        return arena[p0:p1, off:off + n]

    def VB(off, nb, p0=0, p1=128):
        return arena[p0:p1, off:off + nb // 2].bitcast(BF16)

    o_ident = alloc(128); ident = V(o_ident, 128)
    o_ones = alloc(128); ones = V(o_ones, 128)
    o_blk = alloc(128); blk = V(o_blk, 128)
    o_mcat = alloc(512); mcat = V(o_mcat, 512)
    o_msT = alloc(128); msT = V(o_msT, 128)
    o_sp = alloc(NSP); sp = V(o_sp, NSP)
    o_mod = alloc(288); modT = V(o_mod, 288)
    o_A = alloc(96); Acoef = V(o_A, 96)
    o_hg = alloc(96); hgate = V(o_hg, 96)
    o_onesb = alloc(64); onesb = VB(o_onesb, 128)
    o_mib = alloc(64); mib = VB(o_mib, 128)
    o_sel = alloc(16 * 64); selb = VB(o_sel, 16 * 128)
    o_idb = alloc(64); identb = VB(o_idb, 128)
    o_nmb = alloc(64); negmb = VB(o_nmb, 128)
    o_HT = alloc(8192)
    o_ACT = alloc(22016)
    o_WS = alloc(12288)
    o_XS = alloc(2048)
    o_XO = alloc(2048)
    o_MISC = alloc(2560)
    HT = VB(o_HT, 16384)
    ACTT = VB(o_ACT, 44032)
    WSs = [VB(o_WS + 4096 * s, 8192) for s in range(3)]
    XS = [V(o_XS + 512 * i, 512) for i in range(4)]
    XO = [V(o_XO + 512 * i, 512) for i in range(4)]
    RSTD = V(o_MISC, 512)
    TMPA = [V(o_MISC + 512 + 512 * i, 512) for i in range(2)]
    SQ = [V(o_MISC + 1536 + 512 * i, 512) for i in range(2)]

    cnts = dict(ws=0, xs=0, xo=0, sq=0, tmpa=0, grp=0, alt=0)

    def alt_eng():
        cnts["alt"] += 1
        return "vector" if cnts["alt"] % 2 else "scalar"

    def copy_op(eng, out, in_):
        if eng == "scalar":
            return lambda e: e.copy(out=out, in_=in_)
        return lambda e: e.tensor_copy(out=out, in_=in_)

    P.dma("sync", sp, sp_in, writes=["sp"], semkey="const0")
    P.op("gpsimd", lambda e: e.memset(ident, 0.0), writes=["ident"])
    P.op("gpsimd", lambda e: e.affine_select(out=ident, in_=ident, pattern=[[-1, 128]], compare_op=ALU.not_equal,
                                             fill=1.0, base=0, channel_multiplier=1), reads=["ident"], writes=["ident"])
    P.op("gpsimd", lambda e: e.memset(ones, 1.0), writes=["ones"])
    P.op("gpsimd", lambda e: e.memset(onesb, 1.0), writes=["onesb"])
    P.op("gpsimd", lambda e: e.memset(blk, 0.0), writes=["blk"])
    P.op("gpsimd", lambda e: e.memset(blk[0:64, 0:64], 1.0), reads=["blk"], writes=["blk"])
    P.op("gpsimd", lambda e: e.memset(blk[64:128, 64:128], 1.0), reads=["blk"], writes=["blk"])
    P.op("gpsimd", lambda e: e.memset(mcat, 1.0), writes=["mcat"])
    for q in range(4):
        cmp_ = ALU.is_gt if q % 2 == 0 else ALU.is_ge
        P.op("gpsimd", lambda e, q=q, cmp_=cmp_: e.affine_select(
            out=mcat[:, q * 128:(q + 1) * 128], in_=mcat[:, q * 128:(q + 1) * 128], pattern=[[1, 128]],
            compare_op=cmp_, fill=0.0, base=0, channel_multiplier=-1), reads=["mcat"], writes=["mcat"])
    P.op("gpsimd", lambda e: e.memset(msT, 1.0), writes=["msT"])
    P.op("gpsimd", lambda e: e.affine_select(out=msT, in_=msT, pattern=[[-1, 128]], compare_op=ALU.is_gt,
                                             fill=0.0, base=0, channel_multiplier=1), reads=["msT"], writes=["msT"])
    P.op("gpsimd", lambda e: e.tensor_copy(out=mib, in_=mcat[:, 128:256]), reads=["mcat"], writes=["mib"])
    P.op("gpsimd", lambda e: e.tensor_copy(out=identb, in_=ident), reads=["ident"], writes=["identb"])
    P.op("gpsimd", lambda e: e.tensor_scalar(out=negmb, in0=mcat[:, 128:256], scalar1=30000.0, scalar2=-30000.0, op0=ALU.mult, op1=ALU.add),
         reads=["mcat"], writes=["negmb"])
    for q in range(4):
        P.op("gpsimd", lambda e, q=q: e.tensor_tensor(out=mcat[:, q * 128:(q + 1) * 128], in0=mcat[:, q * 128:(q + 1) * 128], in1=blk, op=ALU.mult),
             reads=["mcat", "blk", "mib", "negmb"], writes=["mcat"])
    P.op("gpsimd", lambda e: e.tensor_tensor(out=msT, in0=msT, in1=blk, op=ALU.mult), reads=["msT", "blk"], writes=["msT"])
    P.op("gpsimd", lambda e: e.memset(selb, 0.0), writes=["selb"])
    inv_s = float(128.0 ** 0.5)
    P.op("vector", lambda e: e.tensor_scalar_mul(out=selb[0:16, :].rearrange("p (h k) -> p h k", k=128),
                                                 in0=ident[0:16, 0:16].unsqueeze(2).to_broadcast([16, 16, 128]), scalar1=inv_s),
         reads=["selb", "ident"], writes=["selb"])

    def linear(Wt, G, KT, ntiles, form, act, act_key, evac, nsb=4, ktu=16, ncols=TT):
        units = []
        for g in range(G):
            for k0 in range(0, KT, ktu):
                units.append((g, k0, min(KT, k0 + ktu)))

        def issue(u):
            g, k0, k1 = units[u]
            s = cnts["ws"] % 3
            cnts["ws"] += 1
            dst = WSs[s][:, 0:(k1 - k0) * 512].rearrange("p (k c) -> p k c", c=512)
            P.dma("gpsimd", dst, Wt[g, :, k0:k1, :], writes=[f"ws{s}"], semkey=f"ws{s}")
            return s
        slots = {}
        for u in range(min(2, len(units))):
            slots[u] = issue(u)
        for u, (g, k0, k1) in enumerate(units):
            if u + 2 < len(units):
                slots[u + 2] = issue(u + 2)
            s = slots.pop(u)
            if k0 == 0:
                bset = cnts["grp"] % 2
                cnts["grp"] += 1
            bk = [4 * bset + j for j in range(4)]
            for kt in range(k0, k1):
                kl = kt - k0
                first, last = (kt == 0), (kt == KT - 1)
                if form == "a":
                    for j in range(ntiles(g)):
                        P.op("tensor", lambda e, b=bk[j], s=s, kl=kl, j=j, kt=kt, first=first, last=last:
                             e.matmul(banks[b][:, 0:ncols], lhsT=WSs[s][:, kl * 512 + j * 128: kl * 512 + (j + 1) * 128],
                                      rhs=act(kt), start=first, stop=last),
                             reads=[f"ws{s}", act_key(kt)], writes=[f"bk{bk[j]}"])
                else:
                    for sb in range(nsb):
                        P.op("tensor", lambda e, b=bk[sb], s=s, kl=kl, sb=sb, kt=kt, first=first, last=last:
                             e.matmul(banks[b][:, 0:512], lhsT=act(kt, sb), rhs=WSs[s][:, kl * 512:(kl + 1) * 512],
                                      start=first, stop=last),
                             reads=[f"ws{s}", act_key(kt)], writes=[f"bk{bk[sb]}"])
            if k1 == KT:
                evac(g, bk)

    o_scf = o_ACT
    scf = V(o_scf, 32)
    screp = VB(o_scf + 64, 4096)
    P.op("scalar", lambda e: e.activation(out=scf, in_=sp[:, SP_C:SP_C + 32], func=AF.Silu), reads=["sp"], writes=["scf"])
    for kt in range(32):
        P.op("vector", lambda e, kt=kt: e.tensor_copy(out=screp[:, kt * 128:(kt + 1) * 128],
                                                       in_=scf[:, kt:kt + 1].to_broadcast([128, 128])),
             reads=["scf"], writes=["screp"])

    def mod_evac(g, bk):
        t = TMPA[cnts["tmpa"] % 2]
        tk = f"tmpa{cnts['tmpa'] % 2}"
        cnts["tmpa"] += 1
        b = bk[0]
        P.op("vector", lambda e: e.tensor_tensor(out=t.rearrange("p (a b) -> p a b", b=128),
                                                 in0=banks[b][:, :].rearrange("p (a b) -> p a b", b=128),
                                                 in1=ident.unsqueeze(1).to_broadcast([128, 4, 128]), op=ALU.mult),
             reads=[f"bk{b}", "ident"], writes=[tk])
        P.op("vector", lambda e: e.tensor_reduce(out=modT[:, 4 * g:4 * g + 4], in_=t.rearrange("p (a b) -> p a b", b=128),
                                                 axis=AX.X, op=ALU.add), reads=[tk], writes=["modT"])
    linear(wmod, 72, 32, None, "b", lambda kt, sb: screp[:, kt * 128:(kt + 1) * 128], lambda kt: "screp", mod_evac, nsb=1)
    P.op("vector", lambda e: e.tensor_add(out=modT, in0=modT, in1=sp[:, SP_BMOD:SP_BMOD + 288]), reads=["modT", "sp"], writes=["modT"])
    for n_ in range(3):
        gcol = SP_GAIN + 32 * n_
        P.op("vector", lambda e, n_=n_, gcol=gcol: e.scalar_tensor_tensor(
            out=Acoef[:, 32 * n_:32 * n_ + 32], in0=modT[:, 96 * n_ + 32:96 * n_ + 64], scalar=1.0,
            in1=sp[:, gcol:gcol + 32], op0=ALU.add, op1=ALU.mult), reads=["modT", "sp"], writes=["Acoef"])
        P.op("vector", lambda e, n_=n_: e.tensor_scalar_mul(out=hgate[:, 32 * n_:32 * n_ + 32],
                                                           in0=modT[:, 96 * n_ + 64:96 * n_ + 96],
                                                           scalar1=(1.0 if n_ == 1 else 0.5)),
             reads=["modT"], writes=["hgate"])
    P.barrier()

    XIN = [V(o_ACT + 4096 * s_, 4096) for s_ in range(4)]
    for tg in range(NTG):
        for s_ in range(4):
            r0 = tg * TT + s_ * 128
            P.dma("sync", XIN[s_], x_in[r0:r0 + 128, :], writes=[f"xin{s_}"], semkey=f"xin{s_}")
        for ft in range(NFT):
            b = ft % 8
            for s_ in range(4):
                P.op("tensor", lambda e, b=b, s_=s_, ft=ft: e.transpose(banks[b][:, s_ * 128:(s_ + 1) * 128],
                                                                      XIN[s_][:, ft * 128:(ft + 1) * 128], ident),
                     reads=[f"xin{s_}", "ident"], writes=[f"bk{b}"])
            i = cnts["xo"] % 4
            cnts["xo"] += 1
            P.op(alt_eng(), copy_op("vector" if cnts["alt"] % 2 else "scalar", XO[i], banks[b][:, :]),
                 reads=[f"bk{b}"], writes=[f"xo{i}"])
            P.dma("sync", xT[ft, :, tg * TT:(tg + 1) * TT], XO[i], reads=[f"xo{i}"], writes=[f"xT{ft}_{tg}"], semkey=f"xo{i}")
    P.barrier()

    def load_x(ft, tg):
        i = cnts["xs"] % 4
        cnts["xs"] += 1
        P.dma("sync", XS[i], xT[ft, :, tg * TT:(tg + 1) * TT], reads=[f"xT{ft}_{tg}"], writes=[f"xs{i}"], semkey=f"xs{i}")
        return i

    def sumsq_rstd(tg, bank, src=None):
        for ft in range(NFT):
            i = load_x(ft, tg)
            q = cnts["sq"] % 2
            cnts["sq"] += 1
            P.op("scalar", lambda e, i=i, q=q: e.activation(out=SQ[q], in_=XS[i], func=AF.Square),
                 reads=[f"xs{i}"], writes=[f"sq{q}"])
            P.op("tensor", lambda e, q=q, ft=ft: e.matmul(banks[bank][:, :], lhsT=ones, rhs=SQ[q], start=(ft == 0), stop=(ft == NFT - 1)),
                 reads=[f"sq{q}", "ones"], writes=[f"bk{bank}"])
        P.op("scalar", lambda e: e.activation(out=RSTD, in_=banks[bank][:, :], func=AF.Sqrt, bias=1e-6, scale=1.0 / D),
             reads=[f"bk{bank}"], writes=["rstd"])
        P.op("vector", lambda e: e.reciprocal(out=RSTD, in_=RSTD), reads=["rstd"], writes=["rstd"])

    def norm_to_HT(tg, n_):
        sumsq_rstd(tg, 0)
        for ft in range(NFT):
            i = load_x(ft, tg)
            t = cnts["tmpa"] % 2
            cnts["tmpa"] += 1
            P.op("vector", lambda e, i=i, t=t: e.tensor_tensor(out=TMPA[t], in0=XS[i], in1=RSTD, op=ALU.mult),
                 reads=[f"xs{i}", "rstd"], writes=[f"tmpa{t}"])
            P.op("scalar", lambda e, t=t, ft=ft: e.activation(out=HT[:, ft * TT:(ft + 1) * TT], in_=TMPA[t], func=AF.Identity,
                                                             bias=modT[:, 96 * n_ + ft:96 * n_ + ft + 1],
                                                             scale=Acoef[:, 32 * n_ + ft:32 * n_ + ft + 1]),
                 reads=[f"tmpa{t}", "modT", "Acoef"], writes=[f"ht{ft}"])

    def resid_evac(tg, gcol_base):
        def ev(g, bk):
            for j in range(4):
                ft = 4 * g + j
                i = load_x(ft, tg)
                o = cnts["xo"] % 4
                cnts["xo"] += 1
                P.op("vector", lambda e, b=bk[j], i=i, o=o, ft=ft: e.scalar_tensor_tensor(
                    out=XO[o], in0=banks[b][:, :], scalar=hgate[:, gcol_base + ft:gcol_base + ft + 1], in1=XS[i],
                    op0=ALU.mult, op1=ALU.add), reads=[f"bk{bk[j]}", f"xs{i}", "hgate"], writes=[f"xo{o}"])
                P.dma("sync", xT[ft, :, tg * TT:(tg + 1) * TT], XO[o], reads=[f"xo{o}"], writes=[f"xT{ft}_{tg}"], semkey=f"xo{o}")
        return ev

    def ffn(tg, which, n_):
        norm_to_HT(tg, n_)

        def gu_evac(g, bk):
            for m in range(2):
                t = cnts["tmpa"] % 2
                cnts["tmpa"] += 1
                mt = 2 * g + m
                P.op("scalar", lambda e, b=bk[m], t=t: e.activation(out=TMPA[t], in_=banks[b][:, :], func=AF.Silu),
                     reads=[f"bk{bk[m]}"], writes=[f"tmpa{t}"])
                P.op("vector", lambda e, b=bk[2 + m], t=t, mt=mt: e.tensor_tensor(
                    out=ACTT[:, mt * TT:(mt + 1) * TT], in0=TMPA[t], in1=banks[b][:, :], op=ALU.mult),
                    reads=[f"tmpa{t}", f"bk{bk[2 + m]}"], writes=[f"act{mt}"])
        linear(wgu[which], NGU, 32, lambda g: 4, "a", lambda kt: HT[:, kt * TT:(kt + 1) * TT], lambda kt: f"ht{kt}", gu_evac)
        linear(wdn[which], 8, NMF, lambda g: 4, "a", lambda kt: ACTT[:, kt * TT:(kt + 1) * TT], lambda kt: f"act{kt}",
               resid_evac(tg, 32 * n_))


    def final_phase():
        OUTROW = V(o_ACT, 16384).rearrange("p (s f) -> p s f", f=4096)
        for tg in range(NTD):
            sumsq_rstd(tg, 0)
            for ft in range(NFT):
                i = load_x(ft, tg)
                t = cnts["tmpa"] % 2
                cnts["tmpa"] += 1
                P.op("vector", lambda e, i=i, t=t, ft=ft: e.scalar_tensor_tensor(
                    out=TMPA[t], in0=XS[i], scalar=sp[:, SP_GAIN + 96 + ft:SP_GAIN + 97 + ft], in1=RSTD,
                    op0=ALU.mult, op1=ALU.mult), reads=[f"xs{i}", "rstd", "sp"], writes=[f"tmpa{t}"])
                b = 4 + ft % 4
                for s_ in range(4):
                    P.op("tensor", lambda e, b=b, s_=s_, t=t: e.transpose(banks[b][:, s_ * 128:(s_ + 1) * 128],
                                                                        TMPA[t][:, s_ * 128:(s_ + 1) * 128], ident),
                         reads=[f"tmpa{t}", "ident"], writes=[f"bk{b}"])
                eng = alt_eng()
                P.op(eng, copy_op(eng, OUTROW[:, :, ft * 128:(ft + 1) * 128],
                                  banks[b][:, :].rearrange("p (s f) -> p s f", f=128)),
                     reads=[f"bk{b}"], writes=["outrow"])
            for s_ in range(4):
                r0 = tg * TT + s_ * 128
                P.dma("sync", out_d[r0:r0 + 128, :], OUTROW[:, s_, :], reads=["outrow"], writes=[f"out{s_}"], semkey=f"outd{s_}")
        P.emit(final_keys=[f"out{s_}" for s_ in range(4)])
        es.close()
        if VERBOSE:
            print("PROG stats", P.stats, "nsem", P.nsem, "maxcnt", P.maxcnt, flush=True)
        return nc

    def inproj(tg):
        norm_to_HT(tg, 1)

        def a_evac(g, bk):
            for j in range(4):
                tile = 4 * g + j
                if tile >= 85:
                    continue
                o = cnts["xo"] % 4
                cnts["xo"] += 1
                eng = alt_eng()
                P.op(eng, copy_op(eng, XO[o], banks[bk[j]][:, :]), reads=[f"bk{bk[j]}"], writes=[f"xo{o}"])
                P.dma("sync", pT[tile, :, tg * TT:(tg + 1) * TT], XO[o], reads=[f"xo{o}"], writes=[f"pT{tile}"], semkey=f"xo{o}")
        linear(wina, NTA // 4, 32, lambda g: (4 if g < 21 else 1), "a", lambda kt: HT[:, kt * TT:(kt + 1) * TT],
               lambda kt: f"ht{kt}", a_evac)

        def b_evac(g, bk):
            for sb in range(4):
                t = cnts["tmpa"] % 2
                cnts["tmpa"] += 1
                stg = VB(o_MISC + 512 + 512 * t, 512)
                eng = alt_eng()
                P.op(eng, copy_op(eng, stg, banks[bk[sb]][:, :]), reads=[f"bk{bk[sb]}"], writes=[f"tmpa{t}"])
                P.dma("sync", fvS[tg * 4 + sb, :, g * 512:(g + 1) * 512], stg, reads=[f"tmpa{t}"], writes=["fvS"], semkey=f"tmpa{t}")
        linear(winb, 4, 32, None, "b", lambda kt, sb: HT[:, kt * TT + sb * 128: kt * TT + (sb + 1) * 128],
               lambda kt: f"ht{kt}", b_evac)

    def rwkv_phase():
        nbuf = 30208 // S
        assert nbuf >= 14
        def B(i, p0=0, p1=128):
            return V(o_HT + S * i, S, p0, p1)
        oc = o_WS
        W2B = VB(oc, 2048)
        A2B = VB(oc + 1024, 2048)
        G2B = VB(oc + 2048, 4096)
        TH = VB(oc + 4096, S)
        ADM = VB(oc + 4096 + S // 2, S)
        SG = VB(oc + 4096 + S, 2 * S)
        GTm = V(oc + 4096 + 2 * S, 2 * NCH)
        OMMU = V(oc + 4096 + 2 * S + 32, 52)
        OMKA = V(oc + 4096 + 2 * S + 84, 16)
        ob = oc + 4096 + 2 * S + 112
        M4s = [V(ob + 512 * e, 512) for e in range(2)]
        XA = [V(ob + 1024 + 512 * q, 512) for q in range(2)]
        PTs = [V(ob + 2048 + 256 * q, 256) for q in range(2)]
        NS = V(ob + 2560, 256)
        WSB = V(ob + 2816, 128)
        LU = V(ob + 2944, 192)
        Zp = [V(ob + 3136 + 128 * q, 128) for q in range(2)]
        MST2 = V(ob + 3392, 256)
        assert ob + 3648 <= o_WS + 12288, (ob + 3648, o_WS + 12288)
        c60 = 0.6065306597126334

        P.cur_level = 1
        P.op("vector", lambda e: e.memset(VB(oc, 4096), 0.0), writes=["w2b", "a2b"])
        P.op("vector", lambda e: e.memset(VB(oc + 4096, 2 * S), 0.0), writes=["th", "adm"])
        P.dma("gpsimd", W2B[0:LORA, :], w2_in, reads=["w2b"], writes=["w2b"], semkey="constr")
        P.dma("gpsimd", A2B[0:LORA, :], a2_in, reads=["a2b"], writes=["a2b"], semkey="constr")
        P.dma("gpsimd", G2B.rearrange("p (k c) -> p k c", c=2048), g2_in.rearrange("k p c -> p k c"), writes=["g2b"], semkey="constr")
        P.op("vector", lambda e: e.tensor_scalar(out=OMMU, in0=sp[:, SP_MU:SP_MU + 52], scalar1=-1.0, scalar2=1.0,
                                                 op0=ALU.mult, op1=ALU.add), reads=["sp"], writes=["ommu"])
        P.op("vector", lambda e: e.tensor_scalar(out=OMKA, in0=sp[:, SP_KA:SP_KA + 16], scalar1=-1.0, scalar2=1.0,
                                                 op0=ALU.mult, op1=ALU.add), reads=["sp"], writes=["omka"])
        P.op("gpsimd", lambda e: e.memset(LU, 0.0), writes=["lu"])
        P.op("vector", lambda e: e.tensor_copy(out=MST2[:, 0:128], in_=msT), reads=["msT"], writes=["mst2"])
        P.op("vector", lambda e: e.tensor_copy(out=MST2[:, 128:256], in_=msT), reads=["msT", "mst2"], writes=["mst2"])

        P.cur_level = 2
        def mix(dst, dk, src, sk, tmp, tk, mcol, rows=128):
            mu = sp[0:rows, SP_MU + mcol:SP_MU + mcol + 1]
            om = OMMU[0:rows, mcol:mcol + 1]
            P.op("gpsimd", lambda e: e.memset(tmp[0:rows, 0:1], 0.0), writes=[tk])
            P.op("scalar", lambda e: e.mul(out=tmp[0:rows, 1:S], in_=src[0:rows, 0:S - 1], mul=mu), reads=[sk, "sp", tk], writes=[tk])
            P.op("vector", lambda e: e.scalar_tensor_tensor(out=dst[0:rows, :], in0=src[0:rows, :], scalar=om, in1=tmp[0:rows, :],
                                                            op0=ALU.mult, op1=ALU.add), reads=[sk, tk, "ommu"], writes=[dk])

        P.dma("sync", B(0, 0, LORA), pT[48, 0:LORA, :], reads=["pT48"], writes=["b0"], semkey="rb0")
        mix(B(1), "b1", B(0), "b0", B(2), "b2", 48, LORA)
        P.op("scalar", lambda e: e.activation(out=TH[0:LORA, :], in_=B(1, 0, LORA), func=AF.Tanh), reads=["b1", "th"], writes=["th"])
        P.dma("sync", B(3, 0, LORA), pT[49, 0:LORA, :], reads=["pT49"], writes=["b3"], semkey="rb3")
        mix(B(4), "b4", B(3), "b3", B(5), "b5", 49, LORA)
        P.op("vector", lambda e: e.tensor_copy(out=ADM[0:LORA, :], in_=B(4, 0, LORA)), reads=["b4", "adm"], writes=["adm"])
        for kt in range(2):
            P.dma("sync", B(6 + 3 * kt), pT[50 + kt, :, :], reads=[f"pT{50 + kt}"], writes=[f"b{6 + 3 * kt}"], semkey=f"rb{6 + 3 * kt}")
            mix(B(7 + 3 * kt), f"b{7 + 3 * kt}", B(6 + 3 * kt), f"b{6 + 3 * kt}", B(8 + 3 * kt), f"b{8 + 3 * kt}", 50 + kt)
            P.op("scalar", lambda e, kt=kt: e.activation(out=SG[:, kt * S:(kt + 1) * S], in_=B(7 + 3 * kt), func=AF.Sigmoid),
                 reads=[f"b{7 + 3 * kt}"], writes=["sg"])

        def spcol(base, hp, rows=128):
            return sp[0:rows, base + hp:base + hp + 1]

        for hp in range(NHP):
            R_, K_, V_, SIG, A_, KKN, E1, E2 = B(0), B(1), B(2), B(3), B(4), B(5), B(6), B(7)
            AT, BT, KTf, RT, G_, BON = B(8), B(9), B(10), B(11), B(12), B(13)
            hs = slice(hp * 128, (hp + 1) * 128)
            P.cur_level = 3
            for n_, (raw, dst, tmpb) in enumerate([(8, 0, 11), (9, 1, 12), (10, 2, 13)]):
                P.dma("sync", B(raw), pT[16 * n_ + hp, :, :], reads=[f"pT{16 * n_ + hp}"], writes=[f"b{raw}"], semkey=f"rb{raw}")
                mix(B(dst), f"b{dst}", B(raw), f"b{raw}", B(tmpb), f"b{tmpb}", 16 * n_ + hp)
            P.cur_level = 4
            for tg in range(NTG):
                ts_ = slice(tg * TT, (tg + 1) * TT)
                b = tg % 4
                P.op("tensor", lambda e, b=b, ts_=ts_, hs=hs: e.matmul(banks[b][:, :], lhsT=W2B[:, hs], rhs=TH[:, ts_], start=True, stop=True),
                     reads=["w2b", "th"], writes=[f"bk{b}"])
                P.op("scalar", lambda e, b=b, ts_=ts_, hp=hp: e.activation(out=SIG[:, ts_], in_=banks[b][:, :], func=AF.Sigmoid,
                                                                         bias=spcol(SP_W0, hp), scale=1.0),
                     reads=[f"bk{b}", "sp"], writes=["b3"])
                b = 4 + tg % 4
                P.op("tensor", lambda e, b=b, ts_=ts_, hs=hs: e.matmul(banks[b][:, :], lhsT=A2B[:, hs], rhs=ADM[:, ts_], start=True, stop=True),
                     reads=["a2b", "adm"], writes=[f"bk{b}"])
                P.op("scalar", lambda e, b=b, ts_=ts_, hp=hp: e.activation(out=A_[:, ts_], in_=banks[b][:, :], func=AF.Sigmoid,
                                                                         bias=spcol(SP_A0, hp), scale=1.0),
                     reads=[f"bk{b}", "sp"], writes=["b4"])
            CUMS = B(8)
            P.cur_level = 5
            for c in range(2 * NCH):
                cs = slice(c * 64, (c + 1) * 64)
                P.op("vector", lambda e, cs=cs: e.tensor_tensor_scan(out=CUMS[:, cs], data0=ones[:, 0:64], data1=SIG[:, cs], initial=0.0,
                                                                    op0=ALU.mult, op1=ALU.add), reads=["b3", "ones"], writes=["b8"])
            P.op("scalar", lambda e: e.activation(out=E1, in_=CUMS, func=AF.Exp, scale=-c60), reads=["b8"], writes=["b6"])
            P.op("scalar", lambda e: e.activation(out=E2, in_=CUMS, func=AF.Exp, scale=c60), reads=["b8"], writes=["b7"])
            E3 = B(9)
            P.op("vector", lambda e: e.tensor_sub(out=E3, in0=CUMS, in1=SIG), reads=["b8", "b3"], writes=["b9"])
            P.op("scalar", lambda e: e.activation(out=E3, in_=E3, func=AF.Exp, scale=-c60), reads=["b9"], writes=["b9"])
            P.op("vector", lambda e: e.tensor_copy(out=GTm, in_=E1.rearrange("p (c t) -> p c t", t=64)[:, :, 63]), reads=["b6"], writes=["gt"])
            KKr = B(10)
            P.cur_level = 6
            P.op("scalar", lambda e, hp=hp: e.mul(out=KKr, in_=K_, mul=spcol(SP_KK, hp)), reads=["b1", "sp"], writes=["b10"])
            P.op("scalar", lambda e: e.activation(out=BON, in_=KKr, func=AF.Square), reads=["b10"], writes=["b13"])
            for tg in range(NTG):
                ts_ = slice(tg * TT, (tg + 1) * TT)
                b = tg % 4
                P.op("tensor", lambda e, b=b, ts_=ts_: e.matmul(banks[b][:, :], lhsT=blk, rhs=BON[:, ts_], start=True, stop=True),
                     reads=["blk", "b13"], writes=[f"bk{b}"])
                P.op("scalar", lambda e, b=b, ts_=ts_: e.activation(out=KKN[:, ts_], in_=banks[b][:, :], func=AF.Sqrt, bias=1e-24, scale=1.0),
                     reads=[f"bk{b}"], writes=["b5"])
            P.op("vector", lambda e: e.reciprocal(out=KKN, in_=KKN), reads=["b5"], writes=["b5"])
            P.op("vector", lambda e: e.tensor_tensor(out=KKN, in0=KKr, in1=KKN, op=ALU.mult), reads=["b10", "b5"], writes=["b5"])
            P.cur_level = 7
            P.op("vector", lambda e: e.scalar_tensor_tensor(out=AT, in0=KKN, scalar=-1.0, in1=E3, op0=ALU.mult, op1=ALU.mult),
                 reads=["b5", "b9"], writes=["b8"])
            P.op("vector", lambda e: e.tensor_tensor(out=KKr, in0=KKN, in1=A_, op=ALU.mult), reads=["b5", "b4"], writes=["b10"])
            P.op("vector", lambda e: e.tensor_tensor(out=BT, in0=KKr, in1=E2, op=ALU.mult), reads=["b10", "b7", "b8"], writes=["b9"])
            P.op("vector", lambda e, hp=hp: e.tensor_scalar(out=BON, in0=A_, scalar1=spcol(SP_KA, hp), scalar2=OMKA[:, hp:hp + 1],
                                                           op0=ALU.mult, op1=ALU.add), reads=["b4", "sp", "omka"], writes=["b13"])
            P.op("vector", lambda e: e.tensor_tensor(out=K_, in0=K_, in1=BON, op=ALU.mult), reads=["b1", "b13", "b10"], writes=["b1"])
            P.op("vector", lambda e: e.tensor_tensor(out=KTf, in0=K_, in1=E2, op=ALU.mult), reads=["b1", "b7", "b9"], writes=["b10"])
            P.op("vector", lambda e: e.tensor_tensor(out=RT, in0=R_, in1=E1, op=ALU.mult), reads=["b0", "b6"], writes=["b11"])
            P.cur_level = 8
            for tg in range(NTG):
                ts_ = slice(tg * TT, (tg + 1) * TT)
                b = 4 + tg % 4
                for kt in range(2):
                    P.op("tensor", lambda e, b=b, ts_=ts_, kt=kt, hs=hs: e.matmul(
                        banks[b][:, :], lhsT=G2B[:, kt * 2048 + hs.start: kt * 2048 + hs.stop],
                        rhs=SG[:, kt * S + ts_.start: kt * S + ts_.stop], start=(kt == 0), stop=(kt == 1)),
                        reads=["g2b", "sg"], writes=[f"bk{b}"])
                P.op("scalar", lambda e, b=b, ts_=ts_: e.copy(out=G_[:, ts_], in_=banks[b][:, :]), reads=[f"bk{b}"], writes=["b12"])
            P.cur_level = 9
            P.op("vector", lambda e, hp=hp: e.scalar_tensor_tensor(out=BON, in0=R_, scalar=spcol(SP_RK, hp), in1=K_,
                                                                  op0=ALU.mult, op1=ALU.mult), reads=["b0", "b1", "sp"], writes=["b13"])
            for tg in range(NTG):
                ts_ = slice(tg * TT, (tg + 1) * TT)
                b = tg % 4
                P.op("tensor", lambda e, b=b, ts_=ts_: e.matmul(banks[b][:, :], lhsT=blk, rhs=BON[:, ts_], start=True, stop=True),
                     reads=["blk", "b13"], writes=[f"bk{b}"])
                P.op("vector", lambda e, b=b, ts_=ts_: e.tensor_tensor(out=BON[:, ts_], in0=banks[b][:, :], in1=V_[:, ts_], op=ALU.mult),
                     reads=[f"bk{b}", "b2", "b13"], writes=["b13"])
            P.cur_level = 10
            BTOK = B(0).rearrange("p (c k) -> p c k", k=128)
            KTOK = B(1).rearrange("p (c k) -> p c k", k=128)
            LV = V(o_HT + 3 * S, NCH * 192).rearrange("p (c k) -> p c k", k=192)
            P.op("gpsimd", lambda e: e.memset(V(o_HT + 3 * S, NCH * 192), 0.0), reads=["b3", "b4"], writes=["b3", "b4"])
            for c in range(NCH):
                cs = slice(c * 128, (c + 1) * 128)
                b0_ = 3 * (c % 2)
                b1_, b2_ = b0_ + 1, b0_ + 2
                P.op("tensor", lambda e, b=b0_, cs=cs: e.transpose(banks[b][:, 0:128], BT[:, cs], ident), reads=["b9", "ident"], writes=[f"bk{b0_}"])
                P.op("tensor", lambda e, b=b1_, cs=cs: e.transpose(banks[b][:, 0:128], KTf[:, cs], ident), reads=["b10", "ident"], writes=[f"bk{b1_}"])
                P.op("tensor", lambda e, b=b2_, cs=cs: e.transpose(banks[b][:, 0:128], V_[:, cs], ident), reads=["b2", "ident"], writes=[f"bk{b2_}"])
                P.op("vector", lambda e, b=b0_, c=c: e.tensor_copy(out=BTOK[:, c, :], in_=banks[b][:, 0:128]), reads=[f"bk{b0_}", "b11", "b13"], writes=["b0"])
                P.op("scalar", lambda e, b=b1_, c=c: e.copy(out=KTOK[:, c, :], in_=banks[b][:, 0:128]), reads=[f"bk{b1_}", "b10", "b11", "b13"], writes=["b1"])
                P.op("vector", lambda e, b=b2_, c=c: e.tensor_copy(out=LV[:, c, 0:64], in_=banks[b][:, 0:64]), reads=[f"bk{b2_}"], writes=["b3", "b4"])
                P.op("vector", lambda e, b=b2_, c=c: e.tensor_copy(out=LV[:, c, 128:192], in_=banks[b][:, 64:128]), reads=[f"bk{b2_}", "b3", "b4"], writes=["b3", "b4"])
            YOUT = B(5)
            P.cur_level = 11
            P.op("gpsimd", lambda e: e.memset(Zp[0], 0.0), writes=["zp0"])
            zcnt = [0]
            SC = [dict(M4s=M4s, XA=XA, PTs=PTs, NS=NS),
                  dict(M4s=[V(o_XS + 512 * e, 512) for e in range(2)], XA=[V(o_XS + 1024 + 512 * q, 512) for q in range(2)],
                       PTs=[V(o_XS + 2048 + 256 * q, 256) for q in range(2)], NS=V(o_XS + 2560, 256))]

            def tmat_gen(c, st):
                M4c, XAc, PTc, NSc = SC[st]["M4s"], SC[st]["XA"], SC[st]["PTs"], SC[st]["NS"]
                cs = slice(c * 128, (c + 1) * 128)
                for e_ in range(2):
                    ps_ = slice(64 * e_, 64 * e_ + 64)
                    for q, (lk, lb, rk_, rb) in enumerate([(BT, "b9", AT, "b8"), (BT, "b9", RT, "b11"), (KTf, "b10", AT, "b8"), (KTf, "b10", RT, "b11")]):
                        P.op("tensor", lambda e, e_=e_, q=q, lk=lk, rk_=rk_, ps_=ps_: e.matmul(
                            banks[e_][:, q * 128:(q + 1) * 128], lhsT=lk[ps_, cs], rhs=rk_[ps_, cs], start=True, stop=True),
                            reads=[lb, rb], writes=[f"bk{e_}"])
                    P.op("vector", lambda e, e_=e_: e.tensor_tensor(out=M4c[e_], in0=banks[e_][:, :], in1=mcat, op=ALU.mult),
                         reads=[f"bk{e_}", "mcat"], writes=[f"m4{e_}_{st}"])
                    P.op("tensor", lambda e, e_=e_, ps_=ps_: e.matmul(banks[4][:, 256 + e_ * 128:256 + (e_ + 1) * 128],
                                                                     lhsT=AT[ps_, cs], rhs=BT[ps_, cs], start=True, stop=True),
                         reads=["b8", "b9"], writes=["bk4"])
                    yield
                P.op("vector", lambda e: e.tensor_tensor(out=NSc, in0=banks[4][:, 256:512], in1=MST2, op=ALU.mult),
                     reads=["bk4", "mst2"], writes=[f"ns{st}"])
                for e_ in range(2):
                    P.op("gpsimd", lambda e, e_=e_: e.tensor_tensor(out=PTc[0][:, e_ * 128:(e_ + 1) * 128], in0=M4c[e_][:, 0:128], in1=ident, op=ALU.add),
                         reads=[f"m4{e_}_{st}", "ident"], writes=[f"pt0_{st}"])
                yield
                for lv in range(5):
                    qc, qn = lv % 2, (lv + 1) % 2
                    xkeys = [f"ns{st}", f"m40_{st}", f"m41_{st}"] if lv == 0 else [f"xa{qc}_{st}"]
                    for e_ in range(2):
                        if lv == 0:
                            X_, XT_ = NSc[:, e_ * 128:(e_ + 1) * 128], M4c[e_][:, 0:128]
                        else:
                            X_, XT_ = XAc[qc][:, e_ * 256: e_ * 256 + 128], XAc[qc][:, e_ * 256 + 128: e_ * 256 + 256]
                        P.op("tensor", lambda e, e_=e_, X_=X_, XT_=XT_: e.matmul(banks[3][:, e_ * 256: e_ * 256 + 128], lhsT=XT_, rhs=X_, start=True, stop=True),
                             reads=xkeys, writes=["bk3"])
                        if lv < 4:
                            P.op("tensor", lambda e, e_=e_, X_=X_, XT_=XT_: e.matmul(banks[3][:, e_ * 256 + 128: e_ * 256 + 256], lhsT=X_, rhs=XT_, start=True, stop=True),
                                 reads=xkeys, writes=["bk3"])
                    eng = "scalar" if lv % 2 == 0 else "vector"
                    P.op(eng, copy_op(eng, XAc[qn], banks[3][:, :]), reads=["bk3"], writes=[f"xa{qn}_{st}"])
                    yield
                    for e_ in range(2):
                        es_ = slice(e_ * 128, (e_ + 1) * 128)
                        P.op("tensor", lambda e, es_=es_, qc=qc: e.matmul(banks[4][:, es_], lhsT=ident, rhs=PTc[qc][:, es_], start=True, stop=False),
                             reads=["ident", f"pt{qc}_{st}"], writes=["bk4"])
                        P.op("tensor", lambda e, es_=es_, qc=qc, qn=qn, e_=e_: e.matmul(banks[4][:, es_], lhsT=XAc[qn][:, e_ * 256: e_ * 256 + 128],
                                                                                      rhs=PTc[qc][:, es_], start=False, stop=True),
                             reads=[f"xa{qn}_{st}", f"pt{qc}_{st}"], writes=["bk4"])
                    eng = "vector" if lv % 2 == 0 else "scalar"
                    P.op(eng, copy_op(eng, PTc[qn], banks[4][:, 0:256]), reads=["bk4"], writes=[f"pt{qn}_{st}"])
                    yield

            def chain_gen(c, st):
                M4c = SC[st]["M4s"]
                TmT = SC[st]["PTs"][1]
                cs = slice(c * 128, (c + 1) * 128)
                for s_ in range(2):
                    zc, zn = zcnt[0] % 2, (zcnt[0] + 1) % 2
                    zcnt[0] += 1
                    rows = slice(64 * s_, 64 * s_ + 64)
                    tsub = slice(c * 128 + 64 * s_, c * 128 + 64 * s_ + 64)
                    P.op("tensor", lambda e, zc=zc: e.matmul(banks[5][:, 0:128], lhsT=AT[:, cs], rhs=Zp[zc], start=True, stop=False),
                         reads=["b8", f"zp{zc}"], writes=["bk5"])
                    for e_ in range(2):
                        P.op("tensor", lambda e, e_=e_: e.matmul(banks[5][:, e_ * 64:(e_ + 1) * 64], lhsT=M4c[e_][:, 256:384],
                                                                rhs=LV[:, c, e_ * 128: e_ * 128 + 64], start=False, stop=(e_ == 1)),
                             reads=[f"m4{e_}_{st}", "b3", "b4"], writes=["bk5"])
                    P.op("scalar", lambda e: e.copy(out=WSB, in_=banks[5][:, 0:128]), reads=["bk5"], writes=["wsb"])
                    yield
                    for e_ in range(2):
                        P.op("tensor", lambda e, e_=e_: e.matmul(banks[6][:, e_ * 64:(e_ + 1) * 64], lhsT=TmT[:, e_ * 128:(e_ + 1) * 128],
                                                                rhs=WSB[:, e_ * 64:(e_ + 1) * 64], start=True, stop=True),
                             reads=[f"pt1_{st}", "wsb"], writes=["bk6"])
                    P.op("vector", lambda e: e.tensor_copy(out=LU[:, 0:64], in_=banks[6][:, 0:64]), reads=["bk6"], writes=["lu"])
                    P.op("vector", lambda e: e.tensor_copy(out=LU[:, 128:192], in_=banks[6][:, 64:128]), reads=["bk6", "lu"], writes=["lu"])
                    yield
                    P.op("tensor", lambda e, zc=zc: e.matmul(banks[7][:, 0:128], lhsT=ident, rhs=Zp[zc], start=True, stop=False),
                         reads=["ident", f"zp{zc}"], writes=["bk7"])
                    for e_ in range(2):
                        P.op("tensor", lambda e, e_=e_, rows=rows: e.matmul(banks[7][:, e_ * 64:(e_ + 1) * 64], lhsT=BTOK[rows, c, :],
                                                                           rhs=LU[rows, e_ * 128: e_ * 128 + 64], start=False, stop=False),
                             reads=["b0", "lu"], writes=["bk7"])
                    for e_ in range(2):
                        P.op("tensor", lambda e, e_=e_, rows=rows: e.matmul(banks[7][:, e_ * 64:(e_ + 1) * 64], lhsT=KTOK[rows, c, :],
                                                                           rhs=LV[rows, c, e_ * 128: e_ * 128 + 64], start=False, stop=(e_ == 1)),
                             reads=["b1", "b3", "b4"], writes=["bk7"])
                    P.op("vector", lambda e, s_=s_, zn=zn: e.scalar_tensor_tensor(out=Zp[zn], in0=banks[7][:, 0:128],
                                                                                 scalar=GTm[:, 2 * c + s_:2 * c + s_ + 1], in1=blk,
                                                                                 op0=ALU.mult, op1=ALU.mult),
                         reads=["bk7", "gt", "blk"], writes=[f"zp{zn}"])
                    P.op("tensor", lambda e, tsub=tsub, zc=zc: e.matmul(banks[2][:, 0:64], lhsT=Zp[zc], rhs=RT[:, tsub], start=True, stop=False),
                         reads=[f"zp{zc}", "b11"], writes=["bk2"])
                    for e_ in range(2):
                        P.op("tensor", lambda e, e_=e_, s_=s_: e.matmul(banks[2][:, 0:64], lhsT=LU[:, e_ * 64: e_ * 64 + 128],
                                                                       rhs=M4c[e_][:, 128 + 64 * s_:128 + 64 * s_ + 64],
                                                                       start=False, stop=False), reads=["lu", f"m4{e_}_{st}"], writes=["bk2"])
                    for e_ in range(2):
                        P.op("tensor", lambda e, e_=e_, s_=s_: e.matmul(banks[2][:, 0:64], lhsT=LV[:, c, e_ * 64: e_ * 64 + 128],
                                                                       rhs=M4c[e_][:, 384 + 64 * s_:384 + 64 * s_ + 64],
                                                                       start=False, stop=(e_ == 1)), reads=["b3", "b4", f"m4{e_}_{st}"], writes=["bk2"])
                    P.op("scalar", lambda e, tsub=tsub: e.copy(out=YOUT[:, tsub], in_=banks[2][:, 0:64]), reads=["bk2"], writes=["b5"])
                    yield

            def drive(gA, gB, ratio=2):
                doneA = doneB = False
                while not (doneA and doneB):
                    if not doneA:
                        try:
                            next(gA)
                        except StopIteration:
                            doneA = True
                    for _ in range(ratio):
                        if not doneB:
                            try:
                                next(gB)
                            except StopIteration:
                                doneB = True

            chunks = list(range(NCH)) if "c" in parts else []
            if chunks:
                for _ in tmat_gen(0, 0):
                    pass
            for c in chunks:
                gA = chain_gen(c, c % 2)
                gB = tmat_gen(c + 1, (c + 1) % 2) if c + 1 < NCH else iter(())
                drive(gA, gB)
            P.cur_level = 12
            CEN, SQb = B(6), B(7)
            for tg in range(NTG):
                ts_ = slice(tg * TT, (tg + 1) * TT)
                b = tg % 2
                P.op("tensor", lambda e, b=b, ts_=ts_: e.matmul(banks[b][:, :], lhsT=blk, rhs=YOUT[:, ts_], start=True, stop=True),
                     reads=["blk", "b5"], writes=[f"bk{b}"])
                P.op("vector", lambda e, b=b, ts_=ts_: e.scalar_tensor_tensor(out=CEN[:, ts_], in0=banks[b][:, :], scalar=-1.0 / 64, in1=YOUT[:, ts_],
                                                                             op0=ALU.mult, op1=ALU.add), reads=[f"bk{b}", "b5"], writes=["b6"])
                P.op("scalar", lambda e, ts_=ts_: e.activation(out=SQb[:, ts_], in_=CEN[:, ts_], func=AF.Square), reads=["b6"], writes=["b7"])
                b2 = 2 + tg % 2
                P.op("tensor", lambda e, b2=b2, ts_=ts_: e.matmul(banks[b2][:, :], lhsT=blk, rhs=SQb[:, ts_], start=True, stop=True),
                     reads=["blk", "b7"], writes=[f"bk{b2}"])
                P.op("scalar", lambda e, b2=b2, ts_=ts_: e.activation(out=SQb[:, ts_], in_=banks[b2][:, :], func=AF.Sqrt, bias=64e-5, scale=1.0 / 64),
                     reads=[f"bk{b2}", "b7"], writes=["b7"])
                P.op("vector", lambda e, ts_=ts_: e.reciprocal(out=SQb[:, ts_], in_=SQb[:, ts_]), reads=["b7"], writes=["b7"])
                P.op("vector", lambda e, ts_=ts_: e.tensor_tensor(out=CEN[:, ts_], in0=CEN[:, ts_], in1=SQb[:, ts_], op=ALU.mult), reads=["b6", "b7"], writes=["b6"])
                P.op("scalar", lambda e, ts_=ts_, hp=hp: e.activation(out=CEN[:, ts_], in_=CEN[:, ts_], func=AF.Identity,
                                                                    bias=spcol(SP_LNB, hp), scale=spcol(SP_LNW, hp)), reads=["b6", "sp"], writes=["b6"])
                P.op("vector", lambda e, ts_=ts_: e.tensor_tensor(out=CEN[:, ts_], in0=CEN[:, ts_], in1=BON[:, ts_], op=ALU.add), reads=["b6", "b13"], writes=["b6"])
                t = cnts["tmpa"] % 2
                cnts["tmpa"] += 1
                stg = VB(o_MISC + 512 + 512 * t, 512)
                P.op("vector", lambda e, ts_=ts_, stg=stg: e.tensor_tensor(out=stg, in0=CEN[:, ts_], in1=G_[:, ts_], op=ALU.mult),
                     reads=["b6", "b12"], writes=[f"tmpa{t}"])
                P.dma("sync", yT[hp, :, ts_], stg, reads=[f"tmpa{t}"], writes=[f"yT{hp}"], semkey=f"tmpa{t}")

    def fox_phase():
        oa = o_HT
        QT = VB(oa, S); KTb = VB(oa + S // 2, S); VTm = VB(oa + S, S).rearrange("p (c d) -> p c d", d=128)
        CUM = V(oa + 3 * S // 2, S, 0, 16)
        CUMB = VB(oa + 5 * S // 2, S, 0, 16)
        FRAW = V(oa + 3 * S, S, 0, 16)
        NEGC = V(oa + 4 * S, NCH * 16).rearrange("p (c h) -> p c h", h=16)
        ob = oa + 4 * S + NCH * 16
        PTr = [VB(ob + 256 * i, 512) for i in range(3)]
        OS = V(ob + 768, 512); RS = V(ob + 1280, 512); SQf = V(ob + 1792, 512); RN = V(ob + 2304, 512)
        YB = [VB(ob + 2816 + 256 * i, 512) for i in range(2)]
        scale = float(128.0 ** -0.5)
        P.dma("sync", FRAW, pT[84, 0:16, :], reads=["pT84"], writes=["fraw"], semkey="fraw")
        P.op("scalar", lambda e: e.activation(out=FRAW, in_=FRAW, func=AF.Sigmoid, bias=sp[0:16, SP_BF:SP_BF + 1], scale=1.0),
             reads=["fraw", "sp"], writes=["fraw"])
        P.op("scalar", lambda e: e.activation(out=FRAW, in_=FRAW, func=AF.Ln), reads=["fraw"], writes=["fraw"])
        for c in range(NCH):
            cs = slice(c * 128, (c + 1) * 128)
            init = 0.0 if c == 0 else CUM[:, c * 128 - 1:c * 128]
            P.op("vector", lambda e, cs=cs, init=init: e.tensor_tensor_scan(out=CUM[:, cs], data0=ones[0:16, :], data1=FRAW[:, cs], initial=init,
                                                                           op0=ALU.mult, op1=ALU.add), reads=["fraw", "ones", "cum"], writes=["cum"])
        P.op("vector", lambda e: e.tensor_copy(out=CUMB, in_=CUM), reads=["cum"], writes=["cumb"])
        for c in range(NCH):
            b = c % 4
            P.op("tensor", lambda e, b=b, c=c: e.transpose(banks[b][:, 0:16], CUM[:, c * 128:(c + 1) * 128], ident[0:16, 0:16]),
                 reads=["cum", "ident"], writes=[f"bk{b}"])
            P.op("scalar", lambda e, b=b, c=c: e.mul(out=NEGC[:, c, :], in_=banks[b][:, 0:16], mul=-1.0), reads=[f"bk{b}"], writes=["negc"])
        NQG = S // 512
        for h in range(NFH):
            P.dma("gpsimd", QT, pT[52 + h, :, :], reads=[f"pT{52 + h}"], writes=["qt"], semkey="qt")
            P.dma("gpsimd", KTb, pT[68 + h, :, :], reads=[f"pT{68 + h}"], writes=["ktb"], semkey="ktb")
            P.dma("sync", VTm, fvS[:, :, h * 128:(h + 1) * 128].rearrange("c p d -> p c d"), reads=["fvS"], writes=["vtm"], semkey="vtm")
            for qg in range(NQG):
                qs = slice(qg * 512, (qg + 1) * 512)
                nkb = 4 * (qg + 1)
                for j in range(nkb):
                    c0 = (j - 4 * qg) * 128 if j >= 4 * qg else 0
                    sb_ = j % 2
                    pr = cnts["sq"] % 3
                    cnts["sq"] += 1
                    P.op("tensor", lambda e, sb_=sb_, j=j, qs=qs, c0=c0: e.matmul(banks[sb_][:, c0:512], lhsT=KTb[:, j * 128:(j + 1) * 128],
                                                                                 rhs=QT[:, qs.start + c0:qs.stop], start=True, stop=False),
                         reads=["ktb", "qt"], writes=[f"bk{sb_}"])
                    diag = j >= 4 * qg
                    P.op("tensor", lambda e, sb_=sb_, h=h, qs=qs, c0=c0, diag=diag: e.matmul(banks[sb_][:, c0:512], lhsT=selb[0:16, h * 128:(h + 1) * 128],
                                                                                 rhs=CUMB[:, qs.start + c0:qs.stop], start=False, stop=(not diag)),
                         reads=["selb", "cumb"], writes=[f"bk{sb_}"])
                    if diag:
                        P.op("tensor", lambda e, sb_=sb_, c0=c0: e.matmul(banks[sb_][:, c0:c0 + 128], lhsT=identb, rhs=negmb, start=False, stop=True),
                             reads=["identb", "negmb"], writes=[f"bk{sb_}"])
                    P.op("scalar", lambda e, sb_=sb_, pr=pr, j=j, h=h, c0=c0: e.activation(out=PTr[pr][:, c0:512], in_=banks[sb_][:, c0:512], func=AF.Exp,
                                                                                          bias=NEGC[:, j, h:h + 1], scale=scale),
                         reads=[f"bk{sb_}", "negc"], writes=[f"ptr{pr}"])
                    P.op("tensor", lambda e, pr=pr, j=j, c0=c0, nkb=nkb: e.matmul(banks[2][:, c0:512], lhsT=VTm[:, j, :], rhs=PTr[pr][:, c0:512],
                                                                                 start=(j == 0), stop=(j == nkb - 1)),
                         reads=["vtm", f"ptr{pr}"], writes=["bk2"])
                    P.op("tensor", lambda e, pr=pr, j=j, c0=c0, nkb=nkb: e.matmul(banks[3][:, c0:512], lhsT=onesb, rhs=PTr[pr][:, c0:512],
                                                                                 start=(j == 0), stop=(j == nkb - 1)),
                         reads=["onesb", f"ptr{pr}"], writes=["bk3"])
                P.op("vector", lambda e: e.reciprocal(out=RS, in_=banks[3][:, :]), reads=["bk3"], writes=["rs"])
                P.op("vector", lambda e: e.tensor_tensor(out=OS, in0=banks[2][:, :], in1=RS, op=ALU.mult), reads=["bk2", "rs"], writes=["os"])
                P.op("scalar", lambda e: e.activation(out=SQf, in_=OS, func=AF.Square), reads=["os"], writes=["sqf"])
                P.op("tensor", lambda e: e.matmul(banks[4][:, :], lhsT=ones, rhs=SQf, start=True, stop=True), reads=["ones", "sqf"], writes=["bk4"])
                P.op("scalar", lambda e: e.activation(out=RN, in_=banks[4][:, :], func=AF.Sqrt, bias=1e-6, scale=1.0 / 128), reads=["bk4"], writes=["rn"])
                P.op("vector", lambda e: e.reciprocal(out=RN, in_=RN), reads=["rn"], writes=["rn"])
                yb = cnts["tmpa"] % 2
                cnts["tmpa"] += 1
                P.op("vector", lambda e, yb=yb, h=h: e.scalar_tensor_tensor(out=YB[yb], in0=OS, scalar=sp[:, SP_FON + h:SP_FON + h + 1], in1=RN,
                                                                           op0=ALU.mult, op1=ALU.mult), reads=["os", "rn", "sp"], writes=[f"yb{yb}"])
                P.dma("sync", yT[16 + h, :, qs], YB[yb], reads=[f"yb{yb}"], writes=[f"yT{16 + h}"], semkey=f"yb{yb}")

    def outproj(tg):
        for kt in range(NFT):
            P.dma("sync", HT[:, kt * TT:(kt + 1) * TT], yT[kt, :, tg * TT:(tg + 1) * TT], reads=[f"yT{kt}"], writes=[f"ht{kt}"], semkey=f"htl{kt}")
        linear(wo_in, 8, 32, lambda g: 4, "a", lambda kt: HT[:, kt * TT:(kt + 1) * TT], lambda kt: f"ht{kt}", resid_evac(tg, 32))

    for tg in range(NTG):
        ffn(tg, 0, 0)
    P.barrier()
    if stop_after == "ffn1":
        return final_phase()
    for tg in range(NTG):
        inproj(tg)
    P.barrier()
    parts = "rcfo"
    if stop_after == "nomix":
        parts = ""
    elif stop_after is not None and stop_after.startswith("mix:"):
        parts = stop_after[4:]
    if "r" in parts:
        import re as _re
        m_ = _re.search(r"L(\d+)", parts)
        if m_:
            P.max_level = int(m_.group(1))
        rwkv_phase()
        P.cur_level = 0
        P.barrier()
    if "f" in parts:
        fox_phase()
        P.barrier()
    def select_half():
        H = S // 2
        lo = [f"_{t}" for t in range(NTD)]
        hi = [f"_{t}" for t in range(NTD, NTG)]
        par = sp[:, SP_PAR:SP_PAR + 1]
        omp = sp[:, SP_OMP:SP_OMP + 1]
        for ft in range(NFT):
            q = ft % 2
            a = V(o_ACT + 2048 * q, H)
            bb = V(o_ACT + 2048 * q + 1024, H)
            ka, kb = f"sela{q}", f"selb{q}"
            P.dma("sync", a, xT[ft, :, 0:H], reads=[f"xT{ft}" + t for t in lo], writes=[ka], semkey=ka)
            P.dma("sync", bb, xT[ft, :, H:S], reads=[f"xT{ft}" + t for t in hi], writes=[kb], semkey=kb)
            P.op("vector", lambda e, bb=bb: e.tensor_scalar_mul(out=bb, in0=bb, scalar1=par), reads=[kb, "sp"], writes=[kb])
            P.op("vector", lambda e, a=a, bb=bb: e.scalar_tensor_tensor(out=a, in0=a, scalar=omp, in1=bb, op0=ALU.mult, op1=ALU.add),
                 reads=[ka, kb, "sp"], writes=[ka])
            P.dma("sync", xT[ft, :, 0:H], a, reads=[ka], writes=[f"xT{ft}" + t for t in lo], semkey=f"selo{q}")
        for kt in range(NFT):
            q = kt % 2
            a = VB(o_ACT + 4096 + 1024 * q, H)
            bb = VB(o_ACT + 4096 + 1024 * q + 512, H)
            ka, kb = f"selc{q}", f"seld{q}"
            P.dma("sync", a, yT[kt, :, 0:H], reads=[f"yT{kt}"], writes=[ka], semkey=ka)
            P.dma("sync", bb, yT[kt, :, H:S], reads=[f"yT{kt}"], writes=[kb], semkey=kb)
            P.op("vector", lambda e, bb=bb: e.tensor_scalar_mul(out=bb, in0=bb, scalar1=par), reads=[kb, "sp"], writes=[kb])
            P.op("vector", lambda e, a=a, bb=bb: e.scalar_tensor_tensor(out=a, in0=a, scalar=omp, in1=bb, op0=ALU.mult, op1=ALU.add),
                 reads=[ka, kb, "sp"], writes=[ka])
            P.dma("sync", yT[kt, :, 0:H], a, reads=[ka], writes=[f"yT{kt}"], semkey=f"selp{q}")
        P.barrier()

    if "o" in parts:
        if split:
            select_half()
        for tg in range(NTD):
            outproj(tg)
        P.barrier()
    for tg in range(NTD):
        ffn(tg, 1, 2)
    P.barrier()
    return final_phase()


def _tile_w(W, C=512):
    K, M = W.shape
    return np.ascontiguousarray(W.reshape(K // 128, 128, M // C, C).transpose(2, 1, 0, 3))


def _col(v):
    v = np.asarray(v, np.float32).reshape(-1, 128)
    return np.ascontiguousarray(v.T)


def prepare(inputs, S, DFF):
    f32 = np.float32
    g = lambda k: np.asarray(inputs[k], f32)
    B = g("x").shape[0]
    w_in = g("w_in")[0]
    shared = {}
    shared["wmod"] = _tile_w(g("w_mod")[0])
    for i, nm in ((1, "ffn1"), (2, "ffn2")):
        ga, up, dn = g(nm + "_gate")[0], g(nm + "_up")[0], g(nm + "_down")[0]
        K = ga.shape[0]
        gu = np.concatenate([ga.reshape(K, DFF // 256, 256), up.reshape(K, DFF // 256, 256)], axis=2).reshape(K, 2 * DFF)
        shared[f"wgu{i}"] = _tile_w(gu)
        shared[f"wdn{i}"] = _tile_w(dn)
    wa = np.zeros((D, NTA * 128), f32)
    wa[:, 0:6144] = w_in[:, 0:6144]
    wa[:, 48 * 128:48 * 128 + 96] = w_in[:, 6144:6240]
    wa[:, 49 * 128:49 * 128 + 96] = w_in[:, 6240:6336]
    wa[:, 50 * 128:52 * 128] = w_in[:, 6336:6592]
    wa[:, 52 * 128:84 * 128] = w_in[:, 6592:10688]
    wa[:, 84 * 128:84 * 128 + 16] = w_in[:, 12736:12752]
    shared["wina"] = _tile_w(wa)
    shared["winb"] = _tile_w(w_in[:, 10688:12736])
    shared["wo"] = _tile_w(g("w_out")[0])
    shared["w2"] = np.ascontiguousarray(g("rwkv_w2")[0])
    shared["a2"] = np.ascontiguousarray(g("rwkv_a2")[0])
    shared["g2"] = np.ascontiguousarray(g("rwkv_g2")[0].reshape(2, 128, 2048))
    mu = g("rwkv_mu")[0]
    mut = np.zeros((128, 52), f32)
    mut[:, 0:48] = _col(mu[0:6144])
    mut[0:96, 48] = mu[6144:6240]
    mut[0:96, 49] = mu[6240:6336]
    mut[:, 50:52] = _col(mu[6336:6592])
    per = []
    for b in range(B):
        sp = np.zeros((128, NSP), f32)
        sp[:, SP_C:SP_C + 32] = _col(g("c")[b])
        sp[:, SP_BMOD:SP_BMOD + 288] = _col(g("b_mod")[0])
        sp[:, SP_GAIN:SP_GAIN + 32] = _col(g("norm_ffn1")[0])
        sp[:, SP_GAIN + 32:SP_GAIN + 64] = _col(g("norm_mix")[0])
        sp[:, SP_GAIN + 64:SP_GAIN + 96] = _col(g("norm_ffn2")[0])
        sp[:, SP_GAIN + 96:SP_GAIN + 128] = _col(g("norm_final"))
        sp[:, SP_MU:SP_MU + 52] = mut
        sp[:, SP_W0:SP_W0 + 16] = _col(g("rwkv_w0")[0])
        sp[:, SP_A0:SP_A0 + 16] = _col(g("rwkv_a0")[0])
        sp[:, SP_KK:SP_KK + 16] = _col(g("rwkv_k_k")[0])
        sp[:, SP_KA:SP_KA + 16] = _col(g("rwkv_k_a")[0])
        sp[:, SP_LNW:SP_LNW + 16] = _col(g("rwkv_ln_w")[0])
        sp[:, SP_LNB:SP_LNB + 16] = _col(g("rwkv_ln_b")[0])
        sp[:, SP_RK:SP_RK + 16] = _col(g("rwkv_r_k")[0].reshape(-1))
        sp[:, SP_FON:SP_FON + 16] = _col(g("fox_out_norm")[0])
        sp[0:16, SP_BF] = g("fox_b_f")[0]
        d = dict(shared)
        d["x"] = np.ascontiguousarray(g("x")[b])
        d["sp"] = sp
        per.append(d)
    return per


_NC_CACHE = {}
VERBOSE = False
DEBUG_OUT = False
LAST_RES = [None]


def run(inputs, S, DFF, stop_after=None, n_cores=8):
    import time
    t0 = time.time()
    per = prepare(inputs, S, DFF)
    t1 = time.time()
    B = len(per)
    pair = n_cores // B
    split = (pair == 2 and (S // TT) % 2 == 0 and stop_after is None)
    key = (S, DFF, stop_after, split)
    if key not in _NC_CACHE:
        _NC_CACHE[key] = build_nc(S, DFF, stop_after, split)
    nc = _NC_CACHE[key]
    in_maps = []
    for i in range(n_cores):
        d = dict(per[(i * B) // n_cores])
        par = float(i % pair) if split else 0.0
        spc = d["sp"].copy()
        spc[:, SP_PAR] = par
        spc[:, SP_OMP] = 1.0 - par
        d["sp"] = spc
        in_maps.append(d)
    t2 = time.time()
    res = run_bass_kernel_spmd(nc, in_maps, core_ids=list(range(n_cores)))
    if VERBOSE:
        print("TIMES prepare %.1f build %.1f run %.1f" % (t1 - t0, t2 - t1, time.time() - t2), flush=True)
    step = n_cores // B
    LAST_RES[0] = res
    if split:
        return np.stack([np.concatenate([np.asarray(res.results[2 * b]["out"], np.float32),
                                         np.asarray(res.results[2 * b + 1]["out"], np.float32)], axis=0) for b in range(B)], axis=0)
    return np.stack([np.asarray(res.results[b * step]["out"], np.float32) for b in range(B)], axis=0)


def kernel(**inputs):
    S = int(np.asarray(inputs["x"]).shape[1])
    DFF = int(np.asarray(inputs["ffn1_gate"]).shape[2])
    return run(inputs, S, DFF)
```
